# Optimizing a Trainium2 kernel written in Bass

```python
import math
import jax
import jax.numpy as jnp
from jax import lax

D_MODEL = 1024
BATCH = 2
SEQ = 8192
DEPTH = 1

HEAD_DIM = 64
D_MIX = D_MODEL
RWKV_WIDTH = D_MIX // 2
ATTN_WIDTH = D_MIX - RWKV_WIDTH
RWKV_HEADS = RWKV_WIDTH // HEAD_DIM
ATTN_HEADS = ATTN_WIDTH // HEAD_DIM
W_LORA = max(32, int(round(1.8 * RWKV_WIDTH ** 0.5 / 32)) * 32)
A_LORA = max(32, int(round(1.8 * RWKV_WIDTH ** 0.5 / 32)) * 32)
G_LORA = max(32, int(round(0.6 * RWKV_WIDTH ** 0.8 / 32)) * 32)
RWKV_COLS = 3 * RWKV_WIDTH + W_LORA + A_LORA + G_LORA
ATTN_COLS = 3 * ATTN_WIDTH
PROJ_COLS = RWKV_COLS + ATTN_COLS
D_FF = -(-8 * D_MODEL // (3 * 128)) * 128
CONV_WIDTH = 3
ROPE_THETA = 500000.0
ROT_DIM = HEAD_DIM // 4
DILATED_PATTERNS = ((128, 1), (512, 4), (2048, 16))
ATTN_BLOCK = 128
NORM_EPS = 1e-6
GN_EPS = 64e-5

kernel_name = 'hymba_rwkv7_dilated_convffn'


def _rmsnorm(x, gain):
    xf = x.astype(jnp.float32)
    y = xf * lax.rsqrt(jnp.mean(xf * xf, axis=-1, keepdims=True) + NORM_EPS)
    return (y * gain.astype(jnp.float32)).astype(x.dtype)


def _token_shift(t):
    return jnp.pad(t, ((0, 0), (1, 0), (0, 0)))[:, :-1]


def _partial_rotary(x, positions):
    half = ROT_DIM // 2
    inv_freq = ROPE_THETA ** (-jnp.arange(half, dtype=jnp.float32) * 2.0 / ROT_DIM)
    ang = positions.astype(jnp.float32)[:, None] * inv_freq[None, :]
    cos = jnp.cos(ang)[None, :, None, :]
    sin = jnp.sin(ang)[None, :, None, :]
    xf = x[..., :ROT_DIM].astype(jnp.float32)
    x1, x2 = xf[..., :half], xf[..., half:]
    rot = jnp.concatenate([x1 * cos - x2 * sin, x2 * cos + x1 * sin], axis=-1)
    return jnp.concatenate([rot.astype(x.dtype), x[..., ROT_DIM:]], axis=-1)


def _banded_causal_attention(q, k, v, span):
    B, H, G, L, hd = q.shape
    n_blk = -(-L // ATTN_BLOCK)
    pad = n_blk * ATTN_BLOCK - L
    cfg = ((0, 0), (0, 0), (0, 0), (0, pad), (0, 0))
    q, k, v = (jnp.pad(t, cfg).reshape(B, H, G, n_blk, ATTN_BLOCK, hd) for t in (q, k, v))

    def with_prev(t):
        prev = jnp.pad(t, ((0, 0), (0, 0), (0, 0), (1, 0), (0, 0), (0, 0)))[:, :, :, :-1]
        return jnp.concatenate([prev, t], axis=4)

    kb, vb = with_prev(k), with_prev(v)
    s = jnp.einsum('bhgnqd,bhgnkd->bhgnqk', q, kb, preferred_element_type=jnp.float32) / math.sqrt(hd)
    blk = jnp.arange(n_blk)[:, None, None] * ATTN_BLOCK
    qpos = blk + jnp.arange(ATTN_BLOCK)[None, :, None]
    kpos = blk - ATTN_BLOCK + jnp.arange(2 * ATTN_BLOCK)[None, None, :]
    dist = qpos - kpos
    valid = (dist >= 0) & (dist <= span) & (kpos >= 0)
    s = jnp.where(valid, s, -jnp.inf)
    m = jnp.max(s, axis=-1, keepdims=True)
    p = jnp.exp(s - m)
    den = jnp.sum(p, axis=-1, keepdims=True)
    o = jnp.einsum('bhgnqk,bhgnkd->bhgnqd', p, vb.astype(jnp.float32)) / den
    lse = (m + jnp.log(den))[..., 0]
    o = o.reshape(B, H, G, n_blk * ATTN_BLOCK, hd)[:, :, :, :L]
    lse = lse.reshape(B, H, G, n_blk * ATTN_BLOCK)[..., :L]
    return o, lse


def _dilated_attention(q, k, v):
    B, S, H, hd = q.shape
    q, k, v = (t.transpose(0, 2, 1, 3) for t in (q, k, v))
    outs, lses = [], []
    for window, dil in DILATED_PATTERNS:
        L = S // dil

        def by_stride(t):
            return t.reshape(B, H, L, dil, hd).transpose(0, 1, 3, 2, 4)

        o, lse = _banded_causal_attention(by_stride(q), by_stride(k), by_stride(v), window // dil)
        outs.append(o.transpose(0, 1, 3, 2, 4).reshape(B, H, S, hd))
        lses.append(lse.transpose(0, 1, 3, 2).reshape(B, H, S))
    wts = jax.nn.softmax(jnp.stack(lses), axis=0)
    o = jnp.sum(wts[..., None] * jnp.stack(outs), axis=0)
    return o.transpose(0, 2, 1, 3)


def _rwkv7_scan(r, decay, k, v, kk, a):
    def step(state, inp):
        r_t, w_t, k_t, v_t, kk_t, a_t = inp
        sa = jnp.einsum('bhvk,bhk->bhv', state, -kk_t)
        state = (state * w_t[:, :, None, :]
                 + sa[..., None] * (kk_t * a_t)[:, :, None, :]
                 + v_t[..., None] * k_t[:, :, None, :])
        y = jnp.einsum('bhvk,bhk->bhv', state, r_t)
        return state, y

    B, S, H, N = r.shape
    xs = tuple(jnp.swapaxes(t, 0, 1) for t in (r, decay, k, v, kk, a))
    init = jnp.zeros((B, H, N, N), jnp.float32)
    _, ys = lax.scan(step, init, xs)
    return jnp.swapaxes(ys, 0, 1)


def _rwkv7_mixer(p, shift_mix, w0, w_lora_up, a0, a_lora_up, g_lora_up, k_k, k_a, r_k, ln_x_w, ln_x_b):
    B, S, _ = p.shape
    f32 = jnp.float32
    p = p + (_token_shift(p) - p) * shift_mix
    idx = [RWKV_WIDTH, 2 * RWKV_WIDTH, 3 * RWKV_WIDTH, 3 * RWKV_WIDTH + W_LORA, 3 * RWKV_WIDTH + W_LORA + A_LORA]
    r, k, v, wd, ad, gd = jnp.split(p, idx, axis=-1)
    w_log = -jax.nn.softplus(-(w0 + jnp.tanh(wd) @ w_lora_up)) - 0.5
    decay = jnp.exp(-jnp.exp(w_log.astype(f32)))
    a = jax.nn.sigmoid(a0 + ad @ a_lora_up)
    g = jax.nn.sigmoid(gd) @ g_lora_up

    def heads(t):
        return t.reshape(B, S, RWKV_HEADS, HEAD_DIM).astype(f32)

    kk = heads(k * k_k)
    kk = kk / jnp.maximum(jnp.sqrt(jnp.sum(kk * kk, axis=-1, keepdims=True)), 1e-12)
    k = k * (1.0 + (a - 1.0) * k_a)
    rh, kh, vh, ah, wh = heads(r), heads(k), heads(v), heads(a), heads(decay)
    y = _rwkv7_scan(rh, wh, kh, vh, kk, ah)
    mu = jnp.mean(y, axis=-1, keepdims=True)
    var = jnp.mean(jnp.square(y - mu), axis=-1, keepdims=True)
    y = ((y - mu) * lax.rsqrt(var + GN_EPS)).reshape(B, S, RWKV_WIDTH) * ln_x_w + ln_x_b
    bonus = jnp.sum(rh * kh * r_k.astype(f32), axis=-1, keepdims=True) * vh
    y = y + bonus.reshape(B, S, RWKV_WIDTH)
    return (y * g).astype(p.dtype)


def _causal_dwconv(t, w, b):
    out = lax.conv_general_dilated(t, w[:, None, :], window_strides=(1,), padding=[(CONV_WIDTH - 1, 0)],
                                   dimension_numbers=('NWC', 'WIO', 'NWC'), feature_group_count=t.shape[-1])
    return out + b


def setup_inputs(seed: int = 0) -> dict:
    key = jax.random.key(seed)
    ks = jax.random.split(key, 24)
    f32 = jnp.float32
    L = DEPTH

    def normal(k, shape, scale):
        return jax.random.normal(k, shape, f32) * scale

    return {
        'x': normal(ks[0], (BATCH, SEQ, D_MODEL), 1.0),
        'mix_norm_gain': 1.0 + normal(ks[1], (L, D_MODEL), 0.02),
        'w_in': normal(ks[2], (L, D_MODEL, PROJ_COLS), D_MODEL ** -0.5),
        'rwkv_shift_mix': jax.random.uniform(ks[3], (L, RWKV_COLS), f32),
        'w0': jax.random.uniform(ks[4], (L, RWKV_WIDTH), f32, -4.0, 0.0),
        'w_lora_up': normal(ks[5], (L, W_LORA, RWKV_WIDTH), 0.1),
        'a0': normal(ks[6], (L, RWKV_WIDTH), 0.1),
        'a_lora_up': normal(ks[7], (L, A_LORA, RWKV_WIDTH), 0.1),
        'g_lora_up': normal(ks[8], (L, G_LORA, RWKV_WIDTH), G_LORA ** -0.5),
        'k_k': 0.85 + normal(ks[9], (L, RWKV_WIDTH), 0.02),
        'k_a': 1.0 + normal(ks[10], (L, RWKV_WIDTH), 0.02),
        'r_k': normal(ks[11], (L, RWKV_HEADS, HEAD_DIM), 0.1),
        'ln_x_w': 1.0 + normal(ks[12], (L, RWKV_WIDTH), 0.02),
        'ln_x_b': normal(ks[13], (L, RWKV_WIDTH), 0.02),
        'attn_norm_gain': 1.0 + normal(ks[14], (L, ATTN_WIDTH), 0.02),
        'w_out': normal(ks[15], (L, D_MIX, D_MODEL), D_MIX ** -0.5),
        'ffn_norm_gain': 1.0 + normal(ks[16], (L, D_MODEL), 0.02),
        'w_ffn_up': normal(ks[17], (L, D_MODEL, 2 * D_FF), D_MODEL ** -0.5),
        'ffn_conv_w': normal(ks[18], (L, CONV_WIDTH, D_FF), CONV_WIDTH ** -0.5),
        'ffn_conv_b': normal(ks[19], (L, D_FF), 0.02),
        'w_ffn_down': normal(ks[20], (L, D_FF, D_MODEL), D_FF ** -0.5),
        'final_norm_gain': 1.0 + normal(ks[21], (D_MODEL,), 0.02),
    }


def reference(x, mix_norm_gain, w_in, rwkv_shift_mix, w0, w_lora_up, a0, a_lora_up, g_lora_up,
              k_k, k_a, r_k, ln_x_w, ln_x_b, attn_norm_gain, w_out, ffn_norm_gain, w_ffn_up,
              ffn_conv_w, ffn_conv_b, w_ffn_down, final_norm_gain):
    B, S, _ = x.shape
    positions = jnp.arange(S)
    for l in range(DEPTH):
        h = _rmsnorm(x, mix_norm_gain[l])
        proj = h @ w_in[l]
        p_rwkv, p_attn = proj[..., :RWKV_COLS], proj[..., RWKV_COLS:]
        y_rwkv = _rwkv7_mixer(p_rwkv, rwkv_shift_mix[l], w0[l], w_lora_up[l], a0[l], a_lora_up[l],
                              g_lora_up[l], k_k[l], k_a[l], r_k[l], ln_x_w[l], ln_x_b[l])
        q, k, v = (t.reshape(B, S, ATTN_HEADS, HEAD_DIM) for t in jnp.split(p_attn, 3, axis=-1))
        q = _partial_rotary(q, positions)
        k = _partial_rotary(k, positions)
        o = _dilated_attention(q, k, v)
        o = _rmsnorm(o, attn_norm_gain[l].reshape(ATTN_HEADS, HEAD_DIM)).astype(x.dtype)
        y_attn = o.reshape(B, S, ATTN_WIDTH)
        x = x + jnp.concatenate([y_rwkv, y_attn], axis=-1) @ w_out[l]
        h = _rmsnorm(x, ffn_norm_gain[l])
        gate, val = jnp.split(h @ w_ffn_up[l], 2, axis=-1)
        gate = _causal_dwconv(gate, ffn_conv_w[l], ffn_conv_b[l])
        x = x + (jax.nn.silu(gate) * val) @ w_ffn_down[l]
    return _rmsnorm(x, final_norm_gain)
```

```python
from contextlib import ExitStack
import concourse.bass as bass
import concourse.mybir as mybir

F32 = mybir.dt.float32
BF16 = mybir.dt.bfloat16
AF = mybir.ActivationFunctionType
ALU = mybir.AluOpType

ENGINES = ["pe", "act", "dve", "pool", "sp"]
SEM_LIMIT = 30000
DMA_SLOTS = {"sp": 12, "pool": 6, "act": 4}


class Buf:
    __slots__ = ("name", "w", "rd")

    def __init__(self, name=""):
        self.name = name
        self.w = None
        self.rd = []


class Op:
    __slots__ = ("idx", "eng", "pos", "fn", "deps", "kind", "sem", "consumed", "prev")

    def __init__(self):
        self.consumed = False
        self.sem = None
        self.prev = None


class Sched:
    def __init__(self, nc, stack):
        self.nc = nc
        self.ops = []
        self.eng_ops = {e: [] for e in ENGINES}
        self.emitted = {e: 0 for e in ENGINES}
        self.eng_sems = {}
        self.eng_cnt = {}
        self.dma_sems = {}
        self.dma_cnt = {}
        for e in ["pe", "act", "dve", "pool"]:
            self.eng_sems[e] = [stack.enter_context(nc.semaphore(f"s_{e}{i}")) for i in range(4)]
            self.eng_cnt[e] = [0, 0]
        for e, n in DMA_SLOTS.items():
            self.dma_sems[e] = [stack.enter_context(nc.semaphore(f"d_{e}{i}")) for i in range(n)]
            self.dma_cnt[e] = 0
        self.cc_sem = stack.enter_context(nc.semaphore("cc_sem"))
        self.cc_cnt = 0
        self.pending_barrier = {}
        self.waited = {e: {} for e in ENGINES}
        self.dma_since_barrier = []
        self.cc_pending = []
        self.boundary = 0

    def op(self, eng, fn, reads=(), writes=(), kind="c"):
        o = Op()
        o.idx = len(self.ops)
        o.eng = eng
        o.fn = fn
        o.kind = kind
        o.pos = len(self.eng_ops[eng])
        deps = set()
        for b in reads:
            if b.w is not None:
                deps.add(b.w)
        for b in writes:
            if b.w is not None:
                deps.add(b.w)
            deps.update(b.rd)
        deps = set(d for d in deps if d >= self.boundary)
        for b in reads:
            b.rd.append(o.idx)
        for b in writes:
            b.w = o.idx
            b.rd = []
        if eng in self.pending_barrier:
            deps.update(self.pending_barrier.pop(eng))
        keep = []
        for d in deps:
            p = self.ops[d]
            if p.eng == eng and p.kind == "c" and kind == "c":
                if eng == "pe":
                    continue
                if o.pos - p.pos > 3:
                    continue
            keep.append(d)
        o.deps = keep
        self.ops.append(o)
        self.eng_ops[eng].append(o)
        if kind == "d":
            self.dma_since_barrier.append(o.idx)
        elif kind == "cc":
            self.cc_pending.append(o.idx)
        return o

    def barrier(self, include_cc=True):
        last = set()
        if include_cc:
            last.update(self.cc_pending)
            self.cc_pending = []
        for e in ENGINES:
            if self.eng_ops[e]:
                last.add(self.eng_ops[e][-1].idx)
        last.update(self.dma_since_barrier)
        self.dma_since_barrier = []
        for e in ENGINES:
            self.pending_barrier.setdefault(e, set()).update(last)

    def flush(self, include_cc=True):
        nc = self.nc
        self.barrier(include_cc)
        new_ops = {e: self.eng_ops[e][self.emitted[e]:] for e in ENGINES}
        for e in ENGINES:
            for o in new_ops[e]:
                for d in o.deps:
                    self.ops[d].consumed = True
        for e in ENGINES:
            for o in self.eng_ops[e][-1:]:
                o.consumed = True
        for e in ENGINES:
            for o in new_ops[e]:
                if o.kind == "d":
                    i = self.dma_cnt[e]
                    self.dma_cnt[e] += 1
                    K = len(self.dma_sems[e])
                    s = self.dma_sems[e][i % K]
                    v = 16 * (i // K + 1)
                    o.sem = (s, v)
                    o.prev = (s, v - 16) if v > 16 else None
                    o.consumed = True
                elif o.kind == "cc":
                    self.cc_cnt += 1
                    o.sem = (self.cc_sem, self.cc_cnt)
                    o.consumed = True
                elif o.consumed:
                    st = self.eng_cnt[e]
                    if st[1] >= SEM_LIMIT:
                        st[0] += 1
                        st[1] = 0
                    st[1] += 1
                    o.sem = (self.eng_sems[e][st[0]], st[1])

        def emit_engine(ename, eng):
            waited = self.waited[ename]
            for o in new_ops[ename]:
                need = {}
                for d in o.deps:
                    s, v = self.ops[d].sem
                    if need.get(s, 0) < v:
                        need[s] = v
                if o.prev is not None:
                    s, v = o.prev
                    if need.get(s, 0) < v:
                        need[s] = v
                for s, v in need.items():
                    if waited.get(s, 0) < v:
                        eng.wait_ge(s, v)
                        waited[s] = v
                ins = o.fn(eng)
                if o.sem is not None and o.consumed:
                    if o.kind == "d":
                        ins.then_inc(o.sem[0], 16)
                    else:
                        ins.then_inc(o.sem[0], 1)

        with nc.Block() as block:
            @block.tensor
            def _(eng):
                emit_engine("pe", eng)

            @block.scalar
            def _(eng):
                emit_engine("act", eng)

            @block.vector
            def _(eng):
                emit_engine("dve", eng)

            @block.gpsimd
            def _(eng):
                emit_engine("pool", eng)

            @block.sync
            def _(eng):
                emit_engine("sp", eng)

        for e in ENGINES:
            self.emitted[e] = len(self.eng_ops[e])
        self.boundary = len(self.ops)

    def final_wait(self):
        nc = self.nc
        need = {}
        for d in self.pending_barrier.get("sp", set()):
            s, v = self.ops[d].sem
            if need.get(s, 0) < v:
                need[s] = v

        with nc.Block() as block:
            @block.sync
            def _(eng):
                for s, v in need.items():
                    eng.wait_ge(s, v)


import numpy as np
import ml_dtypes
from contextlib import ExitStack
import concourse.bass as bass
import concourse.mybir as mybir
from concourse.bass_utils import run_bass_kernel_spmd

I32 = mybir.dt.int32
AX = mybir.AxisListType
S = 8192
D = 1024
NB = 16
MW = [128, 128, 128, 64, 96, 128, 128, 128]
MOFF = [0, 128, 256, 384, 448, 544, 672, 800]
NCOL = 928
NPRM = 32
PI = float(np.pi)
C0 = float(np.exp(-0.5))
GN_EPS = 64e-5
P_GAIN, P_MIX, P_W0, P_A0, P_KK, P_KA, P_RK, P_LNW, P_LNB, P_AG = 0, 8, 13, 14, 15, 16, 17, 18, 19, 20
P_F2PI, P_EPS, P_TINY, P_GNEPS, P_OMKA = 22, 24, 25, 26, 27


class TB:
    def __init__(self, t):
        self.t = t
        self.b = Buf()


def v3(ap):
    return ap.rearrange("p (c t) -> p c t", c=4)


def build(phases=(1, 2, 6), nblocks=NB, debug=False, stage=9, do_rwkv=True):
    nc = bass.Bass("TRN2", target_bir_lowering=False)
    P1, P2, P6 = (1 in phases), (2 in phases), (6 in phases)
    xb = nc.dram_tensor("xb", [S, D], F32, kind="ExternalInput").ap()
    w_in = nc.dram_tensor("w_in", [D, NCOL], F32, kind="ExternalInput").ap()
    prm_d = nc.dram_tensor("prm", [128, NPRM], F32, kind="ExternalInput").ap()
    cbf_d = nc.dram_tensor("cbf", [128, 3, 128], BF16, kind="ExternalInput").ap()
    cf_d = nc.dram_tensor("cf", [128, 1408], F32, kind="ExternalInput").ap()
    lora_d = nc.dram_tensor("lora", [128, 3, 128], F32, kind="ExternalInput").ap()
    amask_d = nc.dram_tensor("amask", [128, 512], BF16, kind="ExternalInput").ap()
    qkv_kind = "Internal" if P1 else "ExternalInput"
    if debug and P1:
        qkv_kind = "ExternalOutput"
    dbg_q = nc.dram_tensor("dbg_q", [3, 128, S], BF16, kind=qkv_kind).ap()
    GW = S + 2
    g_in_kind = "Internal" if not debug else "ExternalOutput"
    QW = 2050
    gin_rq = [nc.dram_tensor(f"gin_r{q}", [128, QW], BF16, kind=g_in_kind if P1 else "Internal").ap() for q in range(4)]
    gin_aq = [nc.dram_tensor(f"gin_a{q}", [128, QW], BF16, kind=g_in_kind if P2 else "Internal").ap() for q in range(4)]
    gout_kind = "Internal" if (P1 and P2) else "ExternalInput"
    gout_rq = [nc.dram_tensor(f"gout_r{q}", [512, QW], BF16, kind=gout_kind).ap() for q in range(4)]
    gout_aq = [nc.dram_tensor(f"gout_a{q}", [512, QW], BF16, kind=gout_kind).ap() for q in range(4)]
    O_d = nc.dram_tensor("O_d", [3, S, 130], F32, kind="Internal").ap()
    if P6:
        xq_d = nc.dram_tensor("xq", [2050, D], F32, kind="ExternalInput").ap()
        w_out_d = nc.dram_tensor("w_out", [D, D], F32, kind="ExternalInput").ap()
        w_up_d = nc.dram_tensor("w_up", [D, 5632], F32, kind="ExternalInput").ap()
        w_dn_d = nc.dram_tensor("w_dn", [2816, D], F32, kind="ExternalInput").ap()
        p6_d = nc.dram_tensor("p6", [128, 128], F32, kind="ExternalInput").ap()
        fg_d = nc.dram_tensor("fgain", [128, D], F32, kind="ExternalInput").ap()
        out_d = nc.dram_tensor("out", [2048, D], F32, kind="ExternalOutput").ap()
        x1_d = nc.dram_tensor("x1_d", [2048, D], F32, kind="Internal").ap()
    if False:
        dbg_p = nc.dram_tensor("dbg_p", [8, 128, S], F32, kind="ExternalOutput").ap()

    st = ExitStack()
    with st:
        K = Sched(nc, st)

        cur = [st]

        def sb(name, shape, dt):
            return TB(cur[0].enter_context(nc.sbuf_tensor("s_" + name, shape, dt)))

        def ps(name, shape, dt):
            return TB(cur[0].enter_context(nc.psum_tensor("p_" + name, shape, dt)))

        prm = sb("prm", [128, NPRM], F32)
        cbf = sb("cbf", [128, 3, 128], BF16)
        cf = sb("cf", [128, 1408], F32)
        lora = sb("lora", [128, 3, 128], F32)
        glu_bf = sb("glu_bf", [128, 128], BF16)
        K.op("sp", lambda e: e.dma_start(out=prm.t[:], in_=prm_d), writes=[prm.b], kind="d")
        K.op("sp", lambda e: e.dma_start(out=cbf.t[:], in_=cbf_d), writes=[cbf.b], kind="d")
        K.op("sp", lambda e: e.dma_start(out=cf.t[:], in_=cf_d), writes=[cf.b], kind="d")
        K.op("sp", lambda e: e.dma_start(out=lora.t[:], in_=lora_d), writes=[lora.b], kind="d")
        K.op("pool", lambda e: e.tensor_copy(out=glu_bf.t[0:96, :], in_=lora.t[0:96, 2, :]), reads=[lora.b], writes=[glu_bf.b])
        K.op("pool", lambda e: e.tensor_scalar(out=prm.t[:, P_OMKA:P_OMKA + 1], in0=prm.t[:, P_KA:P_KA + 1], scalar1=-1.0, scalar2=1.0,
                                               op0=ALU.mult, op1=ALU.add), reads=[prm.b], writes=[prm.b])
        ident = cbf.t[:, 0, :]
        prot = cbf.t[:, 1, :]
        ident2 = cbf.t[:, 0:3:2, :]
        iota = cf.t[:, 0:512]
        blockones = cf.t[:, 512:640]
        maskXY = cf.t[:, 640:1152].rearrange("p (h x) -> p h x", h=2)
        mask3 = cf.t[:, 1152:1408]
        ones128 = None

        def pc(col, n=128, p0=0):
            return prm.t[p0:p0 + n, col:col + 1]

        def phase1():
            w_bf = sb("w_bf", [128, 8, NCOL], BF16)
            wst = [sb(f"wst{i}", [128, NCOL], F32) for i in range(2)]
            for kc in range(8):
                i = kc % 2
                K.op("sp", lambda e, kc=kc, i=i: e.dma_start(out=wst[i].t[:], in_=w_in[kc * 128:(kc + 1) * 128, :]),
                     writes=[wst[i].b], kind="d")
                if kc % 2 == 0:
                    K.op("dve", lambda e, kc=kc, i=i: e.tensor_scalar(out=w_bf.t[:, kc, :], in0=wst[i].t[:], scalar1=pc(P_GAIN + kc),
                                                                     scalar2=None, op0=ALU.mult),
                         reads=[wst[i].b, prm.b], writes=[w_bf.b])
                else:
                    K.op("act", lambda e, kc=kc, i=i: e.activation(out=w_bf.t[:, kc, :], in_=wst[i].t[:], func=AF.Copy, scale=pc(P_GAIN + kc)),
                         reads=[wst[i].b, prm.b], writes=[w_bf.b])

            NXT = 6
            xts = [sb(f"xt{i}", [128, D], F32) for i in range(NXT)]
            junk = sb("junk", [128, D], BF16)
            ssq = sb("ssq", [128, 64], F32)
            rstd = sb("rstd", [128, 64], F32)
            b_ss = [Buf() for _ in range(NB)]
            xns = [sb(f"xn{i}", [128, D], BF16) for i in range(2)]
            hTs = [sb(f"hT{i}", [128, 8, 512], BF16) for i in range(2)]
            tpbs = [ps(f"tpb{i}", [128, 8, 128], BF16) for i in range(2)]
            pjs = [ps(f"pj{i}", [128, 512], F32) for i in range(2)]
            bank3 = ps("bank3", [128, 512], F32)
            AA = ps("AA", [128, 512], F32)
            psxy = ps("psxy", [128, 2, 256], F32)
            bank7 = ps("bank7", [128, 512], F32)
            b_at = b_pp = bank3.b
            b_pz = bank7.b
            b_tok, b_ynT = tpbs[0].b, tpbs[1].b
            b_pu = b_psn = b_py = bank7.b
            ps_at = bank3.t[:, 0:256].rearrange("p (h x) -> p h x", h=2)
            ps_p = bank3.t[:, 256:512].rearrange("p (h x) -> p h x", h=2)
            ps_z = bank7.t[:, 384:512].rearrange("p (h x) -> p h x", h=2)
            ps_tok = tpbs[0].t[:, 0:5, :]
            ps_ynT = tpbs[1].t[:, 0:4, :].rearrange("p k x -> p (k x)")
            ps_u = bank7.t[:, 0:128].rearrange("p (h x) -> p h x", h=2)
            ps_s = bank7.t[:, 128:256].rearrange("p (h x) -> p h x", h=2)
            ps_y = bank7.t[:, 256:384]
            ostg = [sb(f"ostg{i}", [128, 512], BF16) for i in range(4)]
            ostg_i = [0]

            def next_ostg():
                o = ostg[ostg_i[0] % 4]
                ostg_i[0] += 1
                return o
            pbuf = [sb(f"pbuf{m}", [128, 513], F32) for m in range(5)]
            pmix = [sb(f"pmix{m}", [128, 512], F32) for m in range(5)]
            qb = sb("qb", [128, 512], BF16)
            ang = sb("ang", [128, 512], F32)
            angi = sb("angi", [128, 512], I32)
            ctab = sb("ctab", [128, 512], F32)
            stab = sb("stab", [128, 512], F32)
            t1, t2 = wst[0], wst[1]
            for m in range(5):
                K.op("pool", lambda e, m=m: e.memset(pbuf[m].t[:, 0:1], 0.0), writes=[pbuf[m].b])

            if do_rwkv:
                ones = sb("ones", [128, 128], F32)
                K.op("pool", lambda e: e.memset(ones.t[:], 1.0), writes=[ones.b])
                tw = sb("tw", [32, 512], F32)
                sg = sb("sg", [128, 512], F32)
                aa = sb("aa", [128, 512], F32)
                sgd = sb("sgd", [96, 512], BF16)
                gTS = [sb(f"gT{i}", [128, 512], BF16) for i in range(2)]
                ELcS = [sb(f"ELc{i}", [128, 4], F32) for i in range(2)]
                Lbuf = sb("Lbuf", [128, 4, 129], F32)
                K.op("pool", lambda e: e.memset(Lbuf.t[:], 0.0), writes=[Lbuf.b])
                EL = sb("EL", [128, 512], F32)
                ELn = sb("ELn", [128, 512], F32)
                ELx = sb("ELx", [128, 512], F32)
                kkr = sb("kkr", [128, 512], F32)
                sq = sb("sq", [128, 512], F32)
                rn = sb("rn", [128, 512], F32)
                kk = sb("kk", [128, 512], F32)
                k2 = sb("k2", [128, 512], F32)
                kka = sb("kka", [128, 512], F32)
                bonusTS = [sb(f"bonusT{i}", [128, 512], F32) for i in range(2)]
                ARzS = [[sb(f"ARz{i}_{h}", [128, 4, 256], BF16) for h in range(2)] for i in range(2)]
                for i in range(2):
                    for h in range(2):
                        K.op("pool", lambda e, h=h, i=i: e.memset(ARzS[i][h].t[:], 0.0), writes=[ARzS[i][h].b])
                KTS = [sb(f"KT{i}", [128, 512], BF16) for i in range(2)]
                BTS = [sb(f"BT{i}", [128, 512], BF16) for i in range(2)]
                VBS = [sb(f"VB{i}", [128, 512], BF16) for i in range(2)]
                TOK = [sb(f"TOK{c}", [128, 5, 128], BF16) for c in range(4)]
                M3 = [[sb(f"M3_{c}_{h}", [128, 384], BF16) for h in range(2)] for c in range(4)]
                XY = [[sb(f"XY{c}_{i}", [128, 2, 256], BF16) for i in range(2)] for c in range(4)]
                PP = [[sb(f"PP{c}_{i}", [128, 2, 128], BF16) for i in range(2)] for c in range(4)]
                ATz = [[sb(f"ATz{c}_{h}", [128, 128], BF16) for h in range(2)] for c in range(4)]
                for c in range(4):
                    for h in range(2):
                        K.op("pool", lambda e, c=c, h=h: e.memset(ATz[c][h].t[:], 0.0), writes=[ATz[c][h].b])
                Zs = [sb(f"Zs{c}", [128, 2, 64], BF16) for c in range(4)]
                Us = [sb(f"Us{c}", [128, 2, 64], BF16) for c in range(4)]
                Sbf = [sb(f"Sbf{i}", [128, 64], BF16) for i in range(2)]
                K.op("pool", lambda e: e.memset(Sbf[0].t[:], 0.0), writes=[Sbf[0].b])
                K.op("pool", lambda e: e.memset(Sbf[1].t[:], 0.0), writes=[Sbf[1].b])
                ys = sb("ys", [128, 4, 128], F32)
                bst = sb("bst", [128, 8, 6], F32)
                mv = sb("mv", [128, 8, 2], F32)
                grs = sb("grs", [128, 8], F32)
                yn = sb("yn", [128, 4, 128], BF16)
                yt = sb("yt", [128, 512], F32)
                s_idx = [0]

            TWO_PI_S = 2 * PI * (1 - 1e-6)
            Cb = sb("Cb", [128, 512], F32)
            Sb = sb("Sb", [128, 512], F32)
            cjs = sb("cjs", [128, 16], F32)
            sjs = sb("sjs", [128, 16], F32)
            nsj = sb("nsj", [128, 16], F32)

            def sincos(n, u_fn, s_out, c_out):
                K.op("dve", lambda e: u_fn(e, ang.t[:, 0:n]), reads=[cf.b, prm.b], writes=[ang.b])
                for add, dst in ((0.0, s_out), (0.25, c_out)):
                    if add:
                        K.op("dve", lambda e: e.tensor_scalar(out=ang.t[:, 0:n], in0=ang.t[:, 0:n], scalar1=0.25, scalar2=None, op0=ALU.add), reads=[ang.b], writes=[ang.b])
                    K.op("dve", lambda e: e.tensor_copy(out=angi.t[:, 0:n], in_=ang.t[:, 0:n]), reads=[ang.b], writes=[angi.b])
                    K.op("dve", lambda e: e.tensor_copy(out=t2.t[:, 0:n], in_=angi.t[:, 0:n]), reads=[angi.b], writes=[t2.b])
                    K.op("dve", lambda e: e.tensor_tensor(out=t1.t[:, 0:n], in0=ang.t[:, 0:n], in1=t2.t[:, 0:n], op=ALU.subtract), reads=[ang.b, t2.b], writes=[t1.b])
                    K.op("act", lambda e, dst=dst: e.activation(out=dst.t[:, 0:n], in_=t1.t[:, 0:n], func=AF.Sin, scale=TWO_PI_S), reads=[t1.b], writes=[dst.b])
            sincos(512, lambda e, o: e.tensor_scalar(out=o, in0=iota, scalar1=pc(P_F2PI), scalar2=None, op0=ALU.mult), Sb, Cb)
            sincos(16, lambda e, o: e.tensor_scalar(out=o, in0=iota[:, 0:16], scalar1=512.0, scalar2=pc(P_F2PI), op0=ALU.mult, op1=ALU.mult), sjs, cjs)
            K.op("dve", lambda e: e.tensor_scalar(out=nsj.t[:], in0=sjs.t[:], scalar1=-1.0, scalar2=None, op0=ALU.mult), reads=[sjs.b], writes=[nsj.b])
            pj_rr = [0]

            def next_pj():
                p = pjs[pj_rr[0] % 2]
                pj_rr[0] += 1
                return p

            def stageA(j):
                hT = hTs[j % 2]
                for tt in range(4):
                    ti = 4 * j + tt
                    xt = xts[ti % NXT]
                    r0 = 512 * j + 128 * tt
                    K.op("sp", lambda e, xt=xt, r0=r0: e.dma_start(out=xt.t[:], in_=xb[r0:r0 + 128, :]), writes=[xt.b], kind="d")
                    K.op("act", lambda e, xt=xt, ti=ti: e.activation(out=junk.t[:], in_=xt.t[:], func=AF.Square, accum_out=ssq.t[:, ti:ti + 1]),
                         reads=[xt.b], writes=[junk.b, b_ss[j]])
                K.op("act", lambda e, j=j: e.activation(out=rstd.t[:, 4 * j:4 * j + 4], in_=ssq.t[:, 4 * j:4 * j + 4], func=AF.Sqrt, scale=1.0 / D, bias=pc(P_EPS)),
                     reads=[b_ss[j], prm.b], writes=[b_ss[j]])
                K.op("dve", lambda e, j=j: e.reciprocal(out=rstd.t[:, 4 * j:4 * j + 4], in_=rstd.t[:, 4 * j:4 * j + 4]), reads=[b_ss[j]], writes=[b_ss[j]])
                for tt in range(4):
                    ti = 4 * j + tt
                    xt = xts[ti % NXT]
                    xn = xns[ti % 2]
                    K.op("act", lambda e, xt=xt, xn=xn, ti=ti: e.activation(out=xn.t[:], in_=xt.t[:], func=AF.Copy, scale=rstd.t[:, ti:ti + 1]),
                         reads=[xt.b, b_ss[j]], writes=[xn.b])
                    tpb = tpbs[ti % 2]
                    for fc in range(8):
                        K.op("pe", lambda e, xn=xn, fc=fc, tpb=tpb: e.transpose(out=tpb.t[:, fc, :], in_=xn.t[:, fc * 128:(fc + 1) * 128], identity=ident),
                             reads=[xn.b, cbf.b], writes=[tpb.b])
                    K.op("act", lambda e, hT=hT, tt=tt, tpb=tpb: e.copy(out=hT.t[:, :, tt * 128:(tt + 1) * 128], in_=tpb.t[:]),
                         reads=[tpb.b], writes=[hT.b])
                    yield
                K.op("dve", lambda e, j=j: e.tensor_scalar(out=ctab.t[:], in0=Cb.t[:], scalar1=cjs.t[:, j:j + 1], scalar2=None, op0=ALU.mult),
                     reads=[Cb.b, cjs.b], writes=[ctab.b])
                K.op("dve", lambda e, j=j: e.scalar_tensor_tensor(out=ctab.t[:], in0=Sb.t[:], scalar=nsj.t[:, j:j + 1], in1=ctab.t[:], op0=ALU.mult, op1=ALU.add),
                     reads=[Sb.b, nsj.b, ctab.b], writes=[ctab.b])
                K.op("dve", lambda e, j=j: e.tensor_scalar(out=stab.t[:], in0=Cb.t[:], scalar1=sjs.t[:, j:j + 1], scalar2=None, op0=ALU.mult),
                     reads=[Cb.b, sjs.b], writes=[stab.b])
                K.op("dve", lambda e, j=j: e.scalar_tensor_tensor(out=stab.t[:], in0=Sb.t[:], scalar=cjs.t[:, j:j + 1], in1=stab.t[:], op0=ALU.mult, op1=ALU.add),
                     reads=[Sb.b, cjs.b, stab.b], writes=[stab.b])
                for m in range(8):
                    pj = next_pj()
                    mw = MW[m]
                    for kc in range(8):
                        K.op("pe", lambda e, pj=pj, m=m, kc=kc, mw=mw, hT=hT: e.matmul(
                            out=pj.t[0:mw, :], lhsT=w_bf.t[:, kc, MOFF[m]:MOFF[m] + mw], rhs=hT.t[:, kc, :], start=(kc == 0), stop=(kc == 7)),
                            reads=[w_bf.b, hT.b], writes=[pj.b])
                    if m < 5:
                        pb = pbuf[m]
                        K.op("act", lambda e, pb=pb, pj=pj, mw=mw: e.copy(out=pb.t[0:mw, 1:513], in_=pj.t[0:mw, :]), reads=[pj.b], writes=[pb.b])
                        K.op("dve", lambda e, pb=pb, mw=mw: e.tensor_tensor(out=t1.t[0:mw, 0:512], in0=pb.t[0:mw, 0:512], in1=pb.t[0:mw, 1:513], op=ALU.subtract),
                             reads=[pb.b], writes=[t1.b])
                        K.op("dve", lambda e, pb=pb, mw=mw, m=m: e.scalar_tensor_tensor(out=pmix[m].t[0:mw, :], in0=t1.t[0:mw, 0:512], scalar=pc(P_MIX + m, mw),
                                                                                     in1=pb.t[0:mw, 1:513], op0=ALU.mult, op1=ALU.add),
                             reads=[pb.b, t1.b, prm.b], writes=[pmix[m].b])
                        K.op("pool", lambda e, pb=pb, mw=mw: e.tensor_copy(out=pb.t[0:mw, 0:1], in_=pb.t[0:mw, 512:513]), reads=[pb.b], writes=[pb.b])
                    elif m < 7:
                        og = next_ostg()
                        K.op("act", lambda e, pj=pj: e.copy(out=qb.t[:], in_=pj.t[:]), reads=[pj.b], writes=[qb.b])
                        pr = next_pj()
                        K.op("pe", lambda e, pr=pr: e.matmul(out=pr.t[:], lhsT=prot, rhs=qb.t[:], start=True, stop=True), reads=[qb.b, cbf.b], writes=[pr.b])
                        K.op("dve", lambda e: e.tensor_tensor(out=t1.t[:, 0:512], in0=qb.t[:], in1=ctab.t[:], op=ALU.mult), reads=[qb.b, ctab.b], writes=[t1.b])
                        K.op("dve", lambda e, pr=pr: e.tensor_tensor(out=t2.t[:, 0:512], in0=pr.t[:], in1=stab.t[:], op=ALU.mult), reads=[pr.b, stab.b], writes=[t2.b])
                        K.op("pool", lambda e, og=og: e.tensor_tensor(out=og.t[:], in0=t1.t[:, 0:512], in1=t2.t[:, 0:512], op=ALU.add),
                             reads=[t1.b, t2.b], writes=[og.b])
                        K.op("sp", lambda e, og=og, j=j, m=m: e.dma_start(out=dbg_q[m - 5, :, 512 * j:512 * j + 512], in_=og.t[:]), reads=[og.b], kind="d")
                    else:
                        og = next_ostg()
                        K.op("act", lambda e, pj=pj, og=og: e.copy(out=og.t[:], in_=pj.t[:]), reads=[pj.b], writes=[og.b])
                        K.op("sp", lambda e, og=og, j=j: e.dma_start(out=dbg_q[2, :, 512 * j:512 * j + 512], in_=og.t[:]), reads=[og.b], kind="d")
                    yield
                yield
            def stageB(j, ARz, KT, BT, VB, ELc, bonusT, gT):
                rp, kp, vp, lo, gd = pmix
                K.op("act", lambda e: e.activation(out=tw.t[:], in_=lo.t[0:32, :], func=AF.Tanh), reads=[lo.b], writes=[tw.b])
                pz = next_pj()
                K.op("pe", lambda e, pz=pz: e.matmul(out=pz.t[:], lhsT=lora.t[0:32, 0, :], rhs=tw.t[:], start=True, stop=True),
                     reads=[lora.b, tw.b], writes=[pz.b])
                K.op("act", lambda e, pz=pz: e.activation(out=sg.t[:], in_=pz.t[:], func=AF.Sigmoid, bias=pc(P_W0)), reads=[pz.b, prm.b], writes=[sg.b])
                pa = next_pj()
                K.op("pe", lambda e, pa=pa: e.matmul(out=pa.t[:], lhsT=lora.t[32:64, 1, :], rhs=lo.t[32:64, :], start=True, stop=True),
                     reads=[lora.b, lo.b], writes=[pa.b])
                K.op("act", lambda e, pa=pa: e.activation(out=aa.t[:], in_=pa.t[:], func=AF.Sigmoid, bias=pc(P_A0)), reads=[pa.b, prm.b], writes=[aa.b])
                yield
                K.op("act", lambda e: e.activation(out=sgd.t[:], in_=gd.t[0:96, :], func=AF.Sigmoid), reads=[gd.b], writes=[sgd.b])
                pg = next_pj()
                K.op("pe", lambda e, pg=pg: e.matmul(out=pg.t[:], lhsT=glu_bf.t[0:96, :], rhs=sgd.t[:], start=True, stop=True),
                     reads=[glu_bf.b, sgd.b], writes=[pg.b])
                K.op("act", lambda e, pg=pg: e.copy(out=gT.t[:], in_=pg.t[:]), reads=[pg.b], writes=[gT.b])
                yield
                for c in range(4):
                    K.op("dve", lambda e, c=c: e.tensor_tensor_scan(out=Lbuf.t[:, c, 1:129], data0=ones.t[:], data1=sg.t[:, 128 * c:128 * c + 128],
                                                                    initial=0.0, op0=ALU.mult, op1=ALU.add), reads=[ones.b, sg.b], writes=[Lbuf.b])
                K.op("act", lambda e: e.activation(out=v3(EL.t[:]), in_=Lbuf.t[:, :, 1:129], func=AF.Exp, scale=-C0), reads=[Lbuf.b], writes=[EL.b])
                K.op("act", lambda e: e.activation(out=v3(ELn.t[:]), in_=Lbuf.t[:, :, 1:129], func=AF.Exp, scale=C0), reads=[Lbuf.b], writes=[ELn.b])
                K.op("act", lambda e: e.activation(out=v3(ELx.t[:]), in_=Lbuf.t[:, :, 0:128], func=AF.Exp, scale=-C0), reads=[Lbuf.b], writes=[ELx.b])
                K.op("pool", lambda e: e.tensor_copy(out=ELc.t[:], in_=EL.t[:, 127:512:128]), reads=[EL.b], writes=[ELc.b])
                yield
                K.op("dve", lambda e: e.tensor_scalar(out=kkr.t[:], in0=kp.t[:], scalar1=pc(P_KK), scalar2=None, op0=ALU.mult), reads=[kp.b, prm.b], writes=[kkr.b])
                K.op("pool", lambda e: e.tensor_tensor(out=sq.t[:], in0=kkr.t[:], in1=kkr.t[:], op=ALU.mult), reads=[kkr.b], writes=[sq.b])
                pss = next_pj()
                K.op("pe", lambda e, pss=pss: e.matmul(out=pss.t[:], lhsT=blockones, rhs=sq.t[:], start=True, stop=True), reads=[cf.b, sq.b], writes=[pss.b])
                K.op("act", lambda e, pss=pss: e.activation(out=rn.t[:], in_=pss.t[:], func=AF.Sqrt, bias=pc(P_TINY)), reads=[pss.b, prm.b], writes=[rn.b])
                K.op("dve", lambda e: e.reciprocal(out=rn.t[:], in_=rn.t[:]), reads=[rn.b], writes=[rn.b])
                K.op("pool", lambda e: e.tensor_tensor(out=kk.t[:], in0=kkr.t[:], in1=rn.t[:], op=ALU.mult), reads=[kkr.b, rn.b], writes=[kk.b])
                yield
                K.op("dve", lambda e: e.tensor_scalar(out=k2.t[:], in0=aa.t[:], scalar1=pc(P_KA), scalar2=pc(P_OMKA), op0=ALU.mult, op1=ALU.add),
                     reads=[aa.b, prm.b], writes=[k2.b])
                K.op("pool", lambda e: e.tensor_tensor(out=k2.t[:], in0=k2.t[:], in1=kp.t[:], op=ALU.mult), reads=[k2.b, kp.b], writes=[k2.b])
                yield
                K.op("dve", lambda e: e.scalar_tensor_tensor(out=sq.t[:], in0=rp.t[:], scalar=pc(P_RK), in1=k2.t[:], op0=ALU.mult, op1=ALU.mult),
                     reads=[rp.b, k2.b, prm.b, sq.b], writes=[sq.b])
                pbs = next_pj()
                K.op("pe", lambda e, pbs=pbs: e.matmul(out=pbs.t[:], lhsT=blockones, rhs=sq.t[:], start=True, stop=True), reads=[cf.b, sq.b], writes=[pbs.b])
                K.op("dve", lambda e, pbs=pbs: e.tensor_tensor(out=bonusT.t[:], in0=pbs.t[:], in1=vp.t[:], op=ALU.mult), reads=[pbs.b, vp.b], writes=[bonusT.b])
                K.op("pool", lambda e: e.tensor_tensor(out=kka.t[:], in0=kk.t[:], in1=aa.t[:], op=ALU.mult), reads=[kk.b, aa.b], writes=[kka.b])
                yield
                for h in range(2):
                    hs = slice(64 * h, 64 * h + 64)
                    K.op("dve", lambda e, h=h, hs=hs: e.tensor_tensor(out=ARz[h].t[hs, :, 128:256], in0=v3(rp.t[hs, :]), in1=v3(EL.t[hs, :]), op=ALU.mult),
                         reads=[rp.b, EL.b], writes=[ARz[h].b])
                    K.op("dve", lambda e, h=h, hs=hs: e.scalar_tensor_tensor(out=ARz[h].t[hs, :, 0:128], in0=v3(kk.t[hs, :]), scalar=-1.0, in1=v3(ELx.t[hs, :]), op0=ALU.mult, op1=ALU.mult),
                         reads=[kk.b, ELx.b], writes=[ARz[h].b])
                K.op("dve", lambda e: e.tensor_tensor(out=KT.t[:], in0=k2.t[:], in1=ELn.t[:], op=ALU.mult), reads=[k2.b, ELn.b], writes=[KT.b])
                K.op("pool", lambda e: e.tensor_tensor(out=BT.t[:], in0=kka.t[:], in1=ELn.t[:], op=ALU.mult), reads=[kka.b, ELn.b], writes=[BT.b])
                K.op("pool", lambda e: e.tensor_copy(out=VB.t[:], in_=vp.t[:]), reads=[vp.b], writes=[VB.b])
                yield

            def blockCDE(j, pump, ARz, KT, BT, VB, ELc, bonusT, gT):
                for c in range(4):
                    cs = slice(128 * c, 128 * c + 128)
                    K.op("pe", lambda e, cs=cs: e.transpose(out=ps_tok[:, 0, :], in_=KT.t[:, cs], identity=ident), reads=[KT.b, cbf.b], writes=[b_tok])
                    K.op("pe", lambda e, cs=cs: e.transpose(out=ps_tok[:, 1, :], in_=BT.t[:, cs], identity=ident), reads=[BT.b, cbf.b], writes=[b_tok])
                    for h in range(2):
                        K.op("pe", lambda e, c=c, h=h: e.transpose(out=ps_tok[:, 2 + h, :], in_=ARz[h].t[:, c, 0:128], identity=ident), reads=[ARz[h].b, cbf.b], writes=[b_tok])
                    K.op("pe", lambda e, cs=cs: e.transpose(out=ps_tok[:, 4, :], in_=VB.t[:, cs], identity=ident), reads=[VB.b, cbf.b], writes=[b_tok])
                    K.op("act", lambda e, c=c: e.copy(out=TOK[c].t[:], in_=ps_tok), reads=[b_tok], writes=[TOK[c].b])
                if stage < 2:
                    return
                for c in range(4):
                    cs = slice(128 * c, 128 * c + 128)
                    for h in range(2):
                        hs = slice(64 * h, 64 * h + 64)
                        K.op("pe", lambda e, c=c, cs=cs, hs=hs, h=h: e.matmul(out=AA.t[:, 256 * h:256 * h + 128], lhsT=BT.t[:, cs], rhs=ARz[h].t[:, c, 0:128], start=True, stop=True),
                             reads=[BT.b, ARz[h].b], writes=[AA.b])
                        K.op("pe", lambda e, c=c, cs=cs, hs=hs, h=h: e.matmul(out=AA.t[:, 256 * h + 128:256 * h + 256], lhsT=ARz[h].t[:, c, 0:128], rhs=BT.t[:, cs], start=True, stop=True),
                             reads=[BT.b, ARz[h].b], writes=[AA.b])
                    K.op("dve", lambda e, c=c: e.tensor_tensor(out=XY[c][0].t[:], in0=AA.t[:].rearrange("p (h x) -> p h x", h=2), in1=maskXY, op=ALU.mult),
                         reads=[AA.b, cf.b], writes=[XY[c][0].b])
                    if stage >= 2.2:
                        K.op("pool", lambda e, c=c: e.tensor_tensor(out=PP[c][0].t[:], in0=XY[c][0].t[:, :, 0:128], in1=ident2, op=ALU.add),
                             reads=[XY[c][0].b, cbf.b], writes=[PP[c][0].b])
                if stage < 2.5:
                    return
                for c in range(4):
                    cs = slice(128 * c, 128 * c + 128)
                    for h in range(2):
                        hs = slice(64 * h, 64 * h + 64)
                        K.op("pe", lambda e, c=c, cs=cs, h=h: e.matmul(out=AA.t[:, 0:128], lhsT=BT.t[:, cs], rhs=ARz[h].t[:, c, 128:256], start=True, stop=True),
                             reads=[BT.b, ARz[h].b], writes=[AA.b])
                        K.op("pe", lambda e, c=c, cs=cs, h=h: e.matmul(out=AA.t[:, 128:384], lhsT=KT.t[:, cs], rhs=ARz[h].t[:, c, 0:256], start=True, stop=True),
                             reads=[KT.b, ARz[h].b], writes=[AA.b])
                        K.op("dve", lambda e, c=c, h=h: e.tensor_tensor(out=M3[c][h].t[:, 0:256], in0=AA.t[:, 0:256], in1=mask3, op=ALU.mult),
                             reads=[AA.b, cf.b], writes=[M3[c][h].b])
                        K.op("dve", lambda e, c=c, h=h: e.tensor_tensor(out=M3[c][h].t[:, 256:384], in0=AA.t[:, 256:384], in1=mask3[:, 0:128], op=ALU.mult),
                             reads=[AA.b, cf.b], writes=[M3[c][h].b])
                if stage < 3:
                    return
                xy_ring = [(psxy.t[:], psxy.b), (bank7.t[:].rearrange("p (h x) -> p h x", h=2), bank7.b)]
                pp_ring = [(ps_p, bank3.b), (AA.t[:, 0:256].rearrange("p (h x) -> p h x", h=2), AA.b)]
                for kq in range(6):
                    cur, nxt = kq % 2, (kq + 1) % 2

                    def xy_part(c, kq=kq, cur=cur, nxt=nxt):
                        Xc, Xn = XY[c][cur], XY[c][nxt]
                        pxy, bxy = xy_ring[c % 2]
                        for h in range(2):
                            if kq < 5:
                                K.op("pe", lambda e, h=h: e.matmul(out=pxy[:, h, 0:128], lhsT=Xc.t[:, h, 128:256], rhs=Xc.t[:, h, 0:128], start=True, stop=True),
                                     reads=[Xc.b], writes=[bxy])
                            K.op("pe", lambda e, h=h: e.matmul(out=pxy[:, h, 128:256], lhsT=Xc.t[:, h, 0:128], rhs=Xc.t[:, h, 128:256], start=True, stop=True),
                                 reads=[Xc.b], writes=[bxy])
                        if kq < 5:
                            K.op("act", lambda e: e.copy(out=Xn.t[:], in_=pxy), reads=[bxy], writes=[Xn.b])
                        else:
                            K.op("act", lambda e: e.copy(out=Xn.t[:, :, 128:256], in_=pxy[:, :, 128:256]), reads=[bxy], writes=[Xn.b])

                    def p_part(c, kq=kq, cur=cur, nxt=nxt):
                        Xn = XY[c][nxt]
                        Pc, Pn = PP[c][cur], PP[c][nxt]
                        ppp, bpp = pp_ring[c % 2]
                        for h in range(2):
                            K.op("pe", lambda e, h=h: e.matmul(out=ppp[:, h, :], lhsT=Xn.t[:, h, 128:256], rhs=Pc.t[:, h, :], start=True, stop=True),
                                 reads=[Xn.b, Pc.b], writes=[bpp])
                        K.op("dve", lambda e: e.tensor_tensor(out=Pn.t[:], in0=ppp, in1=Pc.t[:], op=ALU.add), reads=[bpp, Pc.b], writes=[Pn.b])
                    xy_part(0)
                    for c in range(1, 4):
                        xy_part(c)
                        p_part(c - 1)
                        pump()
                    p_part(3)
                    pump()
                if stage < 4:
                    return
                for c in range(4):
                    Tm = PP[c][0]
                    for h in range(2):
                        hs = slice(64 * h, 64 * h + 64)
                        K.op("pe", lambda e, c=c, h=h, hs=hs, Tm=Tm: e.matmul(out=ps_at[:, h, :], lhsT=TOK[c].t[:, 2 + h, :], rhs=Tm.t[:, h, :], start=True, stop=True),
                             reads=[TOK[c].b, Tm.b], writes=[b_at])
                        K.op("pe", lambda e, c=c, h=h, hs=hs: e.matmul(out=ps_z[:, h, :], lhsT=M3[c][h].t[:, 128:256], rhs=TOK[c].t[:, 4, hs], start=True, stop=True),
                             reads=[M3[c][h].b, TOK[c].b], writes=[b_pz])
                    for h in range(2):
                        hs = slice(64 * h, 64 * h + 64)
                        K.op("act", lambda e, c=c, h=h, hs=hs: e.copy(out=ATz[c][h].t[hs, :], in_=ps_at[hs, h, :]), reads=[b_at], writes=[ATz[c][h].b])
                    K.op("dve", lambda e, c=c: e.tensor_copy(out=Zs[c].t[:], in_=ps_z), reads=[b_pz], writes=[Zs[c].b])
                if stage < 5:
                    return
                for c in range(4):
                    Tm = PP[c][0]
                    Sc = Sbf[s_idx[0] % 2]
                    Sn = Sbf[(s_idx[0] + 1) % 2]
                    s_idx[0] += 1
                    for h in range(2):
                        hs = slice(64 * h, 64 * h + 64)
                        K.op("pe", lambda e, c=c, h=h, Tm=Tm: e.matmul(out=ps_u[:, h, :], lhsT=Tm.t[:, h, :], rhs=Zs[c].t[:, h, :], start=True, stop=False),
                             reads=[Tm.b, Zs[c].b], writes=[b_pu])
                        K.op("pe", lambda e, c=c, h=h, hs=hs, Sc=Sc: e.matmul(out=ps_u[:, h, :], lhsT=ATz[c][h].t[:], rhs=Sc.t[:], start=False, stop=True),
                             reads=[ATz[c][h].b, Sc.b], writes=[b_pu])
                    K.op("act", lambda e, c=c: e.copy(out=Us[c].t[:], in_=ps_u), reads=[b_pu], writes=[Us[c].b])
                    pump()
                    for h in range(2):
                        hs = slice(64 * h, 64 * h + 64)
                        K.op("pe", lambda e, c=c, h=h, hs=hs: e.matmul(out=ps_s[:, h, :], lhsT=TOK[c].t[:, 0, :], rhs=TOK[c].t[:, 4, hs], start=True, stop=False),
                             reads=[TOK[c].b], writes=[b_psn])
                        K.op("pe", lambda e, c=c, h=h, hs=hs: e.matmul(out=ps_s[:, h, :], lhsT=TOK[c].t[:, 1, :], rhs=Us[c].t[:, h, :], start=False, stop=False),
                             reads=[TOK[c].b, Us[c].b], writes=[b_psn])
                        K.op("pe", lambda e, c=c, h=h, hs=hs, Sc=Sc: e.matmul(out=ps_s[:, h, :], lhsT=ident, rhs=Sc.t[:], start=False, stop=True),
                             reads=[cbf.b, Sc.b], writes=[b_psn])
                    for h in range(2):
                        hs = slice(64 * h, 64 * h + 64)
                        K.op("dve" if h == 0 else "act", (lambda e, c=c, Sn=Sn, h=h, hs=hs: e.tensor_scalar(out=Sn.t[hs, :], in0=ps_s[hs, h, :], scalar1=ELc.t[hs, c:c + 1], scalar2=None, op0=ALU.mult)) if h == 0 else
                             (lambda e, c=c, Sn=Sn, h=h, hs=hs: e.activation(out=Sn.t[hs, :], in_=ps_s[hs, h, :], func=AF.Copy, scale=ELc.t[hs, c:c + 1])),
                             reads=[b_psn, ELc.b], writes=[Sn.b])
                    for h in range(2):
                        hs = slice(64 * h, 64 * h + 64)
                        K.op("pe", lambda e, c=c, h=h, hs=hs, Sc=Sc: e.matmul(out=bank7.t[:, 256 + 64 * h:320 + 64 * h], lhsT=ARz[h].t[:, c, 128:256], rhs=Sc.t[:], start=True, stop=False),
                             reads=[ARz[h].b, Sc.b], writes=[b_py])
                        K.op("pe", lambda e, c=c, h=h: e.matmul(out=bank7.t[:, 256 + 64 * h:320 + 64 * h], lhsT=M3[c][h].t[:, 0:128], rhs=Us[c].t[:, h, :], start=False, stop=False),
                             reads=[M3[c][h].b, Us[c].b], writes=[b_py])
                        K.op("pe", lambda e, c=c, h=h, hs=hs: e.matmul(out=bank7.t[:, 256 + 64 * h:320 + 64 * h], lhsT=M3[c][h].t[:, 256:384], rhs=TOK[c].t[:, 4, hs], start=False, stop=True),
                             reads=[M3[c][h].b, TOK[c].b], writes=[b_py])
                    K.op("act", lambda e, c=c: e.copy(out=ys.t[:, c, :], in_=ps_y), reads=[b_py], writes=[ys.b])
                    pump()
                    for h in range(2):
                        K.op("dve", lambda e, c=c, h=h: e.bn_stats(out=bst.t[:, 2 * c + h, :], in_=ys.t[:, c, 64 * h:64 * h + 64]), reads=[ys.b], writes=[bst.b])
                        K.op("dve", lambda e, c=c, h=h: e.bn_aggr(out=mv.t[:, 2 * c + h, :], in_=bst.t[:, 2 * c + h, :]), reads=[bst.b], writes=[mv.b])
                if stage < 6:
                    return
                K.op("act", lambda e: e.activation(out=grs.t[:], in_=mv.t[:, :, 1], func=AF.Sqrt, bias=pc(P_GNEPS)), reads=[mv.b, prm.b], writes=[grs.b])
                K.op("dve", lambda e: e.reciprocal(out=grs.t[:], in_=grs.t[:]), reads=[grs.b], writes=[grs.b])
                for c in range(4):
                    for h in range(2):
                        i = 2 * c + h
                        K.op("dve", lambda e, c=c, h=h, i=i: e.tensor_scalar(out=yn.t[:, c, 64 * h:64 * h + 64], in0=ys.t[:, c, 64 * h:64 * h + 64],
                                                                            scalar1=mv.t[:, i, 0:1], scalar2=grs.t[:, i:i + 1], op0=ALU.subtract, op1=ALU.mult),
                             reads=[ys.b, mv.b, grs.b], writes=[yn.b])
                for c in range(4):
                    K.op("pe", lambda e, c=c: e.transpose(out=ps_ynT[:, 128 * c:128 * c + 128], in_=yn.t[:, c, :], identity=ident), reads=[yn.b, cbf.b], writes=[b_ynT])
                K.op("dve", lambda e: e.tensor_scalar(out=yt.t[:], in0=ps_ynT, scalar1=pc(P_LNW), scalar2=pc(P_LNB), op0=ALU.mult, op1=ALU.add),
                     reads=[b_ynT, prm.b], writes=[yt.b])
                K.op("pool", lambda e: e.tensor_tensor(out=yt.t[:], in0=yt.t[:], in1=bonusT.t[:], op=ALU.add), reads=[yt.b, bonusT.b], writes=[yt.b])
                pump(100)
                og = next_ostg()
                K.op("pool", lambda e, og=og: e.tensor_tensor(out=og.t[:], in0=yt.t[:], in1=gT.t[:], op=ALU.mult),
                     reads=[yt.b, gT.b], writes=[og.b])
                K.op("sp", lambda e, og=og, j=j: e.dma_start(out=gin_rq[j // 4][:, 2 + 512 * (j % 4):2 + 512 * (j % 4) + 512], in_=og.t[:]), reads=[og.b], writes=[b_ginr[j // 4]], kind="d")
                if j % 4 == 3 and j < 15:
                    K.op("sp", lambda e, og=og, j=j: e.dma_start(out=gin_rq[j // 4 + 1][:, 0:2], in_=og.t[:, 510:512]), reads=[og.b], writes=[b_ginr[j // 4 + 1]], kind="d")
                if j == 0:
                    K.op("pool", lambda e: e.memset(qb.t[:, 0:2], 0.0), writes=[qb.b])
                    K.op("sp", lambda e: e.dma_start(out=gin_rq[0][:, 0:2], in_=qb.t[:, 0:2]), reads=[qb.b], writes=[b_ginr[0]], kind="d")
                if j % 4 == 3 and P2 and P6:
                    K.op("pool", lambda e, q=j // 4: e.collective_compute("AllGather", ALU.bypass, replica_groups=[[0, 1, 2, 3], [4, 5, 6, 7]], ins=[gin_rq[q]], outs=[gout_rq[q]]),
                         reads=[b_ginr[j // 4]], kind="cc")
                if debug and False:
                    for i, tl in enumerate([sg, aa, kk, k2, EL, ys]):
                        src = tl.t[:] if tl is not ys else ys.t[:].rearrange("p c t -> p (c t)")
                        K.op("sp", lambda e, i=i, src=src, j=j: e.dma_start(out=dbg_p[i, :, 512 * j:512 * j + 512], in_=src), reads=[tl.b], kind="d")

            import itertools
            b_ginr = [Buf() for _ in range(4)]
            sets = [dict(ARz=ARzS[i], KT=KTS[i], BT=BTS[i], VB=VBS[i], ELc=ELcS[i], bonusT=bonusTS[i], gT=gTS[i]) for i in range(2)]
            for _ in stageA(0):
                pass
            for _ in stageB(0, **sets[0]):
                pass
            for j in range(nblocks):
                gens = [stageA(j + 1), stageB(j + 1, **sets[(j + 1) % 2])] if j + 1 < nblocks else []
                itr = itertools.chain(*gens)

                def pump(k=1, itr=itr):
                    for _ in range(k):
                        next(itr, None)
                blockCDE(j, pump, **sets[j % 2])
                pump(1000)

        def phase2():
            kTt = sb("kTt", [128, S], BF16)
            vTt = sb("vTt", [128, S], BF16)
            qz = sb("qz", [128, 2, S], BF16)
            amask = sb("amask", [128, 512], BF16)
            zt = sb("zt", [128, 2], BF16)
            K.op("pool", lambda e: e.memset(zt.t[:], 0.0), writes=[zt.b])
            b_gina = [Buf() for _ in range(4)]
            K.op("sp", lambda e: e.dma_start(out=gin_aq[0][:, 0:2], in_=zt.t[:]), reads=[zt.b], writes=[b_gina[0]], kind="d")
            K.op("sp", lambda e: e.dma_start(out=amask.t[:], in_=amask_d), writes=[amask.b], kind="d")
            K.op("sp", lambda e: e.dma_start(out=kTt.t[:], in_=dbg_q[1]), writes=[kTt.b], kind="d")
            K.op("pool", lambda e: e.memset(qz.t[64:128, 0, :], 0.0), writes=[qz.b])
            K.op("pool", lambda e: e.memset(qz.t[0:64, 1, :], 0.0), writes=[qz.b])
            K.op("sp", lambda e: e.dma_start(out=qz.t[0:64, 0, :], in_=dbg_q[0, 0:64, :]), writes=[qz.b], kind="d")
            K.op("sp", lambda e: e.dma_start(out=qz.t[64:128, 1, :], in_=dbg_q[0, 64:128, :]), writes=[qz.b], kind="d")
            K.op("sp", lambda e: e.dma_start(out=vTt.t[:], in_=dbg_q[2]), writes=[vTt.b], kind="d")
            Vaug = [sb(f"Vaug{i}", [128, 2, 65], BF16) for i in range(5)]
            for i in range(5):
                K.op("pool", lambda e, i=i: e.memset(Vaug[i].t[:], 1.0), writes=[Vaug[i].b])
            Pm = [sb(f"Pm{i}", [128, 512], BF16) for i in range(4)]
            Ost = [sb(f"Ost{i}", [128, 8, 130], F32) for i in range(2)]
            psS = [ps(f"psS{i}", [128, 512], F32) for i in range(3)]
            psO = [ps(f"psO{i}", [128, 512], F32) for i in range(2)]
            psV = [ps(f"psV{i}", [128, 1024], BF16) for i in range(2)]
            b_Od = [Buf() for _ in range(3)]
            tiles = []
            obi = 0
            ti = 0
            for p, d in enumerate((1, 4, 16)):
                nblk = S // (128 * d)
                Ov = O_d[p].rearrange("(n i r) c -> r i n c", i=128, r=d)
                nb8 = min(8, nblk)
                for r in range(d):
                    vprev = None
                    for n in range(nblk):
                        t0 = r + 128 * d * n
                        T = dict(p=p, d=d, r=r, n=n, Ov=Ov, nb8=nb8, ti=ti,
                                 tok=slice(t0, t0 + 127 * d + 1, d), ptok=slice(t0 - 128 * d, t0 - d + 1, d),
                                 va=Vaug[ti % 5], vprev=vprev, pv=psV[ti % 2], pS=psS[ti % 3], pO=psO[ti % 2], pm=Pm[ti % 4])
                        vprev = T["va"]
                        tiles.append(T)
                        ti += 1

            def front(T):
                pv, va, pS, pm, tok, ptok, n = T["pv"], T["va"], T["pS"], T["pm"], T["tok"], T["ptok"], T["n"]
                K.op("pe", lambda e: e.transpose(out=pv.t[:, 0:128], in_=vTt.t[:, tok], identity=ident), reads=[vTt.b, cbf.b], writes=[pv.b])
                K.op("dve", lambda e: e.tensor_copy(out=va.t[:, :, 0:64], in_=pv.t[:, 0:128].rearrange("p (h x) -> p h x", h=2)), reads=[pv.b], writes=[va.b])
                K.op("pe", lambda e: e.matmul(out=pS.t[:, 0:256], lhsT=kTt.t[:, tok], rhs=qz.t[:, :, tok], start=True, stop=True), reads=[kTt.b, qz.b], writes=[pS.b])
                w = 256
                if n > 0:
                    K.op("pe", lambda e: e.matmul(out=pS.t[:, 256:512], lhsT=kTt.t[:, ptok], rhs=qz.t[:, :, tok], start=True, stop=True), reads=[kTt.b, qz.b], writes=[pS.b])
                    w = 512
                K.op("act", lambda e: e.activation(out=pm.t[:, 0:w], in_=pS.t[:, 0:w], func=AF.Exp, scale=0.125), reads=[pS.b], writes=[pm.b])
                K.op("pool", lambda e: e.tensor_tensor(out=pm.t[:, 0:w], in0=pm.t[:, 0:w], in1=amask.t[:, 0:w], op=ALU.mult), reads=[pm.b, amask.b], writes=[pm.b])

            def back(T):
                nonlocal obi
                pO, pm, va, vprev, n, p, r, nb8, Ov = T["pO"], T["pm"], T["va"], T["vprev"], T["n"], T["p"], T["r"], T["nb8"], T["Ov"]
                for h in range(2):
                    K.op("pe", lambda e, h=h: e.matmul(out=pO.t[:, 65 * h:65 * h + 65], lhsT=pm.t[:, 128 * h:128 * h + 128], rhs=va.t[:, h, :], start=True, stop=(n == 0)),
                         reads=[pm.b, va.b], writes=[pO.b])
                    if n > 0:
                        K.op("pe", lambda e, h=h: e.matmul(out=pO.t[:, 65 * h:65 * h + 65], lhsT=pm.t[:, 256 + 128 * h:256 + 128 * h + 128], rhs=vprev.t[:, h, :], start=False, stop=True),
                             reads=[pm.b, vprev.b], writes=[pO.b])
                ob = Ost[obi % 2]
                K.op("dve", lambda e: e.tensor_copy(out=ob.t[:, n % 8, :], in_=pO.t[:, 0:130]), reads=[pO.b], writes=[ob.b])
                if n % 8 == nb8 - 1:
                    n0 = n - (nb8 - 1)
                    K.op("sp", lambda e: e.dma_start(out=Ov[r, :, n0:n0 + nb8, :], in_=ob.t[:, 0:nb8, :]), reads=[ob.b], writes=[b_Od[p]], kind="d")
                    obi += 1

            for i, T in enumerate(tiles):
                front(T)
                if i > 1:
                    back(tiles[i - 2])
            back(tiles[-2])
            back(tiles[-1])
            S3 = sb("S3", [128, 3, 8, 130], F32)
            b_S3 = [Buf() for _ in range(3)]
            sqt = sb("sqt", [128, 16, 64], F32)
            ssq2 = sb("ssq2", [128, 16], F32)
            d2 = sb("d2", [128, 16], F32)
            rr = sb("rr", [128, 16], F32)
            yb = sb("yb", [128, 16, 64], BF16)
            ya = [sb(f"ya{i}", [128, 1024], BF16) for i in range(2)]
            psT = ps("psT", [128, 8, 128], BF16)
            for bt in range(8):
                for p in range(3):
                    src = O_d[p][1024 * bt:1024 * bt + 1024, :].rearrange("(k i) c -> i k c", i=128)
                    K.op("sp", lambda e, p=p, src=src: e.dma_start(out=S3.t[:, p, :, :], in_=src), reads=[b_Od[p]], writes=[b_S3[p]], kind="d")
                acc = S3.t[:, 0, :, :].rearrange("p k c -> p (k c)")
                for p in (1, 2):
                    K.op("dve", lambda e, p=p, acc=acc: e.tensor_tensor(out=acc, in0=acc, in1=S3.t[:, p, :, :].rearrange("p k c -> p (k c)"), op=ALU.add),
                         reads=[b_S3[0], b_S3[p]], writes=[b_S3[0]])
                a16 = S3.t[:, 0, :, :].rearrange("p k (h c) -> p (k h) c", h=2)
                num = a16[:, :, 0:64]
                den = a16[:, :, 64]
                K.op("act", lambda e, num=num: e.activation(out=sqt.t[:], in_=num, func=AF.Square), reads=[b_S3[0]], writes=[sqt.b])
                K.op("dve", lambda e: e.tensor_reduce(out=ssq2.t[:], in_=sqt.t[:], axis=AX.X, op=ALU.add), reads=[sqt.b], writes=[ssq2.b])
                K.op("dve", lambda e, den=den: e.tensor_tensor(out=d2.t[:], in0=den, in1=den, op=ALU.mult), reads=[b_S3[0]], writes=[d2.b])
                K.op("dve", lambda e: e.tensor_scalar(out=d2.t[:], in0=d2.t[:], scalar1=1e-6, scalar2=None, op0=ALU.mult), reads=[d2.b], writes=[d2.b])
                K.op("dve", lambda e: e.scalar_tensor_tensor(out=rr.t[:], in0=ssq2.t[:], scalar=1.0 / 64, in1=d2.t[:], op0=ALU.mult, op1=ALU.add),
                     reads=[ssq2.b, d2.b], writes=[rr.b])
                K.op("act", lambda e: e.activation(out=rr.t[:], in_=rr.t[:], func=AF.Sqrt), reads=[rr.b], writes=[rr.b])
                K.op("dve", lambda e: e.reciprocal(out=rr.t[:], in_=rr.t[:]), reads=[rr.b], writes=[rr.b])
                rrb = bass.AP(rr.t, 0, [[16, 128], [1, 16], [0, 64]])
                K.op("dve", lambda e, num=num, rrb=rrb: e.tensor_tensor(out=yb.t[:], in0=num, in1=rrb, op=ALU.mult), reads=[b_S3[0], rr.b], writes=[yb.b])
                for k in range(8):
                    K.op("pe", lambda e, k=k: e.transpose(out=psT.t[:, k, :], in_=yb.t[:, 2 * k:2 * k + 2, :].rearrange("p h c -> p (h c)"), identity=ident),
                         reads=[yb.b, cbf.b], writes=[psT.b])
                yo = ya[bt % 2]
                K.op("act", lambda e, yo=yo: e.activation(out=yo.t[:], in_=psT.t[:].rearrange("p k c -> p (k c)"), func=AF.Copy, scale=pc(P_AG)),
                     reads=[psT.b, prm.b], writes=[yo.b])
                K.op("sp", lambda e, yo=yo, bt=bt: e.dma_start(out=gin_aq[bt // 2][:, 2 + 1024 * (bt % 2):2 + 1024 * (bt % 2) + 1024], in_=yo.t[:]), reads=[yo.b], writes=[b_gina[bt // 2]], kind="d")
                if bt % 2 == 1 and bt < 7:
                    K.op("sp", lambda e, yo=yo, bt=bt: e.dma_start(out=gin_aq[bt // 2 + 1][:, 0:2], in_=yo.t[:, 1022:1024]), reads=[yo.b], writes=[b_gina[bt // 2 + 1]], kind="d")
                if bt % 2 == 1 and P1 and P6:
                    K.op("pool", lambda e, q=bt // 2: e.collective_compute("AllGather", ALU.bypass, replica_groups=[[0, 1, 2, 3], [4, 5, 6, 7]], ins=[gin_aq[q]], outs=[gout_aq[q]]),
                         reads=[b_gina[bt // 2]], kind="cc")


        def phase6a(h2T, h2Th):
            RG = [[0, 1, 2, 3], [4, 5, 6, 7]]
            b_gr = [Buf() for _ in range(4)]
            b_ga = [Buf() for _ in range(4)]
            if P1 and P2:
                pass
            wo = sb("wo", [128, 8, D], BF16)
            wstg = [sb(f"wstg{i}", [128, 1024], F32) for i in range(2)]
            bw_o = [Buf() for _ in range(8)]
            for kc in range(8):
                stg = wstg[kc % 2]
                K.op("sp", lambda e, stg=stg, kc=kc: e.dma_start(out=stg.t[:], in_=w_out_d[128 * kc:128 * kc + 128, :]), writes=[stg.b], kind="d")
                K.op("pool" if kc % 2 else "dve", lambda e, stg=stg, kc=kc: e.tensor_copy(out=wo.t[:, kc, :], in_=stg.t[:]), reads=[stg.b], writes=[bw_o[kc]])
            Gq = [sb(f"Gq{i}", [128, 4, 512], BF16) for i in range(2)]
            yT = sb("yT", [128, 8, 512], BF16)
            xt6 = [sb(f"xt6_{i}", [128, D], F32) for i in range(2)]
            x1s = [sb(f"x1s{i}", [128, D], F32) for i in range(4)]
            junk6 = sb("junk6", [128, D], BF16)
            ssq6 = sb("ssq6", [128, 4], F32)
            r6 = sb("r6", [128, 4], F32)
            xn6 = [sb(f"xn6_{i}", [128, D], BF16) for i in range(2)]
            psA = ps("psA", [128, 1024], F32)
            tp6 = ps("tp6", [128, 8, 128], BF16)
            gr4 = [g_.rearrange("(c p) t -> p c t", p=128) for g_ in gout_rq]
            ga4 = [g_.rearrange("(c p) t -> p c t", p=128) for g_ in gout_aq]
            gi = [0]

            seltmp = sb("seltmp", [128, 4, 512], BF16)
            yTa_b = Buf()

            def block(kb, ntok, col_in_q, xrow0, hdst, hcol0):
                for q in range(4):
                    col = col_in_q
                    for part, (g4, bg) in enumerate(((gr4[q], b_gr[q]), (ga4[q], b_ga[q]))):
                        G = Gq[gi[0] % 2]
                        gi[0] += 1
                        K.op("sp", lambda e, G=G, col=col, g4=g4: e.dma_start(out=G.t[:, :, 0:ntok], in_=g4[:, :, col:col + ntok]), reads=[bg], writes=[G.b], kind="d")
                        ysl = yT.t[:, 4 * part:4 * part + 4, 0:ntok]
                        if part == 1:
                            if q == 0:
                                K.op("act", lambda e, G=G, ysl=ysl: e.activation(out=ysl, in_=G.t[:, :, 0:ntok], func=AF.Copy, scale=q6(FL)),
                                     reads=[G.b, p6.b], writes=[yTa_b])
                            else:
                                K.op("act", lambda e, G=G, q=q: e.activation(out=seltmp.t[:, :, 0:ntok], in_=G.t[:, :, 0:ntok], func=AF.Copy, scale=q6(FL + q)),
                                     reads=[G.b, p6.b], writes=[seltmp.b])
                                K.op("pool", lambda e, ysl=ysl: e.tensor_tensor(out=ysl, in0=ysl, in1=seltmp.t[:, :, 0:ntok], op=ALU.add),
                                     reads=[seltmp.b, yTa_b], writes=[yTa_b])
                        elif q == 0:
                            K.op("dve", lambda e, G=G, ysl=ysl: e.tensor_scalar(out=ysl, in0=G.t[:, :, 0:ntok], scalar1=q6(FL), scalar2=None, op0=ALU.mult),
                                 reads=[G.b, p6.b], writes=[yT.b])
                        else:
                            K.op("dve", lambda e, G=G, q=q, ysl=ysl: e.scalar_tensor_tensor(out=ysl, in0=G.t[:, :, 0:ntok], scalar=q6(FL + q), in1=ysl,
                                                                                            op0=ALU.mult, op1=ALU.add), reads=[G.b, p6.b, yT.b], writes=[yT.b])
                ntt = (ntok + 127) // 128
                for tt in range(ntt):
                    tw_ = min(128, ntok - 128 * tt)
                    xt = xt6[tt % 2]
                    x1 = x1s[tt]
                    K.op("sp", lambda e, xt=xt, tt=tt, tw_=tw_: e.dma_start(out=xt.t[0:tw_, :], in_=xq_d[xrow0 + 128 * tt:xrow0 + 128 * tt + tw_, :]), writes=[xt.b], kind="d")
                    for half in range(2):
                        for kc in range(8):
                            K.op("pe", lambda e, half=half, kc=kc, tt=tt, tw_=tw_: e.matmul(out=psA.t[0:tw_, 512 * half:512 * half + 512], lhsT=yT.t[:, kc, 128 * tt:128 * tt + tw_],
                                                                                         rhs=wo.t[:, kc, 512 * half:512 * half + 512], start=(kc == 0), stop=(kc == 7)),
                                 reads=[yT.b, yTa_b, bw_o[kc]], writes=[psA.b])
                    K.op("dve", lambda e, xt=xt, x1=x1, tw_=tw_: e.tensor_tensor(out=x1.t[0:tw_, :], in0=psA.t[0:tw_, :], in1=xt.t[0:tw_, :], op=ALU.add),
                         reads=[psA.b, xt.b], writes=[x1.b])
                    if kb >= 0:
                        r0 = 512 * kb + 128 * tt
                        K.op("sp", lambda e, x1=x1, r0=r0: e.dma_start(out=x1_d[r0:r0 + 128, :], in_=x1.t[:]), reads=[x1.b], kind="d")
                    K.op("act", lambda e, x1=x1, tt=tt, tw_=tw_: e.activation(out=junk6.t[0:tw_, :], in_=x1.t[0:tw_, :], func=AF.Square, accum_out=ssq6.t[0:tw_, tt:tt + 1]),
                         reads=[x1.b], writes=[junk6.b, ssq6.b])
                pw = min(128, ntok)
                K.op("act", lambda e: e.activation(out=r6.t[0:pw, 0:ntt], in_=ssq6.t[0:pw, 0:ntt], func=AF.Sqrt, scale=1.0 / D, bias=q6(EP, pw)), reads=[ssq6.b, p6.b], writes=[r6.b])
                K.op("dve", lambda e: e.reciprocal(out=r6.t[0:pw, 0:ntt], in_=r6.t[0:pw, 0:ntt]), reads=[r6.b], writes=[r6.b])
                for tt in range(ntt):
                    tw_ = min(128, ntok - 128 * tt)
                    x1 = x1s[tt]
                    xn = xn6[tt % 2]
                    K.op("act", lambda e, x1=x1, xn=xn, tt=tt, tw_=tw_: e.activation(out=xn.t[0:tw_, :], in_=x1.t[0:tw_, :], func=AF.Copy, scale=r6.t[0:tw_, tt:tt + 1]),
                         reads=[x1.b, r6.b], writes=[xn.b])
                    for fc in range(8):
                        K.op("pe", lambda e, xn=xn, fc=fc, tw_=tw_: e.transpose(out=tp6.t[:, fc, 0:tw_], in_=xn.t[0:tw_, fc * 128:(fc + 1) * 128], identity=cbf.t[0:tw_, 0, 0:tw_]),
                             reads=[xn.b, cbf.b], writes=[tp6.b])
                    c0 = hcol0 + 128 * tt
                    K.op("act", lambda e, tw_=tw_, c0=c0: e.copy(out=hdst.t[:, :, c0:c0 + tw_], in_=tp6.t[:, :, 0:tw_]), reads=[tp6.b], writes=[hdst.b])

            block(-1, 2, 0, 0, h2Th, 0)
            for kb in range(4):
                block(kb, 512, 2 + 512 * kb, 2 + 512 * kb, h2T, 512 * kb)

        def phase6b(h2T, h2Th):
            print("phase6b start remaining", nc.sbuf_bytes_remaining)
            wd = sb("wd", [128, 22, D], BF16)
            wdst = [sb(f"wdst{i}", [128, 1024], F32) for i in range(1)]
            bw_d = [Buf() for _ in range(22)]
            wust = [sb(f"wust{i}", [128, 8, 256], F32) for i in range(1)]
            wub = [sb(f"wub{i}", [128, 8, 256], BF16) for i in range(2)]
            actT = sb("actT", [128, 22, 2048], BF16)
            gbs = [sb(f"gb{i}", [128, 514], F32) for i in range(2)]
            gbi = 0
            c1 = [sb(f"c1_{i}", [128, 512], F32) for i in range(3)]
            psg = [ps(f"psg{i}", [128, 512], F32) for i in range(2)]
            psv = [ps(f"psv{i}", [128, 512], F32) for i in range(2)]
            psA = ps("psA2", [128, 1024], F32)
            wup3 = w_up_d.rearrange("(kc p) c -> p kc c", p=128)
            pi = 0
            for m in range(22):
                ws = wust[0]
                wb = wub[m % 2]
                K.op("sp", lambda e, ws=ws, m=m: e.dma_start(out=ws.t[:, :, 0:128], in_=wup3[:, :, 128 * m:128 * m + 128]), writes=[ws.b], kind="d")
                K.op("sp", lambda e, ws=ws, m=m: e.dma_start(out=ws.t[:, :, 128:256], in_=wup3[:, :, 2816 + 128 * m:2816 + 128 * m + 128]), writes=[ws.b], kind="d")
                for kc in range(8):
                    if kc % 2 == 0:
                        K.op("dve", lambda e, ws=ws, wb=wb, kc=kc: e.tensor_scalar(out=wb.t[:, kc, :], in0=ws.t[:, kc, :], scalar1=q6(G2 + kc), scalar2=None, op0=ALU.mult),
                             reads=[ws.b, p6.b], writes=[wb.b])
                    else:
                        K.op("act", lambda e, ws=ws, wb=wb, kc=kc: e.activation(out=wb.t[:, kc, :], in_=ws.t[:, kc, :], func=AF.Copy, scale=q6(G2 + kc)),
                             reads=[ws.b, p6.b], writes=[wb.b])
                wst_ = wdst[0]
                K.op("sp", lambda e, wst_=wst_, m=m: e.dma_start(out=wst_.t[:], in_=w_dn_d[128 * m:128 * m + 128, :]), writes=[wst_.b], kind="d")
                K.op("act", lambda e, wst_=wst_, m=m: e.copy(out=wd.t[:, m, :], in_=wst_.t[:]), reads=[wst_.b], writes=[bw_d[m]])
                pg = psg[pi % 2]
                for kc in range(8):
                    K.op("pe", lambda e, pg=pg, wb=wb, kc=kc: e.matmul(out=pg.t[:, 0:2], lhsT=wb.t[:, kc, 0:128], rhs=h2Th.t[:, kc, :], start=(kc == 0), stop=(kc == 7)),
                         reads=[wb.b, h2Th.b], writes=[pg.b])
                gb = gbs[gbi % 2]
                K.op("act", lambda e, pg=pg, gb=gb: e.copy(out=gb.t[:, 0:2], in_=pg.t[:, 0:2]), reads=[pg.b], writes=[gb.b])
                pi += 1
                for kb in range(4):
                    pg = psg[pi % 2]
                    pvv = psv[pi % 2]
                    c_ = c1[pi % 3]
                    pi += 1
                    gb = gbs[gbi % 2]
                    gbn = gbs[(gbi + 1) % 2]
                    gbi += 1
                    hsl = slice(512 * kb, 512 * kb + 512)
                    for kc in range(8):
                        K.op("pe", lambda e, pg=pg, wb=wb, kc=kc, hsl=hsl: e.matmul(out=pg.t[:], lhsT=wb.t[:, kc, 0:128], rhs=h2T.t[:, kc, hsl], start=(kc == 0), stop=(kc == 7)),
                             reads=[wb.b, h2T.b], writes=[pg.b])
                    for kc in range(8):
                        K.op("pe", lambda e, pvv=pvv, wb=wb, kc=kc, hsl=hsl: e.matmul(out=pvv.t[:], lhsT=wb.t[:, kc, 128:256], rhs=h2T.t[:, kc, hsl], start=(kc == 0), stop=(kc == 7)),
                             reads=[wb.b, h2T.b], writes=[pvv.b])
                    K.op("act", lambda e, pg=pg, gb=gb: e.copy(out=gb.t[:, 2:514], in_=pg.t[:]), reads=[pg.b], writes=[gb.b])
                    K.op("act", lambda e, c_=c_, m=m, pg=pg: e.activation(out=c_.t[:], in_=pg.t[:], func=AF.Identity, scale=q6(CW + 3 * m + 2), bias=q6(CB + m)),
                         reads=[pg.b, p6.b], writes=[c_.b])
                    K.op("dve", lambda e, c_=c_, m=m, gb=gb: e.scalar_tensor_tensor(out=c_.t[:], in0=gb.t[:, 1:513], scalar=q6(CW + 3 * m + 1), in1=c_.t[:], op0=ALU.mult, op1=ALU.add),
                         reads=[gb.b, p6.b, c_.b], writes=[c_.b])
                    K.op("dve", lambda e, c_=c_, m=m, gb=gb: e.scalar_tensor_tensor(out=c_.t[:], in0=gb.t[:, 0:512], scalar=q6(CW + 3 * m + 0), in1=c_.t[:], op0=ALU.mult, op1=ALU.add),
                         reads=[gb.b, p6.b, c_.b], writes=[c_.b])
                    if kb < 3:
                        K.op("pool", lambda e, gb=gb, gbn=gbn: e.tensor_copy(out=gbn.t[:, 0:2], in_=gb.t[:, 512:514]), reads=[gb.b], writes=[gbn.b])
                    K.op("act", lambda e, c_=c_: e.activation(out=c_.t[:], in_=c_.t[:], func=AF.Silu), reads=[c_.b], writes=[c_.b])
                    K.op("dve", lambda e, c_=c_, pvv=pvv, m=m, hsl=hsl: e.tensor_tensor(out=actT.t[:, m, hsl], in0=pvv.t[:], in1=c_.t[:], op=ALU.mult), reads=[c_.b, pvv.b], writes=[actT.b])
            class VW:
                def __init__(self, ap, b):
                    self.t = ap
                    self.b = b
            fg = wdst[0]
            K.op("sp", lambda e: e.dma_start(out=fg.t[:], in_=fg_d), writes=[fg.b], kind="d")
            xr = [VW(wust[0].t[:].rearrange("p a b -> p (a b)")[:, 1024 * i:1024 * i + 1024], wust[0].b if i == 0 else Buf()) for i in range(2)]
            x2 = xr
            junk7 = VW(wub[0].t[:].rearrange("p a b -> p (a b)")[:, 0:1024], wub[0].b)
            ssq7 = sb("ssq7", [128, 16], F32)
            r7 = sb("r7", [128, 16], F32)
            for t16 in range(16):
                xr_ = xr[t16 % 2]
                x2_ = x2[t16 % 2]
                K.op("sp", lambda e, xr_=xr_, t16=t16: e.dma_start(out=xr_.t[:], in_=x1_d[128 * t16:128 * t16 + 128, :]), writes=[xr_.b], kind="d")
                for half in range(2):
                    for m in range(22):
                        K.op("pe", lambda e, half=half, m=m, t16=t16: e.matmul(out=psA.t[:, 512 * half:512 * half + 512], lhsT=actT.t[:, m, 128 * t16:128 * t16 + 128],
                                                                             rhs=wd.t[:, m, 512 * half:512 * half + 512], start=(m == 0), stop=(m == 21)),
                             reads=[actT.b, bw_d[m]], writes=[psA.b])
                K.op("dve", lambda e, x2_=x2_, xr_=xr_: e.tensor_tensor(out=x2_.t[:], in0=psA.t[:], in1=xr_.t[:], op=ALU.add), reads=[psA.b, xr_.b], writes=[x2_.b])
                K.op("act", lambda e, x2_=x2_, t16=t16: e.activation(out=junk7.t[:], in_=x2_.t[:], func=AF.Square, accum_out=ssq7.t[:, t16:t16 + 1]),
                     reads=[x2_.b], writes=[junk7.b, ssq7.b])
                K.op("act", lambda e, t16=t16: e.activation(out=r7.t[:, t16:t16 + 1], in_=ssq7.t[:, t16:t16 + 1], func=AF.Sqrt, scale=1.0 / D, bias=q6(EP)), reads=[ssq7.b, p6.b], writes=[r7.b])
                K.op("dve", lambda e, t16=t16: e.reciprocal(out=r7.t[:, t16:t16 + 1], in_=r7.t[:, t16:t16 + 1]), reads=[r7.b], writes=[r7.b])
                K.op("dve", lambda e, x2_=x2_, t16=t16: e.scalar_tensor_tensor(out=x2_.t[:], in0=x2_.t[:], scalar=r7.t[:, t16:t16 + 1], in1=fg.t[:], op0=ALU.mult, op1=ALU.mult),
                     reads=[x2_.b, r7.b, fg.b], writes=[x2_.b])
                K.op("sp", lambda e, x2_=x2_, t16=t16: e.dma_start(out=out_d[128 * t16:128 * t16 + 128, :], in_=x2_.t[:]), reads=[x2_.b], kind="d")


        if P1:
            ph = ExitStack()
            cur[0] = ph
            with ph:
                phase1()
                print("phase1 sbuf bytes remaining", nc.sbuf_bytes_remaining)
                K.flush(include_cc=False)
            cur[0] = st
        if P2:
            ph = ExitStack()
            cur[0] = ph
            with ph:
                phase2()
                print("phase2 sbuf bytes remaining", nc.sbuf_bytes_remaining)
                K.flush()
            cur[0] = st
        if P6:
            G2, FL, CB, CW, EP = 0, 8, 12, 34, 100
            ph0 = ExitStack()
            cur[0] = ph0
            with ph0:
                p6 = sb("p6", [128, 128], F32)
                K.op("sp", lambda e: e.dma_start(out=p6.t[:], in_=p6_d), writes=[p6.b], kind="d")

                def q6(col, n=128):
                    return p6.t[0:n, col:col + 1]
                h2T = sb("h2T", [128, 8, 2048], BF16)
                h2Th = sb("h2Th", [128, 8, 2], BF16)
                ph = ExitStack()
                cur[0] = ph
                with ph:
                    phase6a(h2T, h2Th)
                    print("phase6a sbuf bytes remaining", nc.sbuf_bytes_remaining)
                    K.flush()
                ph = ExitStack()
                cur[0] = ph
                with ph:
                    phase6b(h2T, h2Th)
                    print("phase6b sbuf bytes remaining", nc.sbuf_bytes_remaining)
                    K.flush()
            cur[0] = st
        K.final_wait()
    return nc


def _core_inputs_p1(inp, c):
    b, g = c // 4, c % 4
    l = 0
    w = inp['w_in'][l]
    sm = inp['rwkv_shift_mix'][l]
    A0 = 1696
    r128 = np.arange(128*g, 128*g+128)
    cols = np.concatenate([r128, 512+r128, 1024+r128, np.arange(1536,1600), np.arange(1600,1696), A0+r128, A0+512+r128, A0+1024+r128])
    w_c = np.ascontiguousarray(w[:, cols])
    prm = np.zeros((128, 32), np.float32)
    prm[:, 0:8] = inp['mix_norm_gain'][l].reshape(8,128).T
    MO = [0,128,256,384,448]; MW=[128,128,128,64,96]
    for m in range(5):
        prm[:MW[m], 8+m] = sm[cols[MO[m]:MO[m]+MW[m]]]
    ch = slice(128*g, 128*g+128)
    prm[:, 13] = inp['w0'][l][ch]; prm[:, 14] = inp['a0'][l][ch]; prm[:,15] = inp['k_k'][l][ch]; prm[:,16]=inp['k_a'][l][ch]
    prm[:, 17] = inp['r_k'][l].reshape(-1)[ch]; prm[:,18]=inp['ln_x_w'][l][ch]; prm[:,19]=inp['ln_x_b'][l][ch]
    prm[:, 20] = inp['attn_norm_gain'][l][ch]
    invf = (500000.0 ** (-np.arange(8, dtype=np.float32) * 2.0 / 16)).astype(np.float32)
    for h in range(2):
        for cc in range(16):
            prm[64*h+cc, 22] = invf[cc % 8] / np.float32(2*np.pi)
    prm[:, 24] = 1e-6; prm[:, 25] = 1e-24; prm[:, 26] = 64e-5
    cbf = np.zeros((128,3,128), np.float32)
    cbf[:,0,:] = np.eye(128); cbf[:,2,:] = np.eye(128)
    for h in range(2):
        for cc in range(8):
            cbf[64*h+cc+8, 1, 64*h+cc] = -1.0
            cbf[64*h+cc, 1, 64*h+cc+8] = 1.0
    cf = np.zeros((128, 1408), np.float32)
    cf[:, 0:512] = np.arange(512, dtype=np.float32)[None, :]
    bo = np.zeros((128,128), np.float32); bo[:64,:64] = 1; bo[64:,64:] = 1
    cf[:, 512:640] = bo
    ii = np.arange(128)
    SU = (ii[:,None] < ii[None,:]).astype(np.float32); SL = (ii[:,None] > ii[None,:]).astype(np.float32); UI = (ii[:,None] <= ii[None,:]).astype(np.float32)
    cf[:, 640:1152] = np.concatenate([SU, SL, SU, SL], axis=1)
    cf[:, 1152:1408] = np.concatenate([UI, SU], axis=1)
    lora = np.zeros((128,3,128), np.float32)
    lora[0:32, 0, :] = inp['w_lora_up'][l][:, ch]
    lora[32:64, 1, :] = inp['a_lora_up'][l][:, ch]
    lora[0:96, 2, :] = inp['g_lora_up'][l][:, ch]
    return dict(xb=np.ascontiguousarray(inp['x'][b]), w_in=w_c, prm=prm, cbf=cbf.astype(ml_dtypes.bfloat16), cf=cf, lora=lora), cols


def _core_inputs(inp, c):
    m, cols = _core_inputs_p1(inp, c)
    b, g = c // 4, c % 4
    ii = np.arange(128)
    UI = (ii[:,None] <= ii[None,:]).astype(np.float32); LI = (ii[:,None] >= ii[None,:]).astype(np.float32)
    m['amask'] = np.concatenate([UI, UI, LI, LI], axis=1).astype(ml_dtypes.bfloat16)
    l = 0
    x = inp['x'][b]
    xq = np.zeros((2050, 1024), np.float32)
    lo = 2048*g - 2
    if g == 0:
        xq[2:] = x[0:2048]
    else:
        xq[:] = x[lo:lo+2050]
    m['xq'] = xq
    m['w_out'] = np.ascontiguousarray(inp['w_out'][l])
    m['w_up'] = np.ascontiguousarray(inp['w_ffn_up'][l])
    m['w_dn'] = np.ascontiguousarray(inp['w_ffn_down'][l])
    p6 = np.zeros((128,128), np.float32)
    p6[:, 0:8] = inp['ffn_norm_gain'][l].reshape(8,128).T
    p6[:, 8+g] = 1.0
    p6[:, 12:34] = inp['ffn_conv_b'][l].reshape(22,128).T
    cw = inp['ffn_conv_w'][l]
    for k in range(3):
        p6[:, 34+k:34+66:3] = cw[k].reshape(22,128).T
    p6[:, 100] = 1e-6
    m['p6'] = p6
    m['fgain'] = np.ascontiguousarray(np.broadcast_to(inp['final_norm_gain'][None,:], (128,1024))).astype(np.float32)
    return m, cols


def kernel(**inputs):
    inp = {k: np.asarray(v) for k, v in inputs.items()}
    nc = build(phases=(1, 2, 6))
    maps = [_core_inputs(inp, c)[0] for c in range(8)]
    res = run_bass_kernel_spmd(nc, maps, core_ids=list(range(8)))
    out = np.zeros((2, S, D), np.float32)
    for c in range(8):
        b, g = c // 4, c % 4
        out[b, 2048 * g:2048 * g + 2048] = np.asarray(res.results[c]["out"], dtype=np.float32)
    return out
```

```python
from contextlib import ExitStack
import concourse.bass as bass
import concourse.mybir as mybir

F32 = mybir.dt.float32
BF16 = mybir.dt.bfloat16
AF = mybir.ActivationFunctionType
ALU = mybir.AluOpType

ENGINES = ["pe", "act", "dve", "pool", "sp"]
SEM_LIMIT = 30000
DMA_SLOTS = {"sp": 12, "pool": 6, "act": 4}


class Buf:
    __slots__ = ("name", "w", "rd")

    def __init__(self, name=""):
        self.name = name
        self.w = None
        self.rd = []


class Op:
    __slots__ = ("idx", "eng", "pos", "fn", "deps", "kind", "sem", "consumed", "prev")

    def __init__(self):
        self.consumed = False
        self.sem = None
        self.prev = None


class Sched:
    def __init__(self, nc, stack):
        self.nc = nc
        self.ops = []
        self.eng_ops = {e: [] for e in ENGINES}
        self.emitted = {e: 0 for e in ENGINES}
        self.eng_sems = {}
        self.eng_cnt = {}
        self.dma_sems = {}
        self.dma_cnt = {}
        for e in ["pe", "act", "dve", "pool"]:
            self.eng_sems[e] = [stack.enter_context(nc.semaphore(f"s_{e}{i}")) for i in range(4)]
            self.eng_cnt[e] = [0, 0]
        for e, n in DMA_SLOTS.items():
            self.dma_sems[e] = [stack.enter_context(nc.semaphore(f"d_{e}{i}")) for i in range(n)]
            self.dma_cnt[e] = 0
        self.cc_sem = stack.enter_context(nc.semaphore("cc_sem"))
        self.cc_cnt = 0
        self.pending_barrier = {}
        self.waited = {e: {} for e in ENGINES}
        self.dma_since_barrier = []
        self.cc_pending = []
        self.boundary = 0

    def op(self, eng, fn, reads=(), writes=(), kind="c", extra=()):
        o = Op()
        o.idx = len(self.ops)
        o.eng = eng
        o.fn = fn
        o.kind = kind
        o.pos = len(self.eng_ops[eng])
        deps = set()
        for b in reads:
            if b.w is not None:
                deps.add(b.w)
        for b in writes:
            if b.w is not None:
                deps.add(b.w)
            deps.update(b.rd)
        deps = set(d for d in deps if d >= self.boundary)
        for b in reads:
            b.rd.append(o.idx)
        for b in writes:
            b.w = o.idx
            b.rd = []
        if eng in self.pending_barrier:
            deps.update(self.pending_barrier.pop(eng))
        keep = []
        for d in deps:
            p = self.ops[d]
            if p.eng == eng and p.kind == "c" and kind == "c":
                if eng == "pe":
                    continue
                if o.pos - p.pos > 3:
                    continue
            keep.append(d)
        keep.extend(extra)
        o.deps = keep
        self.ops.append(o)
        self.eng_ops[eng].append(o)
        if kind == "d":
            self.dma_since_barrier.append(o.idx)
        elif kind == "cc":
            self.cc_pending.append(o.idx)
        return o

    def barrier(self, include_cc=True):
        last = set()
        if include_cc:
            last.update(self.cc_pending)
            self.cc_pending = []
        for e in ENGINES:
            if self.eng_ops[e]:
                last.add(self.eng_ops[e][-1].idx)
        last.update(self.dma_since_barrier)
        self.dma_since_barrier = []
        for e in ENGINES:
            self.pending_barrier.setdefault(e, set()).update(last)

    def flush(self, include_cc=True):
        nc = self.nc
        self.barrier(include_cc)
        new_ops = {e: self.eng_ops[e][self.emitted[e]:] for e in ENGINES}
        for e in ENGINES:
            for o in new_ops[e]:
                for d in o.deps:
                    self.ops[d].consumed = True
        for e in ENGINES:
            for o in self.eng_ops[e][-1:]:
                o.consumed = True
        for e in ENGINES:
            for o in new_ops[e]:
                if o.kind == "d":
                    i = self.dma_cnt[e]
                    self.dma_cnt[e] += 1
                    K = len(self.dma_sems[e])
                    s = self.dma_sems[e][i % K]
                    v = 16 * (i // K + 1)
                    o.sem = (s, v)
                    o.prev = (s, v - 16) if v > 16 else None
                    o.consumed = True
                elif o.kind == "cc":
                    self.cc_cnt += 1
                    o.sem = (self.cc_sem, self.cc_cnt)
                    o.consumed = True
                elif o.consumed:
                    st = self.eng_cnt[e]
                    if st[1] >= SEM_LIMIT:
                        st[0] += 1
                        st[1] = 0
                    st[1] += 1
                    o.sem = (self.eng_sems[e][st[0]], st[1])

        def emit_engine(ename, eng):
            waited = self.waited[ename]
            for o in new_ops[ename]:
                need = {}
                for d in o.deps:
                    s, v = self.ops[d].sem
                    if need.get(s, 0) < v:
                        need[s] = v
                if o.prev is not None:
                    s, v = o.prev
                    if need.get(s, 0) < v:
                        need[s] = v
                for s, v in need.items():
                    if waited.get(s, 0) < v:
                        eng.wait_ge(s, v)
                        waited[s] = v
                ins = o.fn(eng)
                if o.sem is not None and o.consumed:
                    if o.kind == "d":
                        ins.then_inc(o.sem[0], 16)
                    else:
                        ins.then_inc(o.sem[0], 1)

        with nc.Block() as block:
            @block.tensor
            def _(eng):
                emit_engine("pe", eng)

            @block.scalar
            def _(eng):
                emit_engine("act", eng)

            @block.vector
            def _(eng):
                emit_engine("dve", eng)

            @block.gpsimd
            def _(eng):
                emit_engine("pool", eng)

            @block.sync
            def _(eng):
                emit_engine("sp", eng)

        for e in ENGINES:
            self.emitted[e] = len(self.eng_ops[e])
        self.boundary = len(self.ops)

    def final_wait(self):
        nc = self.nc
        need = {}
        for d in self.pending_barrier.get("sp", set()):
            s, v = self.ops[d].sem
            if need.get(s, 0) < v:
                need[s] = v

        with nc.Block() as block:
            @block.sync
            def _(eng):
                for s, v in need.items():
                    eng.wait_ge(s, v)


import numpy as np
import ml_dtypes
from contextlib import ExitStack
import concourse.bass as bass
import concourse.mybir as mybir
from concourse.bass_utils import run_bass_kernel_spmd

I32 = mybir.dt.int32
AX = mybir.AxisListType
S = 8192
D = 1024
NB = 16
MW = [128, 128, 128, 64, 96, 128, 128, 128]
MOFF = [0, 128, 256, 384, 448, 544, 672, 800]
NCOL = 928
NPRM = 32
PI = float(np.pi)
C0 = float(np.exp(-0.5))
GN_EPS = 64e-5
P_GAIN, P_MIX, P_W0, P_A0, P_KK, P_KA, P_RK, P_LNW, P_LNB, P_AG = 0, 8, 13, 14, 15, 16, 17, 18, 19, 20
P_F2PI, P_EPS, P_TINY, P_GNEPS, P_OMKA = 22, 24, 25, 26, 27


class TB:
    def __init__(self, t):
        self.t = t
        self.b = Buf()


def v3(ap):
    return ap.rearrange("p (c t) -> p c t", c=4)


def build(phases=(1, 2, 6), nblocks=NB, debug=False, stage=9, do_rwkv=True):
    nc = bass.Bass("TRN2", target_bir_lowering=False)
    P1, P2, P6 = (1 in phases), (2 in phases), (6 in phases)
    xb = nc.dram_tensor("xb", [S, D], F32, kind="ExternalInput").ap()
    w_in = nc.dram_tensor("w_in", [D, NCOL], F32, kind="ExternalInput").ap()
    prm_d = nc.dram_tensor("prm", [128, NPRM], F32, kind="ExternalInput").ap()
    cbf_d = nc.dram_tensor("cbf", [128, 3, 128], BF16, kind="ExternalInput").ap()
    cf_d = nc.dram_tensor("cf", [128, 1408], F32, kind="ExternalInput").ap()
    lora_d = nc.dram_tensor("lora", [128, 3, 128], F32, kind="ExternalInput").ap()
    amask_d = nc.dram_tensor("amask", [128, 512], BF16, kind="ExternalInput").ap()
    qkv_kind = "Internal" if P1 else "ExternalInput"
    if debug and P1:
        qkv_kind = "ExternalOutput"
    dbg_q = nc.dram_tensor("dbg_q", [3, 128, S], BF16, kind=qkv_kind).ap()
    GW = S + 2
    g_in_kind = "Internal" if not debug else "ExternalOutput"
    QW = 2050
    gin_rq = [nc.dram_tensor(f"gin_r{q}", [128, QW], BF16, kind=g_in_kind if P1 else "Internal").ap() for q in range(4)]
    gin_aq = [nc.dram_tensor(f"gin_a{q}", [128, QW], BF16, kind=g_in_kind if P2 else "Internal").ap() for q in range(4)]
    gout_kind = "Internal" if (P1 and P2) else "ExternalInput"
    gout_rq = [nc.dram_tensor(f"gout_r{q}", [512, QW], BF16, kind=gout_kind).ap() for q in range(4)]
    gout_aq = [nc.dram_tensor(f"gout_a{q}", [512, QW], BF16, kind=gout_kind).ap() for q in range(4)]
    O_d = nc.dram_tensor("O_d", [3, S, 130], F32, kind="Internal").ap()
    if P6:
        xq_d = nc.dram_tensor("xq", [2050, D], F32, kind="ExternalInput").ap()
        w_out_d = nc.dram_tensor("w_out", [D, D], F32, kind="ExternalInput").ap()
        w_up_d = nc.dram_tensor("w_up", [D, 5632], F32, kind="ExternalInput").ap()
        w_dn_d = nc.dram_tensor("w_dn", [2816, D], F32, kind="ExternalInput").ap()
        p6_d = nc.dram_tensor("p6", [128, 128], F32, kind="ExternalInput").ap()
        fg_d = nc.dram_tensor("fgain", [128, D], F32, kind="ExternalInput").ap()
        out_d = nc.dram_tensor("out", [2048, D], F32, kind="ExternalOutput").ap()
        x1_d = nc.dram_tensor("x1_d", [2048, D], F32, kind="Internal").ap()
    if False:
        dbg_p = nc.dram_tensor("dbg_p", [8, 128, S], F32, kind="ExternalOutput").ap()

    st = ExitStack()
    with st:
        K = Sched(nc, st)

        cur = [st]

        def sb(name, shape, dt):
            return TB(cur[0].enter_context(nc.sbuf_tensor("s_" + name, shape, dt)))

        def ps(name, shape, dt):
            return TB(cur[0].enter_context(nc.psum_tensor("p_" + name, shape, dt)))

        prm = sb("prm", [128, NPRM], F32)
        cbf = sb("cbf", [128, 3, 128], BF16)
        cf = sb("cf", [128, 1408], F32)
        lora = sb("lora", [128, 3, 128], F32)
        glu_bf = sb("glu_bf", [128, 128], BF16)
        K.op("sp", lambda e: e.dma_start(out=prm.t[:], in_=prm_d), writes=[prm.b], kind="d")
        K.op("sp", lambda e: e.dma_start(out=cbf.t[:], in_=cbf_d), writes=[cbf.b], kind="d")
        K.op("sp", lambda e: e.dma_start(out=cf.t[:], in_=cf_d), writes=[cf.b], kind="d")
        K.op("sp", lambda e: e.dma_start(out=lora.t[:], in_=lora_d), writes=[lora.b], kind="d")
        K.op("pool", lambda e: e.tensor_copy(out=glu_bf.t[0:96, :], in_=lora.t[0:96, 2, :]), reads=[lora.b], writes=[glu_bf.b])
        K.op("pool", lambda e: e.tensor_scalar(out=prm.t[:, P_OMKA:P_OMKA + 1], in0=prm.t[:, P_KA:P_KA + 1], scalar1=-1.0, scalar2=1.0,
                                               op0=ALU.mult, op1=ALU.add), reads=[prm.b], writes=[prm.b])
        ident = cbf.t[:, 0, :]
        prot = cbf.t[:, 1, :]
        ident2 = cbf.t[:, 0:3:2, :]
        iota = cf.t[:, 0:512]
        blockones = cf.t[:, 512:640]
        maskXY = cf.t[:, 640:1152].rearrange("p (h x) -> p h x", h=2)
        mask3 = cf.t[:, 1152:1408]
        ones128 = None

        def pc(col, n=128, p0=0):
            return prm.t[p0:p0 + n, col:col + 1]

        def phase1():
            w_bf = sb("w_bf", [128, 8, NCOL], BF16)
            wst = [sb(f"wst{i}", [128, NCOL], F32) for i in range(2)]
            for kc in range(8):
                i = kc % 2
                K.op("sp", lambda e, kc=kc, i=i: e.dma_start(out=wst[i].t[:], in_=w_in[kc * 128:(kc + 1) * 128, :]),
                     writes=[wst[i].b], kind="d")
                if kc % 2 == 0:
                    K.op("dve", lambda e, kc=kc, i=i: e.tensor_scalar(out=w_bf.t[:, kc, :], in0=wst[i].t[:], scalar1=pc(P_GAIN + kc),
                                                                     scalar2=None, op0=ALU.mult),
                         reads=[wst[i].b, prm.b], writes=[w_bf.b])
                else:
                    K.op("act", lambda e, kc=kc, i=i: e.activation(out=w_bf.t[:, kc, :], in_=wst[i].t[:], func=AF.Copy, scale=pc(P_GAIN + kc)),
                         reads=[wst[i].b, prm.b], writes=[w_bf.b])

            NXT = 6
            xts = [sb(f"xt{i}", [128, D], F32) for i in range(NXT)]
            junk = sb("junk", [128, D], BF16)
            ssq = sb("ssq", [128, 64], F32)
            rstd = sb("rstd", [128, 64], F32)
            b_ss = [Buf() for _ in range(NB)]
            xns = [sb(f"xn{i}", [128, D], BF16) for i in range(2)]
            hTs = [sb(f"hT{i}", [128, 8, 512], BF16) for i in range(2)]
            tpbs = [ps(f"tpb{i}", [128, 8, 128], BF16) for i in range(2)]
            pjs = [ps(f"pj{i}", [128, 512], F32) for i in range(2)]
            bank3 = ps("bank3", [128, 512], F32)
            AA = ps("AA", [128, 512], F32)
            psxy = ps("psxy", [128, 2, 256], F32)
            bank7 = ps("bank7", [128, 512], F32)
            b_at = b_pp = bank3.b
            b_pz = bank7.b
            b_tok, b_ynT = tpbs[0].b, tpbs[1].b
            b_pu = b_psn = b_py = bank7.b
            ps_at = bank3.t[:, 0:256].rearrange("p (h x) -> p h x", h=2)
            ps_p = bank3.t[:, 256:512].rearrange("p (h x) -> p h x", h=2)
            ps_z = bank7.t[:, 384:512].rearrange("p (h x) -> p h x", h=2)
            ps_tok = tpbs[0].t[:, 0:5, :]
            ps_ynT = tpbs[1].t[:, 0:4, :].rearrange("p k x -> p (k x)")
            ps_u = bank7.t[:, 0:128].rearrange("p (h x) -> p h x", h=2)
            ps_s = bank7.t[:, 128:256].rearrange("p (h x) -> p h x", h=2)
            ps_y = bank7.t[:, 256:384]
            ostg = [sb(f"ostg{i}", [128, 512], BF16) for i in range(4)]
            ostg_i = [0]

            def next_ostg():
                o = ostg[ostg_i[0] % 4]
                ostg_i[0] += 1
                return o
            pbuf = [sb(f"pbuf{m}", [128, 513], F32) for m in range(5)]
            pmix = [sb(f"pmix{m}", [128, 512], F32) for m in range(5)]
            qb = sb("qb", [128, 512], BF16)
            ang = sb("ang", [128, 512], F32)
            angi = sb("angi", [128, 512], I32)
            ctab = sb("ctab", [128, 512], F32)
            stab = sb("stab", [128, 512], F32)
            t1, t2 = wst[0], wst[1]
            for m in range(5):
                K.op("pool", lambda e, m=m: e.memset(pbuf[m].t[:, 0:1], 0.0), writes=[pbuf[m].b])

            if do_rwkv:
                ones = sb("ones", [128, 128], F32)
                K.op("pool", lambda e: e.memset(ones.t[:], 1.0), writes=[ones.b])
                tw = sb("tw", [32, 512], F32)
                sg = sb("sg", [128, 512], F32)
                aa = sb("aa", [128, 512], F32)
                sgd = sb("sgd", [96, 512], BF16)
                gTS = [sb(f"gT{i}", [128, 512], BF16) for i in range(2)]
                ELcS = [sb(f"ELc{i}", [128, 4], F32) for i in range(2)]
                Lbuf = sb("Lbuf", [128, 4, 129], F32)
                K.op("pool", lambda e: e.memset(Lbuf.t[:], 0.0), writes=[Lbuf.b])
                EL = sb("EL", [128, 512], F32)
                ELn = sb("ELn", [128, 512], F32)
                ELx = sb("ELx", [128, 512], F32)
                kkr = sb("kkr", [128, 512], F32)
                sq = sb("sq", [128, 512], F32)
                rn = sb("rn", [128, 512], F32)
                kk = sb("kk", [128, 512], F32)
                k2 = sb("k2", [128, 512], F32)
                kka = sb("kka", [128, 512], F32)
                bonusTS = [sb(f"bonusT{i}", [128, 512], F32) for i in range(2)]
                ARzS = [[sb(f"ARz{i}_{h}", [128, 4, 256], BF16) for h in range(2)] for i in range(2)]
                for i in range(2):
                    for h in range(2):
                        K.op("pool", lambda e, h=h, i=i: e.memset(ARzS[i][h].t[:], 0.0), writes=[ARzS[i][h].b])
                KTS = [sb(f"KT{i}", [128, 512], BF16) for i in range(2)]
                BTS = [sb(f"BT{i}", [128, 512], BF16) for i in range(2)]
                VBS = [sb(f"VB{i}", [128, 512], BF16) for i in range(2)]
                TOK = [sb(f"TOK{c}", [128, 5, 128], BF16) for c in range(4)]
                M3 = [[sb(f"M3_{c}_{h}", [128, 384], BF16) for h in range(2)] for c in range(4)]
                XY = [[sb(f"XY{c}_{i}", [128, 2, 256], BF16) for i in range(2)] for c in range(4)]
                PP = [[sb(f"PP{c}_{i}", [128, 2, 128], BF16) for i in range(2)] for c in range(4)]
                ATz = [[sb(f"ATz{c}_{h}", [128, 128], BF16) for h in range(2)] for c in range(4)]
                for c in range(4):
                    for h in range(2):
                        K.op("pool", lambda e, c=c, h=h: e.memset(ATz[c][h].t[:], 0.0), writes=[ATz[c][h].b])
                Zs = [sb(f"Zs{c}", [128, 2, 64], BF16) for c in range(4)]
                Us = [sb(f"Us{c}", [128, 2, 64], BF16) for c in range(4)]
                Sbf = [sb(f"Sbf{i}", [128, 64], BF16) for i in range(2)]
                K.op("pool", lambda e: e.memset(Sbf[0].t[:], 0.0), writes=[Sbf[0].b])
                K.op("pool", lambda e: e.memset(Sbf[1].t[:], 0.0), writes=[Sbf[1].b])
                ys = sb("ys", [128, 4, 128], F32)
                bst = sb("bst", [128, 8, 6], F32)
                mv = sb("mv", [128, 8, 2], F32)
                grs = sb("grs", [128, 8], F32)
                yn = sb("yn", [128, 4, 128], BF16)
                yt = sb("yt", [128, 512], F32)
                s_idx = [0]

            TWO_PI_S = 2 * PI * (1 - 1e-6)
            Cb = sb("Cb", [128, 512], F32)
            Sb = sb("Sb", [128, 512], F32)
            cjs = sb("cjs", [128, 16], F32)
            sjs = sb("sjs", [128, 16], F32)
            nsj = sb("nsj", [128, 16], F32)

            def sincos(n, u_fn, s_out, c_out):
                K.op("dve", lambda e: u_fn(e, ang.t[:, 0:n]), reads=[cf.b, prm.b], writes=[ang.b])
                for add, dst in ((0.0, s_out), (0.25, c_out)):
                    if add:
                        K.op("dve", lambda e: e.tensor_scalar(out=ang.t[:, 0:n], in0=ang.t[:, 0:n], scalar1=0.25, scalar2=None, op0=ALU.add), reads=[ang.b], writes=[ang.b])
                    K.op("dve", lambda e: e.tensor_copy(out=angi.t[:, 0:n], in_=ang.t[:, 0:n]), reads=[ang.b], writes=[angi.b])
                    K.op("dve", lambda e: e.tensor_copy(out=t2.t[:, 0:n], in_=angi.t[:, 0:n]), reads=[angi.b], writes=[t2.b])
                    K.op("dve", lambda e: e.tensor_tensor(out=t1.t[:, 0:n], in0=ang.t[:, 0:n], in1=t2.t[:, 0:n], op=ALU.subtract), reads=[ang.b, t2.b], writes=[t1.b])
                    K.op("act", lambda e, dst=dst: e.activation(out=dst.t[:, 0:n], in_=t1.t[:, 0:n], func=AF.Sin, scale=TWO_PI_S), reads=[t1.b], writes=[dst.b])
            sincos(512, lambda e, o: e.tensor_scalar(out=o, in0=iota, scalar1=pc(P_F2PI), scalar2=None, op0=ALU.mult), Sb, Cb)
            sincos(16, lambda e, o: e.tensor_scalar(out=o, in0=iota[:, 0:16], scalar1=512.0, scalar2=pc(P_F2PI), op0=ALU.mult, op1=ALU.mult), sjs, cjs)
            K.op("dve", lambda e: e.tensor_scalar(out=nsj.t[:], in0=sjs.t[:], scalar1=-1.0, scalar2=None, op0=ALU.mult), reads=[sjs.b], writes=[nsj.b])
            pj_rr = [0]

            def next_pj():
                p = pjs[pj_rr[0] % 2]
                pj_rr[0] += 1
                return p

            def stageA(j):
                hT = hTs[j % 2]
                for tt in range(4):
                    ti = 4 * j + tt
                    xt = xts[ti % NXT]
                    r0 = 512 * j + 128 * tt
                    K.op("sp", lambda e, xt=xt, r0=r0: e.dma_start(out=xt.t[:], in_=xb[r0:r0 + 128, :]), writes=[xt.b], kind="d")
                    K.op("act", lambda e, xt=xt, ti=ti: e.activation(out=junk.t[:], in_=xt.t[:], func=AF.Square, accum_out=ssq.t[:, ti:ti + 1]),
                         reads=[xt.b], writes=[junk.b, b_ss[j]])
                K.op("act", lambda e, j=j: e.activation(out=rstd.t[:, 4 * j:4 * j + 4], in_=ssq.t[:, 4 * j:4 * j + 4], func=AF.Sqrt, scale=1.0 / D, bias=pc(P_EPS)),
                     reads=[b_ss[j], prm.b], writes=[b_ss[j]])
                K.op("dve", lambda e, j=j: e.reciprocal(out=rstd.t[:, 4 * j:4 * j + 4], in_=rstd.t[:, 4 * j:4 * j + 4]), reads=[b_ss[j]], writes=[b_ss[j]])
                for tt in range(4):
                    ti = 4 * j + tt
                    xt = xts[ti % NXT]
                    xn = xns[ti % 2]
                    K.op("act", lambda e, xt=xt, xn=xn, ti=ti: e.activation(out=xn.t[:], in_=xt.t[:], func=AF.Copy, scale=rstd.t[:, ti:ti + 1]),
                         reads=[xt.b, b_ss[j]], writes=[xn.b])
                    tpb = tpbs[ti % 2]
                    for fc in range(8):
                        K.op("pe", lambda e, xn=xn, fc=fc, tpb=tpb: e.transpose(out=tpb.t[:, fc, :], in_=xn.t[:, fc * 128:(fc + 1) * 128], identity=ident),
                             reads=[xn.b, cbf.b], writes=[tpb.b])
                    K.op("act", lambda e, hT=hT, tt=tt, tpb=tpb: e.copy(out=hT.t[:, :, tt * 128:(tt + 1) * 128], in_=tpb.t[:]),
                         reads=[tpb.b], writes=[hT.b])
                    yield
                K.op("dve", lambda e, j=j: e.tensor_scalar(out=ctab.t[:], in0=Cb.t[:], scalar1=cjs.t[:, j:j + 1], scalar2=None, op0=ALU.mult),
                     reads=[Cb.b, cjs.b], writes=[ctab.b])
                K.op("dve", lambda e, j=j: e.scalar_tensor_tensor(out=ctab.t[:], in0=Sb.t[:], scalar=nsj.t[:, j:j + 1], in1=ctab.t[:], op0=ALU.mult, op1=ALU.add),
                     reads=[Sb.b, nsj.b, ctab.b], writes=[ctab.b])
                K.op("dve", lambda e, j=j: e.tensor_scalar(out=stab.t[:], in0=Cb.t[:], scalar1=sjs.t[:, j:j + 1], scalar2=None, op0=ALU.mult),
                     reads=[Cb.b, sjs.b], writes=[stab.b])
                K.op("dve", lambda e, j=j: e.scalar_tensor_tensor(out=stab.t[:], in0=Sb.t[:], scalar=cjs.t[:, j:j + 1], in1=stab.t[:], op0=ALU.mult, op1=ALU.add),
                     reads=[Sb.b, cjs.b, stab.b], writes=[stab.b])
                for m in range(8):
                    pj = next_pj()
                    mw = MW[m]
                    for kc in range(8):
                        K.op("pe", lambda e, pj=pj, m=m, kc=kc, mw=mw, hT=hT: e.matmul(
                            out=pj.t[0:mw, :], lhsT=w_bf.t[:, kc, MOFF[m]:MOFF[m] + mw], rhs=hT.t[:, kc, :], start=(kc == 0), stop=(kc == 7)),
                            reads=[w_bf.b, hT.b], writes=[pj.b])
                    if m < 5:
                        pb = pbuf[m]
                        K.op("act", lambda e, pb=pb, pj=pj, mw=mw: e.copy(out=pb.t[0:mw, 1:513], in_=pj.t[0:mw, :]), reads=[pj.b], writes=[pb.b])
                        K.op("dve", lambda e, pb=pb, mw=mw: e.tensor_tensor(out=t1.t[0:mw, 0:512], in0=pb.t[0:mw, 0:512], in1=pb.t[0:mw, 1:513], op=ALU.subtract),
                             reads=[pb.b], writes=[t1.b])
                        K.op("dve", lambda e, pb=pb, mw=mw, m=m: e.scalar_tensor_tensor(out=pmix[m].t[0:mw, :], in0=t1.t[0:mw, 0:512], scalar=pc(P_MIX + m, mw),
                                                                                     in1=pb.t[0:mw, 1:513], op0=ALU.mult, op1=ALU.add),
                             reads=[pb.b, t1.b, prm.b], writes=[pmix[m].b])
                        K.op("pool", lambda e, pb=pb, mw=mw: e.tensor_copy(out=pb.t[0:mw, 0:1], in_=pb.t[0:mw, 512:513]), reads=[pb.b], writes=[pb.b])
                    elif m < 7:
                        og = next_ostg()
                        K.op("act", lambda e, pj=pj: e.copy(out=qb.t[:], in_=pj.t[:]), reads=[pj.b], writes=[qb.b])
                        pr = next_pj()
                        K.op("pe", lambda e, pr=pr: e.matmul(out=pr.t[:], lhsT=prot, rhs=qb.t[:], start=True, stop=True), reads=[qb.b, cbf.b], writes=[pr.b])
                        K.op("dve", lambda e: e.tensor_tensor(out=t1.t[:, 0:512], in0=qb.t[:], in1=ctab.t[:], op=ALU.mult), reads=[qb.b, ctab.b], writes=[t1.b])
                        K.op("dve", lambda e, pr=pr: e.tensor_tensor(out=t2.t[:, 0:512], in0=pr.t[:], in1=stab.t[:], op=ALU.mult), reads=[pr.b, stab.b], writes=[t2.b])
                        K.op("pool", lambda e, og=og: e.tensor_tensor(out=og.t[:], in0=t1.t[:, 0:512], in1=t2.t[:, 0:512], op=ALU.add),
                             reads=[t1.b, t2.b], writes=[og.b])
                        K.op("sp", lambda e, og=og, j=j, m=m: e.dma_start(out=dbg_q[m - 5, :, 512 * j:512 * j + 512], in_=og.t[:]), reads=[og.b], kind="d")
                    else:
                        og = next_ostg()
                        K.op("act", lambda e, pj=pj, og=og: e.copy(out=og.t[:], in_=pj.t[:]), reads=[pj.b], writes=[og.b])
                        K.op("sp", lambda e, og=og, j=j: e.dma_start(out=dbg_q[2, :, 512 * j:512 * j + 512], in_=og.t[:]), reads=[og.b], kind="d")
                    yield
                yield
            def stageB(j, ARz, KT, BT, VB, ELc, bonusT, gT):
                rp, kp, vp, lo, gd = pmix
                K.op("act", lambda e: e.activation(out=tw.t[:], in_=lo.t[0:32, :], func=AF.Tanh), reads=[lo.b], writes=[tw.b])
                pz = next_pj()
                K.op("pe", lambda e, pz=pz: e.matmul(out=pz.t[:], lhsT=lora.t[0:32, 0, :], rhs=tw.t[:], start=True, stop=True),
                     reads=[lora.b, tw.b], writes=[pz.b])
                K.op("act", lambda e, pz=pz: e.activation(out=sg.t[:], in_=pz.t[:], func=AF.Sigmoid, bias=pc(P_W0)), reads=[pz.b, prm.b], writes=[sg.b])
                pa = next_pj()
                K.op("pe", lambda e, pa=pa: e.matmul(out=pa.t[:], lhsT=lora.t[32:64, 1, :], rhs=lo.t[32:64, :], start=True, stop=True),
                     reads=[lora.b, lo.b], writes=[pa.b])
                K.op("act", lambda e, pa=pa: e.activation(out=aa.t[:], in_=pa.t[:], func=AF.Sigmoid, bias=pc(P_A0)), reads=[pa.b, prm.b], writes=[aa.b])
                yield
                K.op("act", lambda e: e.activation(out=sgd.t[:], in_=gd.t[0:96, :], func=AF.Sigmoid), reads=[gd.b], writes=[sgd.b])
                pg = next_pj()
                K.op("pe", lambda e, pg=pg: e.matmul(out=pg.t[:], lhsT=glu_bf.t[0:96, :], rhs=sgd.t[:], start=True, stop=True),
                     reads=[glu_bf.b, sgd.b], writes=[pg.b])
                K.op("act", lambda e, pg=pg: e.copy(out=gT.t[:], in_=pg.t[:]), reads=[pg.b], writes=[gT.b])
                yield
                for c in range(4):
                    K.op("dve", lambda e, c=c: e.tensor_tensor_scan(out=Lbuf.t[:, c, 1:129], data0=ones.t[:], data1=sg.t[:, 128 * c:128 * c + 128],
                                                                    initial=0.0, op0=ALU.mult, op1=ALU.add), reads=[ones.b, sg.b], writes=[Lbuf.b])
                K.op("act", lambda e: e.activation(out=v3(EL.t[:]), in_=Lbuf.t[:, :, 1:129], func=AF.Exp, scale=-C0), reads=[Lbuf.b], writes=[EL.b])
                K.op("act", lambda e: e.activation(out=v3(ELn.t[:]), in_=Lbuf.t[:, :, 1:129], func=AF.Exp, scale=C0), reads=[Lbuf.b], writes=[ELn.b])
                K.op("act", lambda e: e.activation(out=v3(ELx.t[:]), in_=Lbuf.t[:, :, 0:128], func=AF.Exp, scale=-C0), reads=[Lbuf.b], writes=[ELx.b])
                K.op("pool", lambda e: e.tensor_copy(out=ELc.t[:], in_=EL.t[:, 127:512:128]), reads=[EL.b], writes=[ELc.b])
                yield
                K.op("dve", lambda e: e.tensor_scalar(out=kkr.t[:], in0=kp.t[:], scalar1=pc(P_KK), scalar2=None, op0=ALU.mult), reads=[kp.b, prm.b], writes=[kkr.b])
                K.op("pool", lambda e: e.tensor_tensor(out=sq.t[:], in0=kkr.t[:], in1=kkr.t[:], op=ALU.mult), reads=[kkr.b], writes=[sq.b])
                pss = next_pj()
                K.op("pe", lambda e, pss=pss: e.matmul(out=pss.t[:], lhsT=blockones, rhs=sq.t[:], start=True, stop=True), reads=[cf.b, sq.b], writes=[pss.b])
                K.op("act", lambda e, pss=pss: e.activation(out=rn.t[:], in_=pss.t[:], func=AF.Sqrt, bias=pc(P_TINY)), reads=[pss.b, prm.b], writes=[rn.b])
                K.op("dve", lambda e: e.reciprocal(out=rn.t[:], in_=rn.t[:]), reads=[rn.b], writes=[rn.b])
                K.op("pool", lambda e: e.tensor_tensor(out=kk.t[:], in0=kkr.t[:], in1=rn.t[:], op=ALU.mult), reads=[kkr.b, rn.b], writes=[kk.b])
                yield
                K.op("dve", lambda e: e.tensor_scalar(out=k2.t[:], in0=aa.t[:], scalar1=pc(P_KA), scalar2=pc(P_OMKA), op0=ALU.mult, op1=ALU.add),
                     reads=[aa.b, prm.b], writes=[k2.b])
                K.op("pool", lambda e: e.tensor_tensor(out=k2.t[:], in0=k2.t[:], in1=kp.t[:], op=ALU.mult), reads=[k2.b, kp.b], writes=[k2.b])
                yield
                K.op("dve", lambda e: e.scalar_tensor_tensor(out=sq.t[:], in0=rp.t[:], scalar=pc(P_RK), in1=k2.t[:], op0=ALU.mult, op1=ALU.mult),
                     reads=[rp.b, k2.b, prm.b, sq.b], writes=[sq.b])
                pbs = next_pj()
                K.op("pe", lambda e, pbs=pbs: e.matmul(out=pbs.t[:], lhsT=blockones, rhs=sq.t[:], start=True, stop=True), reads=[cf.b, sq.b], writes=[pbs.b])
                K.op("dve", lambda e, pbs=pbs: e.tensor_tensor(out=bonusT.t[:], in0=pbs.t[:], in1=vp.t[:], op=ALU.mult), reads=[pbs.b, vp.b], writes=[bonusT.b])
                K.op("pool", lambda e: e.tensor_tensor(out=kka.t[:], in0=kk.t[:], in1=aa.t[:], op=ALU.mult), reads=[kk.b, aa.b], writes=[kka.b])
                yield
                for h in range(2):
                    hs = slice(64 * h, 64 * h + 64)
                    K.op("dve", lambda e, h=h, hs=hs: e.tensor_tensor(out=ARz[h].t[hs, :, 128:256], in0=v3(rp.t[hs, :]), in1=v3(EL.t[hs, :]), op=ALU.mult),
                         reads=[rp.b, EL.b], writes=[ARz[h].b])
                    K.op("dve", lambda e, h=h, hs=hs: e.scalar_tensor_tensor(out=ARz[h].t[hs, :, 0:128], in0=v3(kk.t[hs, :]), scalar=-1.0, in1=v3(ELx.t[hs, :]), op0=ALU.mult, op1=ALU.mult),
                         reads=[kk.b, ELx.b], writes=[ARz[h].b])
                K.op("dve", lambda e: e.tensor_tensor(out=KT.t[:], in0=k2.t[:], in1=ELn.t[:], op=ALU.mult), reads=[k2.b, ELn.b], writes=[KT.b])
                K.op("pool", lambda e: e.tensor_tensor(out=BT.t[:], in0=kka.t[:], in1=ELn.t[:], op=ALU.mult), reads=[kka.b, ELn.b], writes=[BT.b])
                K.op("pool", lambda e: e.tensor_copy(out=VB.t[:], in_=vp.t[:]), reads=[vp.b], writes=[VB.b])
                yield

            def blockCDE(j, pump, ARz, KT, BT, VB, ELc, bonusT, gT):
                for c in range(4):
                    cs = slice(128 * c, 128 * c + 128)
                    K.op("pe", lambda e, cs=cs: e.transpose(out=ps_tok[:, 0, :], in_=KT.t[:, cs], identity=ident), reads=[KT.b, cbf.b], writes=[b_tok])
                    K.op("pe", lambda e, cs=cs: e.transpose(out=ps_tok[:, 1, :], in_=BT.t[:, cs], identity=ident), reads=[BT.b, cbf.b], writes=[b_tok])
                    for h in range(2):
                        K.op("pe", lambda e, c=c, h=h: e.transpose(out=ps_tok[:, 2 + h, :], in_=ARz[h].t[:, c, 0:128], identity=ident), reads=[ARz[h].b, cbf.b], writes=[b_tok])
                    K.op("pe", lambda e, cs=cs: e.transpose(out=ps_tok[:, 4, :], in_=VB.t[:, cs], identity=ident), reads=[VB.b, cbf.b], writes=[b_tok])
                    K.op("act", lambda e, c=c: e.copy(out=TOK[c].t[:], in_=ps_tok), reads=[b_tok], writes=[TOK[c].b])
                if stage < 2:
                    return
                for c in range(4):
                    cs = slice(128 * c, 128 * c + 128)
                    for h in range(2):
                        hs = slice(64 * h, 64 * h + 64)
                        K.op("pe", lambda e, c=c, cs=cs, hs=hs, h=h: e.matmul(out=AA.t[:, 256 * h:256 * h + 128], lhsT=BT.t[:, cs], rhs=ARz[h].t[:, c, 0:128], start=True, stop=True),
                             reads=[BT.b, ARz[h].b], writes=[AA.b])
                        K.op("pe", lambda e, c=c, cs=cs, hs=hs, h=h: e.matmul(out=AA.t[:, 256 * h + 128:256 * h + 256], lhsT=ARz[h].t[:, c, 0:128], rhs=BT.t[:, cs], start=True, stop=True),
                             reads=[BT.b, ARz[h].b], writes=[AA.b])
                    K.op("dve", lambda e, c=c: e.tensor_tensor(out=XY[c][0].t[:], in0=AA.t[:].rearrange("p (h x) -> p h x", h=2), in1=maskXY, op=ALU.mult),
                         reads=[AA.b, cf.b], writes=[XY[c][0].b])
                    if stage >= 2.2:
                        K.op("pool", lambda e, c=c: e.tensor_tensor(out=PP[c][0].t[:], in0=XY[c][0].t[:, :, 0:128], in1=ident2, op=ALU.add),
                             reads=[XY[c][0].b, cbf.b], writes=[PP[c][0].b])
                if stage < 2.5:
                    return
                for c in range(4):
                    cs = slice(128 * c, 128 * c + 128)
                    for h in range(2):
                        hs = slice(64 * h, 64 * h + 64)
                        K.op("pe", lambda e, c=c, cs=cs, h=h: e.matmul(out=AA.t[:, 0:128], lhsT=BT.t[:, cs], rhs=ARz[h].t[:, c, 128:256], start=True, stop=True),
                             reads=[BT.b, ARz[h].b], writes=[AA.b])
                        K.op("pe", lambda e, c=c, cs=cs, h=h: e.matmul(out=AA.t[:, 128:384], lhsT=KT.t[:, cs], rhs=ARz[h].t[:, c, 0:256], start=True, stop=True),
                             reads=[KT.b, ARz[h].b], writes=[AA.b])
                        K.op("dve", lambda e, c=c, h=h: e.tensor_tensor(out=M3[c][h].t[:, 0:256], in0=AA.t[:, 0:256], in1=mask3, op=ALU.mult),
                             reads=[AA.b, cf.b], writes=[M3[c][h].b])
                        K.op("dve", lambda e, c=c, h=h: e.tensor_tensor(out=M3[c][h].t[:, 256:384], in0=AA.t[:, 256:384], in1=mask3[:, 0:128], op=ALU.mult),
                             reads=[AA.b, cf.b], writes=[M3[c][h].b])
                if stage < 3:
                    return
                xy_ring = [(psxy.t[:], psxy.b), (bank7.t[:].rearrange("p (h x) -> p h x", h=2), bank7.b)]
                pp_ring = [(ps_p, bank3.b), (AA.t[:, 0:256].rearrange("p (h x) -> p h x", h=2), AA.b)]
                for kq in range(6):
                    cur, nxt = kq % 2, (kq + 1) % 2

                    def xy_part(c, kq=kq, cur=cur, nxt=nxt):
                        Xc, Xn = XY[c][cur], XY[c][nxt]
                        pxy, bxy = xy_ring[c % 2]
                        for h in range(2):
                            if kq < 5:
                                K.op("pe", lambda e, h=h: e.matmul(out=pxy[:, h, 0:128], lhsT=Xc.t[:, h, 128:256], rhs=Xc.t[:, h, 0:128], start=True, stop=True),
                                     reads=[Xc.b], writes=[bxy])
                            K.op("pe", lambda e, h=h: e.matmul(out=pxy[:, h, 128:256], lhsT=Xc.t[:, h, 0:128], rhs=Xc.t[:, h, 128:256], start=True, stop=True),
                                 reads=[Xc.b], writes=[bxy])
                        if kq < 5:
                            K.op("act", lambda e: e.copy(out=Xn.t[:], in_=pxy), reads=[bxy], writes=[Xn.b])
                        else:
                            K.op("act", lambda e: e.copy(out=Xn.t[:, :, 128:256], in_=pxy[:, :, 128:256]), reads=[bxy], writes=[Xn.b])

                    def p_part(c, kq=kq, cur=cur, nxt=nxt):
                        Xn = XY[c][nxt]
                        Pc, Pn = PP[c][cur], PP[c][nxt]
                        ppp, bpp = pp_ring[c % 2]
                        for h in range(2):
                            K.op("pe", lambda e, h=h: e.matmul(out=ppp[:, h, :], lhsT=Xn.t[:, h, 128:256], rhs=Pc.t[:, h, :], start=True, stop=True),
                                 reads=[Xn.b, Pc.b], writes=[bpp])
                        K.op("dve", lambda e: e.tensor_tensor(out=Pn.t[:], in0=ppp, in1=Pc.t[:], op=ALU.add), reads=[bpp, Pc.b], writes=[Pn.b])
                    xy_part(0)
                    for c in range(1, 4):
                        xy_part(c)
                        p_part(c - 1)
                        pump()
                    p_part(3)
                    pump()
                if stage < 4:
                    return
                for c in range(4):
                    Tm = PP[c][0]
                    for h in range(2):
                        hs = slice(64 * h, 64 * h + 64)
                        K.op("pe", lambda e, c=c, h=h, hs=hs, Tm=Tm: e.matmul(out=ps_at[:, h, :], lhsT=TOK[c].t[:, 2 + h, :], rhs=Tm.t[:, h, :], start=True, stop=True),
                             reads=[TOK[c].b, Tm.b], writes=[b_at])
                        K.op("pe", lambda e, c=c, h=h, hs=hs: e.matmul(out=ps_z[:, h, :], lhsT=M3[c][h].t[:, 128:256], rhs=TOK[c].t[:, 4, hs], start=True, stop=True),
                             reads=[M3[c][h].b, TOK[c].b], writes=[b_pz])
                    for h in range(2):
                        hs = slice(64 * h, 64 * h + 64)
                        K.op("act", lambda e, c=c, h=h, hs=hs: e.copy(out=ATz[c][h].t[hs, :], in_=ps_at[hs, h, :]), reads=[b_at], writes=[ATz[c][h].b])
                    K.op("dve", lambda e, c=c: e.tensor_copy(out=Zs[c].t[:], in_=ps_z), reads=[b_pz], writes=[Zs[c].b])
                if stage < 5:
                    return
                for c in range(4):
                    Tm = PP[c][0]
                    Sc = Sbf[s_idx[0] % 2]
                    Sn = Sbf[(s_idx[0] + 1) % 2]
                    s_idx[0] += 1
                    for h in range(2):
                        hs = slice(64 * h, 64 * h + 64)
                        K.op("pe", lambda e, c=c, h=h, Tm=Tm: e.matmul(out=ps_u[:, h, :], lhsT=Tm.t[:, h, :], rhs=Zs[c].t[:, h, :], start=True, stop=False),
                             reads=[Tm.b, Zs[c].b], writes=[b_pu])
                        K.op("pe", lambda e, c=c, h=h, hs=hs, Sc=Sc: e.matmul(out=ps_u[:, h, :], lhsT=ATz[c][h].t[:], rhs=Sc.t[:], start=False, stop=True),
                             reads=[ATz[c][h].b, Sc.b], writes=[b_pu])
                    K.op("act", lambda e, c=c: e.copy(out=Us[c].t[:], in_=ps_u), reads=[b_pu], writes=[Us[c].b])
                    pump()
                    for h in range(2):
                        hs = slice(64 * h, 64 * h + 64)
                        K.op("pe", lambda e, c=c, h=h, hs=hs: e.matmul(out=ps_s[:, h, :], lhsT=TOK[c].t[:, 0, :], rhs=TOK[c].t[:, 4, hs], start=True, stop=False),
                             reads=[TOK[c].b], writes=[b_psn])
                        K.op("pe", lambda e, c=c, h=h, hs=hs: e.matmul(out=ps_s[:, h, :], lhsT=TOK[c].t[:, 1, :], rhs=Us[c].t[:, h, :], start=False, stop=False),
                             reads=[TOK[c].b, Us[c].b], writes=[b_psn])
                        K.op("pe", lambda e, c=c, h=h, hs=hs, Sc=Sc: e.matmul(out=ps_s[:, h, :], lhsT=ident, rhs=Sc.t[:], start=False, stop=True),
                             reads=[cbf.b, Sc.b], writes=[b_psn])
                    for h in range(2):
                        hs = slice(64 * h, 64 * h + 64)
                        K.op("dve" if h == 0 else "act", (lambda e, c=c, Sn=Sn, h=h, hs=hs: e.tensor_scalar(out=Sn.t[hs, :], in0=ps_s[hs, h, :], scalar1=ELc.t[hs, c:c + 1], scalar2=None, op0=ALU.mult)) if h == 0 else
                             (lambda e, c=c, Sn=Sn, h=h, hs=hs: e.activation(out=Sn.t[hs, :], in_=ps_s[hs, h, :], func=AF.Copy, scale=ELc.t[hs, c:c + 1])),
                             reads=[b_psn, ELc.b], writes=[Sn.b])
                    for h in range(2):
                        hs = slice(64 * h, 64 * h + 64)
                        K.op("pe", lambda e, c=c, h=h, hs=hs, Sc=Sc: e.matmul(out=bank7.t[:, 256 + 64 * h:320 + 64 * h], lhsT=ARz[h].t[:, c, 128:256], rhs=Sc.t[:], start=True, stop=False),
                             reads=[ARz[h].b, Sc.b], writes=[b_py])
                        K.op("pe", lambda e, c=c, h=h: e.matmul(out=bank7.t[:, 256 + 64 * h:320 + 64 * h], lhsT=M3[c][h].t[:, 0:128], rhs=Us[c].t[:, h, :], start=False, stop=False),
                             reads=[M3[c][h].b, Us[c].b], writes=[b_py])
                        K.op("pe", lambda e, c=c, h=h, hs=hs: e.matmul(out=bank7.t[:, 256 + 64 * h:320 + 64 * h], lhsT=M3[c][h].t[:, 256:384], rhs=TOK[c].t[:, 4, hs], start=False, stop=True),
                             reads=[M3[c][h].b, TOK[c].b], writes=[b_py])
                    K.op("act", lambda e, c=c: e.copy(out=ys.t[:, c, :], in_=ps_y), reads=[b_py], writes=[ys.b])
                    pump()
                    for h in range(2):
                        K.op("dve", lambda e, c=c, h=h: e.bn_stats(out=bst.t[:, 2 * c + h, :], in_=ys.t[:, c, 64 * h:64 * h + 64]), reads=[ys.b], writes=[bst.b])
                        K.op("dve", lambda e, c=c, h=h: e.bn_aggr(out=mv.t[:, 2 * c + h, :], in_=bst.t[:, 2 * c + h, :]), reads=[bst.b], writes=[mv.b])
                if stage < 6:
                    return
                K.op("act", lambda e: e.activation(out=grs.t[:], in_=mv.t[:, :, 1], func=AF.Sqrt, bias=pc(P_GNEPS)), reads=[mv.b, prm.b], writes=[grs.b])
                K.op("dve", lambda e: e.reciprocal(out=grs.t[:], in_=grs.t[:]), reads=[grs.b], writes=[grs.b])
                for c in range(4):
                    for h in range(2):
                        i = 2 * c + h
                        K.op("dve", lambda e, c=c, h=h, i=i: e.tensor_scalar(out=yn.t[:, c, 64 * h:64 * h + 64], in0=ys.t[:, c, 64 * h:64 * h + 64],
                                                                            scalar1=mv.t[:, i, 0:1], scalar2=grs.t[:, i:i + 1], op0=ALU.subtract, op1=ALU.mult),
                             reads=[ys.b, mv.b, grs.b], writes=[yn.b])
                for c in range(4):
                    K.op("pe", lambda e, c=c: e.transpose(out=ps_ynT[:, 128 * c:128 * c + 128], in_=yn.t[:, c, :], identity=ident), reads=[yn.b, cbf.b], writes=[b_ynT])
                K.op("dve", lambda e: e.tensor_scalar(out=yt.t[:], in0=ps_ynT, scalar1=pc(P_LNW), scalar2=pc(P_LNB), op0=ALU.mult, op1=ALU.add),
                     reads=[b_ynT, prm.b], writes=[yt.b])
                K.op("pool", lambda e: e.tensor_tensor(out=yt.t[:], in0=yt.t[:], in1=bonusT.t[:], op=ALU.add), reads=[yt.b, bonusT.b], writes=[yt.b])
                pump(100)
                og = next_ostg()
                K.op("pool", lambda e, og=og: e.tensor_tensor(out=og.t[:], in0=yt.t[:], in1=gT.t[:], op=ALU.mult),
                     reads=[yt.b, gT.b], writes=[og.b])
                K.op("sp", lambda e, og=og, j=j: e.dma_start(out=gin_rq[j // 4][:, 2 + 512 * (j % 4):2 + 512 * (j % 4) + 512], in_=og.t[:]), reads=[og.b], writes=[b_ginr[j // 4]], kind="d")
                if j % 4 == 3 and j < 15:
                    K.op("sp", lambda e, og=og, j=j: e.dma_start(out=gin_rq[j // 4 + 1][:, 0:2], in_=og.t[:, 510:512]), reads=[og.b], writes=[b_ginr[j // 4 + 1]], kind="d")
                if j == 0:
                    K.op("pool", lambda e: e.memset(qb.t[:, 0:2], 0.0), writes=[qb.b])
                    K.op("sp", lambda e: e.dma_start(out=gin_rq[0][:, 0:2], in_=qb.t[:, 0:2]), reads=[qb.b], writes=[b_ginr[0]], kind="d")
                if j % 4 == 3 and P2 and P6:
                    K.op("pool", lambda e, q=j // 4: e.collective_compute("AllGather", ALU.bypass, replica_groups=[[0, 1, 2, 3], [4, 5, 6, 7]], ins=[gin_rq[q]], outs=[gout_rq[q]]),
                         reads=[b_ginr[j // 4]], kind="cc")
                if debug and False:
                    for i, tl in enumerate([sg, aa, kk, k2, EL, ys]):
                        src = tl.t[:] if tl is not ys else ys.t[:].rearrange("p c t -> p (c t)")
                        K.op("sp", lambda e, i=i, src=src, j=j: e.dma_start(out=dbg_p[i, :, 512 * j:512 * j + 512], in_=src), reads=[tl.b], kind="d")

            import itertools
            b_ginr = [Buf() for _ in range(4)]
            sets = [dict(ARz=ARzS[i], KT=KTS[i], BT=BTS[i], VB=VBS[i], ELc=ELcS[i], bonusT=bonusTS[i], gT=gTS[i]) for i in range(2)]
            for _ in stageA(0):
                pass
            for _ in stageB(0, **sets[0]):
                pass
            for j in range(nblocks):
                gens = [stageA(j + 1), stageB(j + 1, **sets[(j + 1) % 2])] if j + 1 < nblocks else []
                itr = itertools.chain(*gens)

                def pump(k=1, itr=itr):
                    for _ in range(k):
                        next(itr, None)
                blockCDE(j, pump, **sets[j % 2])
                pump(1000)

        def phase2():
            kTt = sb("kTt", [128, S], BF16)
            vTt = sb("vTt", [128, S], BF16)
            qz = sb("qz", [128, 2, S], BF16)
            amask = sb("amask", [128, 512], BF16)
            zt = sb("zt", [128, 2], BF16)
            K.op("pool", lambda e: e.memset(zt.t[:], 0.0), writes=[zt.b])
            b_gina = [Buf() for _ in range(4)]
            K.op("sp", lambda e: e.dma_start(out=gin_aq[0][:, 0:2], in_=zt.t[:]), reads=[zt.b], writes=[b_gina[0]], kind="d")
            K.op("sp", lambda e: e.dma_start(out=amask.t[:], in_=amask_d), writes=[amask.b], kind="d")
            K.op("sp", lambda e: e.dma_start(out=kTt.t[:], in_=dbg_q[1]), writes=[kTt.b], kind="d")
            K.op("pool", lambda e: e.memset(qz.t[64:128, 0, :], 0.0), writes=[qz.b])
            K.op("pool", lambda e: e.memset(qz.t[0:64, 1, :], 0.0), writes=[qz.b])
            K.op("sp", lambda e: e.dma_start(out=qz.t[0:64, 0, :], in_=dbg_q[0, 0:64, :]), writes=[qz.b], kind="d")
            K.op("sp", lambda e: e.dma_start(out=qz.t[64:128, 1, :], in_=dbg_q[0, 64:128, :]), writes=[qz.b], kind="d")
            K.op("sp", lambda e: e.dma_start(out=vTt.t[:], in_=dbg_q[2]), writes=[vTt.b], kind="d")
            Vaug = [sb(f"Vaug{i}", [128, 2, 65], BF16) for i in range(5)]
            for i in range(5):
                K.op("pool", lambda e, i=i: e.memset(Vaug[i].t[:], 1.0), writes=[Vaug[i].b])
            Pm = [sb(f"Pm{i}", [128, 512], BF16) for i in range(4)]
            Ost = [sb(f"Ost{i}", [128, 8, 130], F32) for i in range(2)]
            psS = [ps(f"psS{i}", [128, 512], F32) for i in range(3)]
            psO = [ps(f"psO{i}", [128, 512], F32) for i in range(2)]
            psV = [ps(f"psV{i}", [128, 1024], BF16) for i in range(2)]
            b_Od = [Buf() for _ in range(3)]
            tiles = []
            obi = 0
            ti = 0
            for p, d in enumerate((1, 4, 16)):
                nblk = S // (128 * d)
                Ov = O_d[p].rearrange("(n i r) c -> r i n c", i=128, r=d)
                nb8 = min(8, nblk)
                for r in range(d):
                    vprev = None
                    for n in range(nblk):
                        t0 = r + 128 * d * n
                        T = dict(p=p, d=d, r=r, n=n, Ov=Ov, nb8=nb8, ti=ti,
                                 tok=slice(t0, t0 + 127 * d + 1, d), ptok=slice(t0 - 128 * d, t0 - d + 1, d),
                                 va=Vaug[ti % 5], vprev=vprev, pv=psV[ti % 2], pS=psS[ti % 3], pO=psO[ti % 2], pm=Pm[ti % 4])
                        vprev = T["va"]
                        tiles.append(T)
                        ti += 1

            def front(T):
                pv, va, pS, pm, tok, ptok, n = T["pv"], T["va"], T["pS"], T["pm"], T["tok"], T["ptok"], T["n"]
                K.op("pe", lambda e: e.transpose(out=pv.t[:, 0:128], in_=vTt.t[:, tok], identity=ident), reads=[vTt.b, cbf.b], writes=[pv.b])
                K.op("dve", lambda e: e.tensor_copy(out=va.t[:, :, 0:64], in_=pv.t[:, 0:128].rearrange("p (h x) -> p h x", h=2)), reads=[pv.b], writes=[va.b])
                K.op("pe", lambda e: e.matmul(out=pS.t[:, 0:256], lhsT=kTt.t[:, tok], rhs=qz.t[:, :, tok], start=True, stop=True), reads=[kTt.b, qz.b], writes=[pS.b])
                w = 256
                if n > 0:
                    K.op("pe", lambda e: e.matmul(out=pS.t[:, 256:512], lhsT=kTt.t[:, ptok], rhs=qz.t[:, :, tok], start=True, stop=True), reads=[kTt.b, qz.b], writes=[pS.b])
                    w = 512
                K.op("act", lambda e: e.activation(out=pm.t[:, 0:w], in_=pS.t[:, 0:w], func=AF.Exp, scale=0.125), reads=[pS.b], writes=[pm.b])
                K.op("pool", lambda e: e.tensor_tensor(out=pm.t[:, 0:w], in0=pm.t[:, 0:w], in1=amask.t[:, 0:w], op=ALU.mult), reads=[pm.b, amask.b], writes=[pm.b])

            def back(T):
                nonlocal obi
                pO, pm, va, vprev, n, p, r, nb8, Ov = T["pO"], T["pm"], T["va"], T["vprev"], T["n"], T["p"], T["r"], T["nb8"], T["Ov"]
                for h in range(2):
                    K.op("pe", lambda e, h=h: e.matmul(out=pO.t[:, 65 * h:65 * h + 65], lhsT=pm.t[:, 128 * h:128 * h + 128], rhs=va.t[:, h, :], start=True, stop=(n == 0)),
                         reads=[pm.b, va.b], writes=[pO.b])
                    if n > 0:
                        K.op("pe", lambda e, h=h: e.matmul(out=pO.t[:, 65 * h:65 * h + 65], lhsT=pm.t[:, 256 + 128 * h:256 + 128 * h + 128], rhs=vprev.t[:, h, :], start=False, stop=True),
                             reads=[pm.b, vprev.b], writes=[pO.b])
                ob = Ost[obi % 2]
                K.op("dve", lambda e: e.tensor_copy(out=ob.t[:, n % 8, :], in_=pO.t[:, 0:130]), reads=[pO.b], writes=[ob.b])
                if n % 8 == nb8 - 1:
                    n0 = n - (nb8 - 1)
                    K.op("sp", lambda e: e.dma_start(out=Ov[r, :, n0:n0 + nb8, :], in_=ob.t[:, 0:nb8, :]), reads=[ob.b], writes=[b_Od[p]], kind="d")
                    obi += 1

            for i, T in enumerate(tiles):
                front(T)
                if i > 1:
                    back(tiles[i - 2])
            back(tiles[-2])
            back(tiles[-1])
            S3 = sb("S3", [128, 3, 8, 130], F32)
            b_S3 = [Buf() for _ in range(3)]
            sqt = sb("sqt", [128, 16, 64], F32)
            ssq2 = sb("ssq2", [128, 16], F32)
            d2 = sb("d2", [128, 16], F32)
            rr = sb("rr", [128, 16], F32)
            yb = sb("yb", [128, 16, 64], BF16)
            ya = [sb(f"ya{i}", [128, 1024], BF16) for i in range(2)]
            psT = ps("psT", [128, 8, 128], BF16)
            for bt in range(8):
                for p in range(3):
                    src = O_d[p][1024 * bt:1024 * bt + 1024, :].rearrange("(k i) c -> i k c", i=128)
                    K.op("sp", lambda e, p=p, src=src: e.dma_start(out=S3.t[:, p, :, :], in_=src), reads=[b_Od[p]], writes=[b_S3[p]], kind="d")
                acc = S3.t[:, 0, :, :].rearrange("p k c -> p (k c)")
                for p in (1, 2):
                    K.op("dve", lambda e, p=p, acc=acc: e.tensor_tensor(out=acc, in0=acc, in1=S3.t[:, p, :, :].rearrange("p k c -> p (k c)"), op=ALU.add),
                         reads=[b_S3[0], b_S3[p]], writes=[b_S3[0]])
                a16 = S3.t[:, 0, :, :].rearrange("p k (h c) -> p (k h) c", h=2)
                num = a16[:, :, 0:64]
                den = a16[:, :, 64]
                K.op("act", lambda e, num=num: e.activation(out=sqt.t[:], in_=num, func=AF.Square), reads=[b_S3[0]], writes=[sqt.b])
                K.op("dve", lambda e: e.tensor_reduce(out=ssq2.t[:], in_=sqt.t[:], axis=AX.X, op=ALU.add), reads=[sqt.b], writes=[ssq2.b])
                K.op("dve", lambda e, den=den: e.tensor_tensor(out=d2.t[:], in0=den, in1=den, op=ALU.mult), reads=[b_S3[0]], writes=[d2.b])
                K.op("dve", lambda e: e.tensor_scalar(out=d2.t[:], in0=d2.t[:], scalar1=1e-6, scalar2=None, op0=ALU.mult), reads=[d2.b], writes=[d2.b])
                K.op("dve", lambda e: e.scalar_tensor_tensor(out=rr.t[:], in0=ssq2.t[:], scalar=1.0 / 64, in1=d2.t[:], op0=ALU.mult, op1=ALU.add),
                     reads=[ssq2.b, d2.b], writes=[rr.b])
                K.op("act", lambda e: e.activation(out=rr.t[:], in_=rr.t[:], func=AF.Sqrt), reads=[rr.b], writes=[rr.b])
                K.op("dve", lambda e: e.reciprocal(out=rr.t[:], in_=rr.t[:]), reads=[rr.b], writes=[rr.b])
                rrb = bass.AP(rr.t, 0, [[16, 128], [1, 16], [0, 64]])
                K.op("dve", lambda e, num=num, rrb=rrb: e.tensor_tensor(out=yb.t[:], in0=num, in1=rrb, op=ALU.mult), reads=[b_S3[0], rr.b], writes=[yb.b])
                for k in range(8):
                    K.op("pe", lambda e, k=k: e.transpose(out=psT.t[:, k, :], in_=yb.t[:, 2 * k:2 * k + 2, :].rearrange("p h c -> p (h c)"), identity=ident),
                         reads=[yb.b, cbf.b], writes=[psT.b])
                yo = ya[bt % 2]
                K.op("act", lambda e, yo=yo: e.activation(out=yo.t[:], in_=psT.t[:].rearrange("p k c -> p (k c)"), func=AF.Copy, scale=pc(P_AG)),
                     reads=[psT.b, prm.b], writes=[yo.b])
                K.op("sp", lambda e, yo=yo, bt=bt: e.dma_start(out=gin_aq[bt // 2][:, 2 + 1024 * (bt % 2):2 + 1024 * (bt % 2) + 1024], in_=yo.t[:]), reads=[yo.b], writes=[b_gina[bt // 2]], kind="d")
                if bt % 2 == 1 and bt < 7:
                    K.op("sp", lambda e, yo=yo, bt=bt: e.dma_start(out=gin_aq[bt // 2 + 1][:, 0:2], in_=yo.t[:, 1022:1024]), reads=[yo.b], writes=[b_gina[bt // 2 + 1]], kind="d")
                if bt % 2 == 1 and P1 and P6:
                    K.op("pool", lambda e, q=bt // 2: e.collective_compute("AllGather", ALU.bypass, replica_groups=[[0, 1, 2, 3], [4, 5, 6, 7]], ins=[gin_aq[q]], outs=[gout_aq[q]]),
                         reads=[b_gina[bt // 2]], kind="cc")


        def phase6a(h2T, h2Th):
            RG = [[0, 1, 2, 3], [4, 5, 6, 7]]
            b_gr = [Buf() for _ in range(4)]
            b_ga = [Buf() for _ in range(4)]
            cc_ids = list(K.cc_pending)
            if P1 and P2:
                pass
            wo = sb("wo", [128, 8, D], BF16)
            wstg = [sb(f"wstg{i}", [128, 1024], F32) for i in range(2)]
            bw_o = [Buf() for _ in range(8)]
            for kc in range(8):
                stg = wstg[kc % 2]
                K.op("sp", lambda e, stg=stg, kc=kc: e.dma_start(out=stg.t[:], in_=w_out_d[128 * kc:128 * kc + 128, :]), writes=[stg.b], kind="d")
                K.op("pool" if kc % 2 else "dve", lambda e, stg=stg, kc=kc: e.tensor_copy(out=wo.t[:, kc, :], in_=stg.t[:]), reads=[stg.b], writes=[bw_o[kc]])
            Gq = [sb(f"Gq{i}", [128, 4, 512], BF16) for i in range(2)]
            yT = sb("yT", [128, 8, 512], BF16)
            xt6 = [sb(f"xt6_{i}", [128, D], F32) for i in range(2)]
            x1s = [sb(f"x1s{i}", [128, D], F32) for i in range(4)]
            junk6 = sb("junk6", [128, D], BF16)
            ssq6 = sb("ssq6", [128, 4], F32)
            r6 = sb("r6", [128, 4], F32)
            xn6 = [sb(f"xn6_{i}", [128, D], BF16) for i in range(2)]
            psA = ps("psA", [128, 1024], F32)
            tp6 = ps("tp6", [128, 8, 128], BF16)
            gr4 = [g_.rearrange("(c p) t -> p c t", p=128) for g_ in gout_rq]
            ga4 = [g_.rearrange("(c p) t -> p c t", p=128) for g_ in gout_aq]
            gi = [0]

            def block(kb, ntok, col_in_q, xrow0, hdst, hcol0):
                for q in range(4):
                    col = col_in_q
                    for part, (g4, bg) in enumerate(((gr4[q], b_gr[q]), (ga4[q], b_ga[q]))):
                        G = Gq[gi[0] % 2]
                        gi[0] += 1
                        K.op("sp", lambda e, G=G, col=col, g4=g4: e.dma_start(out=G.t[:, :, 0:ntok], in_=g4[:, :, col:col + ntok]), reads=[bg], writes=[G.b], kind="d", extra=cc_ids)
                        ysl = yT.t[:, 4 * part:4 * part + 4, 0:ntok]
                        if q == 0:
                            K.op("dve", lambda e, G=G, ysl=ysl: e.tensor_scalar(out=ysl, in0=G.t[:, :, 0:ntok], scalar1=q6(FL), scalar2=None, op0=ALU.mult),
                                 reads=[G.b, p6.b], writes=[yT.b])
                        else:
                            K.op("dve", lambda e, G=G, q=q, ysl=ysl: e.scalar_tensor_tensor(out=ysl, in0=G.t[:, :, 0:ntok], scalar=q6(FL + q), in1=ysl,
                                                                                            op0=ALU.mult, op1=ALU.add), reads=[G.b, p6.b, yT.b], writes=[yT.b])
                ntt = (ntok + 127) // 128
                for tt in range(ntt):
                    tw_ = min(128, ntok - 128 * tt)
                    xt = xt6[tt % 2]
                    x1 = x1s[tt]
                    K.op("sp", lambda e, xt=xt, tt=tt, tw_=tw_: e.dma_start(out=xt.t[0:tw_, :], in_=xq_d[xrow0 + 128 * tt:xrow0 + 128 * tt + tw_, :]), writes=[xt.b], kind="d")
                    for half in range(2):
                        for kc in range(8):
                            K.op("pe", lambda e, half=half, kc=kc, tt=tt, tw_=tw_: e.matmul(out=psA.t[0:tw_, 512 * half:512 * half + 512], lhsT=yT.t[:, kc, 128 * tt:128 * tt + tw_],
                                                                                         rhs=wo.t[:, kc, 512 * half:512 * half + 512], start=(kc == 0), stop=(kc == 7)),
                                 reads=[yT.b, bw_o[kc]], writes=[psA.b])
                    K.op("dve", lambda e, xt=xt, x1=x1, tw_=tw_: e.tensor_tensor(out=x1.t[0:tw_, :], in0=psA.t[0:tw_, :], in1=xt.t[0:tw_, :], op=ALU.add),
                         reads=[psA.b, xt.b], writes=[x1.b])
                    if kb >= 0:
                        r0 = 512 * kb + 128 * tt
                        K.op("sp", lambda e, x1=x1, r0=r0: e.dma_start(out=x1_d[r0:r0 + 128, :], in_=x1.t[:]), reads=[x1.b], kind="d")
                    K.op("act", lambda e, x1=x1, tt=tt, tw_=tw_: e.activation(out=junk6.t[0:tw_, :], in_=x1.t[0:tw_, :], func=AF.Square, accum_out=ssq6.t[0:tw_, tt:tt + 1]),
                         reads=[x1.b], writes=[junk6.b, ssq6.b])
                pw = min(128, ntok)
                K.op("act", lambda e: e.activation(out=r6.t[0:pw, 0:ntt], in_=ssq6.t[0:pw, 0:ntt], func=AF.Sqrt, scale=1.0 / D, bias=q6(EP, pw)), reads=[ssq6.b, p6.b], writes=[r6.b])
                K.op("dve", lambda e: e.reciprocal(out=r6.t[0:pw, 0:ntt], in_=r6.t[0:pw, 0:ntt]), reads=[r6.b], writes=[r6.b])
                for tt in range(ntt):
                    tw_ = min(128, ntok - 128 * tt)
                    x1 = x1s[tt]
                    xn = xn6[tt % 2]
                    K.op("act", lambda e, x1=x1, xn=xn, tt=tt, tw_=tw_: e.activation(out=xn.t[0:tw_, :], in_=x1.t[0:tw_, :], func=AF.Copy, scale=r6.t[0:tw_, tt:tt + 1]),
                         reads=[x1.b, r6.b], writes=[xn.b])
                    for fc in range(8):
                        K.op("pe", lambda e, xn=xn, fc=fc, tw_=tw_: e.transpose(out=tp6.t[:, fc, 0:tw_], in_=xn.t[0:tw_, fc * 128:(fc + 1) * 128], identity=cbf.t[0:tw_, 0, 0:tw_]),
                             reads=[xn.b, cbf.b], writes=[tp6.b])
                    c0 = hcol0 + 128 * tt
                    K.op("act", lambda e, tw_=tw_, c0=c0: e.copy(out=hdst.t[:, :, c0:c0 + tw_], in_=tp6.t[:, :, 0:tw_]), reads=[tp6.b], writes=[hdst.b])

            block(-1, 2, 0, 0, h2Th, 0)
            for kb in range(4):
                block(kb, 512, 2 + 512 * kb, 2 + 512 * kb, h2T, 512 * kb)

        def phase6b(h2T, h2Th):
            print("phase6b start remaining", nc.sbuf_bytes_remaining)
            wd = sb("wd", [128, 22, D], BF16)
            wdst = [sb(f"wdst{i}", [128, 1024], F32) for i in range(1)]
            bw_d = [Buf() for _ in range(22)]
            wust = [sb(f"wust{i}", [128, 8, 256], F32) for i in range(1)]
            wub = [sb(f"wub{i}", [128, 8, 256], BF16) for i in range(2)]
            actT = sb("actT", [128, 22, 2048], BF16)
            gbs = [sb(f"gb{i}", [128, 514], F32) for i in range(2)]
            gbi = 0
            c1 = [sb(f"c1_{i}", [128, 512], F32) for i in range(3)]
            psg = [ps(f"psg{i}", [128, 512], F32) for i in range(2)]
            psv = [ps(f"psv{i}", [128, 512], F32) for i in range(2)]
            psA = ps("psA2", [128, 1024], F32)
            wup3 = w_up_d.rearrange("(kc p) c -> p kc c", p=128)
            pi = 0
            for m in range(22):
                ws = wust[0]
                wb = wub[m % 2]
                K.op("sp", lambda e, ws=ws, m=m: e.dma_start(out=ws.t[:, :, 0:128], in_=wup3[:, :, 128 * m:128 * m + 128]), writes=[ws.b], kind="d")
                K.op("sp", lambda e, ws=ws, m=m: e.dma_start(out=ws.t[:, :, 128:256], in_=wup3[:, :, 2816 + 128 * m:2816 + 128 * m + 128]), writes=[ws.b], kind="d")
                for kc in range(8):
                    if kc % 2 == 0:
                        K.op("dve", lambda e, ws=ws, wb=wb, kc=kc: e.tensor_scalar(out=wb.t[:, kc, :], in0=ws.t[:, kc, :], scalar1=q6(G2 + kc), scalar2=None, op0=ALU.mult),
                             reads=[ws.b, p6.b], writes=[wb.b])
                    else:
                        K.op("act", lambda e, ws=ws, wb=wb, kc=kc: e.activation(out=wb.t[:, kc, :], in_=ws.t[:, kc, :], func=AF.Copy, scale=q6(G2 + kc)),
                             reads=[ws.b, p6.b], writes=[wb.b])
                wst_ = wdst[0]
                K.op("sp", lambda e, wst_=wst_, m=m: e.dma_start(out=wst_.t[:], in_=w_dn_d[128 * m:128 * m + 128, :]), writes=[wst_.b], kind="d")
                K.op("act", lambda e, wst_=wst_, m=m: e.copy(out=wd.t[:, m, :], in_=wst_.t[:]), reads=[wst_.b], writes=[bw_d[m]])
                pg = psg[pi % 2]
                for kc in range(8):
                    K.op("pe", lambda e, pg=pg, wb=wb, kc=kc: e.matmul(out=pg.t[:, 0:2], lhsT=wb.t[:, kc, 0:128], rhs=h2Th.t[:, kc, :], start=(kc == 0), stop=(kc == 7)),
                         reads=[wb.b, h2Th.b], writes=[pg.b])
                gb = gbs[gbi % 2]
                K.op("act", lambda e, pg=pg, gb=gb: e.copy(out=gb.t[:, 0:2], in_=pg.t[:, 0:2]), reads=[pg.b], writes=[gb.b])
                pi += 1
                for kb in range(4):
                    pg = psg[pi % 2]
                    pvv = psv[pi % 2]
                    c_ = c1[pi % 3]
                    pi += 1
                    gb = gbs[gbi % 2]
                    gbn = gbs[(gbi + 1) % 2]
                    gbi += 1
                    hsl = slice(512 * kb, 512 * kb + 512)
                    for kc in range(8):
                        K.op("pe", lambda e, pg=pg, wb=wb, kc=kc, hsl=hsl: e.matmul(out=pg.t[:], lhsT=wb.t[:, kc, 0:128], rhs=h2T.t[:, kc, hsl], start=(kc == 0), stop=(kc == 7)),
                             reads=[wb.b, h2T.b], writes=[pg.b])
                    for kc in range(8):
                        K.op("pe", lambda e, pvv=pvv, wb=wb, kc=kc, hsl=hsl: e.matmul(out=pvv.t[:], lhsT=wb.t[:, kc, 128:256], rhs=h2T.t[:, kc, hsl], start=(kc == 0), stop=(kc == 7)),
                             reads=[wb.b, h2T.b], writes=[pvv.b])
                    K.op("act", lambda e, pg=pg, gb=gb: e.copy(out=gb.t[:, 2:514], in_=pg.t[:]), reads=[pg.b], writes=[gb.b])
                    K.op("act", lambda e, c_=c_, m=m, pg=pg: e.activation(out=c_.t[:], in_=pg.t[:], func=AF.Identity, scale=q6(CW + 3 * m + 2), bias=q6(CB + m)),
                         reads=[pg.b, p6.b], writes=[c_.b])
                    K.op("dve", lambda e, c_=c_, m=m, gb=gb: e.scalar_tensor_tensor(out=c_.t[:], in0=gb.t[:, 1:513], scalar=q6(CW + 3 * m + 1), in1=c_.t[:], op0=ALU.mult, op1=ALU.add),
                         reads=[gb.b, p6.b, c_.b], writes=[c_.b])
                    K.op("dve", lambda e, c_=c_, m=m, gb=gb: e.scalar_tensor_tensor(out=c_.t[:], in0=gb.t[:, 0:512], scalar=q6(CW + 3 * m + 0), in1=c_.t[:], op0=ALU.mult, op1=ALU.add),
                         reads=[gb.b, p6.b, c_.b], writes=[c_.b])
                    if kb < 3:
                        K.op("pool", lambda e, gb=gb, gbn=gbn: e.tensor_copy(out=gbn.t[:, 0:2], in_=gb.t[:, 512:514]), reads=[gb.b], writes=[gbn.b])
                    K.op("act", lambda e, c_=c_: e.activation(out=c_.t[:], in_=c_.t[:], func=AF.Silu), reads=[c_.b], writes=[c_.b])
                    K.op("dve", lambda e, c_=c_, pvv=pvv, m=m, hsl=hsl: e.tensor_tensor(out=actT.t[:, m, hsl], in0=pvv.t[:], in1=c_.t[:], op=ALU.mult), reads=[c_.b, pvv.b], writes=[actT.b])
            class VW:
                def __init__(self, ap, b):
                    self.t = ap
                    self.b = b
            fg = wdst[0]
            K.op("sp", lambda e: e.dma_start(out=fg.t[:], in_=fg_d), writes=[fg.b], kind="d")
            xr = [VW(wust[0].t[:].rearrange("p a b -> p (a b)")[:, 1024 * i:1024 * i + 1024], wust[0].b if i == 0 else Buf()) for i in range(2)]
            x2 = xr
            junk7 = VW(wub[0].t[:].rearrange("p a b -> p (a b)")[:, 0:1024], wub[0].b)
            ssq7 = sb("ssq7", [128, 16], F32)
            r7 = sb("r7", [128, 16], F32)
            for t16 in range(16):
                xr_ = xr[t16 % 2]
                x2_ = x2[t16 % 2]
                K.op("sp", lambda e, xr_=xr_, t16=t16: e.dma_start(out=xr_.t[:], in_=x1_d[128 * t16:128 * t16 + 128, :]), writes=[xr_.b], kind="d")
                for half in range(2):
                    for m in range(22):
                        K.op("pe", lambda e, half=half, m=m, t16=t16: e.matmul(out=psA.t[:, 512 * half:512 * half + 512], lhsT=actT.t[:, m, 128 * t16:128 * t16 + 128],
                                                                             rhs=wd.t[:, m, 512 * half:512 * half + 512], start=(m == 0), stop=(m == 21)),
                             reads=[actT.b, bw_d[m]], writes=[psA.b])
                K.op("dve", lambda e, x2_=x2_, xr_=xr_: e.tensor_tensor(out=x2_.t[:], in0=psA.t[:], in1=xr_.t[:], op=ALU.add), reads=[psA.b, xr_.b], writes=[x2_.b])
                K.op("act", lambda e, x2_=x2_, t16=t16: e.activation(out=junk7.t[:], in_=x2_.t[:], func=AF.Square, accum_out=ssq7.t[:, t16:t16 + 1]),
                     reads=[x2_.b], writes=[junk7.b, ssq7.b])
                K.op("act", lambda e, t16=t16: e.activation(out=r7.t[:, t16:t16 + 1], in_=ssq7.t[:, t16:t16 + 1], func=AF.Sqrt, scale=1.0 / D, bias=q6(EP)), reads=[ssq7.b, p6.b], writes=[r7.b])
                K.op("dve", lambda e, t16=t16: e.reciprocal(out=r7.t[:, t16:t16 + 1], in_=r7.t[:, t16:t16 + 1]), reads=[r7.b], writes=[r7.b])
                K.op("dve", lambda e, x2_=x2_, t16=t16: e.scalar_tensor_tensor(out=x2_.t[:], in0=x2_.t[:], scalar=r7.t[:, t16:t16 + 1], in1=fg.t[:], op0=ALU.mult, op1=ALU.mult),
                     reads=[x2_.b, r7.b, fg.b], writes=[x2_.b])
                K.op("sp", lambda e, x2_=x2_, t16=t16: e.dma_start(out=out_d[128 * t16:128 * t16 + 128, :], in_=x2_.t[:]), reads=[x2_.b], kind="d")


        if P1:
            ph = ExitStack()
            cur[0] = ph
            with ph:
                phase1()
                print("phase1 sbuf bytes remaining", nc.sbuf_bytes_remaining)
                K.flush(include_cc=False)
            cur[0] = st
        if P2:
            ph = ExitStack()
            cur[0] = ph
            with ph:
                phase2()
                print("phase2 sbuf bytes remaining", nc.sbuf_bytes_remaining)
                K.flush(include_cc=False)
            cur[0] = st
        if P6:
            G2, FL, CB, CW, EP = 0, 8, 12, 34, 100
            ph0 = ExitStack()
            cur[0] = ph0
            with ph0:
                p6 = sb("p6", [128, 128], F32)
                K.op("sp", lambda e: e.dma_start(out=p6.t[:], in_=p6_d), writes=[p6.b], kind="d")

                def q6(col, n=128):
                    return p6.t[0:n, col:col + 1]
                h2T = sb("h2T", [128, 8, 2048], BF16)
                h2Th = sb("h2Th", [128, 8, 2], BF16)
                ph = ExitStack()
                cur[0] = ph
                with ph:
                    phase6a(h2T, h2Th)
                    print("phase6a sbuf bytes remaining", nc.sbuf_bytes_remaining)
                    K.flush()
                ph = ExitStack()
                cur[0] = ph
                with ph:
                    phase6b(h2T, h2Th)
                    print("phase6b sbuf bytes remaining", nc.sbuf_bytes_remaining)
                    K.flush()
            cur[0] = st
        K.final_wait()
    return nc


def _core_inputs_p1(inp, c):
    b, g = c // 4, c % 4
    l = 0
    w = inp['w_in'][l]
    sm = inp['rwkv_shift_mix'][l]
    A0 = 1696
    r128 = np.arange(128*g, 128*g+128)
    cols = np.concatenate([r128, 512+r128, 1024+r128, np.arange(1536,1600), np.arange(1600,1696), A0+r128, A0+512+r128, A0+1024+r128])
    w_c = np.ascontiguousarray(w[:, cols])
    prm = np.zeros((128, 32), np.float32)
    prm[:, 0:8] = inp['mix_norm_gain'][l].reshape(8,128).T
    MO = [0,128,256,384,448]; MW=[128,128,128,64,96]
    for m in range(5):
        prm[:MW[m], 8+m] = sm[cols[MO[m]:MO[m]+MW[m]]]
    ch = slice(128*g, 128*g+128)
    prm[:, 13] = inp['w0'][l][ch]; prm[:, 14] = inp['a0'][l][ch]; prm[:,15] = inp['k_k'][l][ch]; prm[:,16]=inp['k_a'][l][ch]
    prm[:, 17] = inp['r_k'][l].reshape(-1)[ch]; prm[:,18]=inp['ln_x_w'][l][ch]; prm[:,19]=inp['ln_x_b'][l][ch]
    prm[:, 20] = inp['attn_norm_gain'][l][ch]
    invf = (500000.0 ** (-np.arange(8, dtype=np.float32) * 2.0 / 16)).astype(np.float32)
    for h in range(2):
        for cc in range(16):
            prm[64*h+cc, 22] = invf[cc % 8] / np.float32(2*np.pi)
    prm[:, 24] = 1e-6; prm[:, 25] = 1e-24; prm[:, 26] = 64e-5
    cbf = np.zeros((128,3,128), np.float32)
    cbf[:,0,:] = np.eye(128); cbf[:,2,:] = np.eye(128)
    for h in range(2):
        for cc in range(8):
            cbf[64*h+cc+8, 1, 64*h+cc] = -1.0
            cbf[64*h+cc, 1, 64*h+cc+8] = 1.0
    cf = np.zeros((128, 1408), np.float32)
    cf[:, 0:512] = np.arange(512, dtype=np.float32)[None, :]
    bo = np.zeros((128,128), np.float32); bo[:64,:64] = 1; bo[64:,64:] = 1
    cf[:, 512:640] = bo
    ii = np.arange(128)
    SU = (ii[:,None] < ii[None,:]).astype(np.float32); SL = (ii[:,None] > ii[None,:]).astype(np.float32); UI = (ii[:,None] <= ii[None,:]).astype(np.float32)
    cf[:, 640:1152] = np.concatenate([SU, SL, SU, SL], axis=1)
    cf[:, 1152:1408] = np.concatenate([UI, SU], axis=1)
    lora = np.zeros((128,3,128), np.float32)
    lora[0:32, 0, :] = inp['w_lora_up'][l][:, ch]
    lora[32:64, 1, :] = inp['a_lora_up'][l][:, ch]
    lora[0:96, 2, :] = inp['g_lora_up'][l][:, ch]
    return dict(xb=np.ascontiguousarray(inp['x'][b]), w_in=w_c, prm=prm, cbf=cbf.astype(ml_dtypes.bfloat16), cf=cf, lora=lora), cols


def _core_inputs(inp, c):
    m, cols = _core_inputs_p1(inp, c)
    b, g = c // 4, c % 4
    ii = np.arange(128)
    UI = (ii[:,None] <= ii[None,:]).astype(np.float32); LI = (ii[:,None] >= ii[None,:]).astype(np.float32)
    m['amask'] = np.concatenate([UI, UI, LI, LI], axis=1).astype(ml_dtypes.bfloat16)
    l = 0
    x = inp['x'][b]
    xq = np.zeros((2050, 1024), np.float32)
    lo = 2048*g - 2
    if g == 0:
        xq[2:] = x[0:2048]
    else:
        xq[:] = x[lo:lo+2050]
    m['xq'] = xq
    m['w_out'] = np.ascontiguousarray(inp['w_out'][l])
    m['w_up'] = np.ascontiguousarray(inp['w_ffn_up'][l])
    m['w_dn'] = np.ascontiguousarray(inp['w_ffn_down'][l])
    p6 = np.zeros((128,128), np.float32)
    p6[:, 0:8] = inp['ffn_norm_gain'][l].reshape(8,128).T
    p6[:, 8+g] = 1.0
    p6[:, 12:34] = inp['ffn_conv_b'][l].reshape(22,128).T
    cw = inp['ffn_conv_w'][l]
    for k in range(3):
        p6[:, 34+k:34+66:3] = cw[k].reshape(22,128).T
    p6[:, 100] = 1e-6
    m['p6'] = p6
    m['fgain'] = np.ascontiguousarray(np.broadcast_to(inp['final_norm_gain'][None,:], (128,1024))).astype(np.float32)
    return m, cols


def kernel(**inputs):
    inp = {k: np.asarray(v) for k, v in inputs.items()}
    nc = build(phases=(1, 2, 6))
    maps = [_core_inputs(inp, c)[0] for c in range(8)]
    res = run_bass_kernel_spmd(nc, maps, core_ids=list(range(8)))
    out = np.zeros((2, S, D), np.float32)
    for c in range(8):
        b, g = c // 4, c % 4
        out[b, 2048 * g:2048 * g + 2048] = np.asarray(res.results[c]["out"], dtype=np.float32)
    return out
```

```python
from contextlib import ExitStack
import concourse.bass as bass
import concourse.mybir as mybir

F32 = mybir.dt.float32
BF16 = mybir.dt.bfloat16
AF = mybir.ActivationFunctionType
ALU = mybir.AluOpType

ENGINES = ["pe", "act", "dve", "pool", "sp"]
SEM_LIMIT = 30000
DMA_SLOTS = {"sp": 12, "pool": 6, "act": 4}


class Buf:
    __slots__ = ("name", "w", "rd")

    def __init__(self, name=""):
        self.name = name
        self.w = None
        self.rd = []


class Op:
    __slots__ = ("idx", "eng", "pos", "fn", "deps", "kind", "sem", "consumed", "prev")

    def __init__(self):
        self.consumed = False
        self.sem = None
        self.prev = None


class Sched:
    def __init__(self, nc, stack):
        self.nc = nc
        self.ops = []
        self.eng_ops = {e: [] for e in ENGINES}
        self.emitted = {e: 0 for e in ENGINES}
        self.eng_sems = {}
        self.eng_cnt = {}
        self.dma_sems = {}
        self.dma_cnt = {}
        for e in ["pe", "act", "dve", "pool"]:
            self.eng_sems[e] = [stack.enter_context(nc.semaphore(f"s_{e}{i}")) for i in range(4)]
            self.eng_cnt[e] = [0, 0]
        for e, n in DMA_SLOTS.items():
            self.dma_sems[e] = [stack.enter_context(nc.semaphore(f"d_{e}{i}")) for i in range(n)]
            self.dma_cnt[e] = 0
        self.cc_sem = stack.enter_context(nc.semaphore("cc_sem"))
        self.cc_cnt = 0
        self.pending_barrier = {}
        self.waited = {e: {} for e in ENGINES}
        self.dma_since_barrier = []
        self.cc_pending = []
        self.boundary = 0

    def op(self, eng, fn, reads=(), writes=(), kind="c"):
        o = Op()
        o.idx = len(self.ops)
        o.eng = eng
        o.fn = fn
        o.kind = kind
        o.pos = len(self.eng_ops[eng])
        deps = set()
        for b in reads:
            if b.w is not None:
                deps.add(b.w)
        for b in writes:
            if b.w is not None:
                deps.add(b.w)
            deps.update(b.rd)
        deps = set(d for d in deps if d >= self.boundary)
        for b in reads:
            b.rd.append(o.idx)
        for b in writes:
            b.w = o.idx
            b.rd = []
        if eng in self.pending_barrier:
            deps.update(self.pending_barrier.pop(eng))
        keep = []
        for d in deps:
            p = self.ops[d]
            if p.eng == eng and p.kind == "c" and kind == "c":
                if eng == "pe":
                    continue
                if o.pos - p.pos > 3:
                    continue
            keep.append(d)
        o.deps = keep
        self.ops.append(o)
        self.eng_ops[eng].append(o)
        if kind == "d":
            self.dma_since_barrier.append(o.idx)
        elif kind == "cc":
            self.cc_pending.append(o.idx)
        return o

    def barrier(self, include_cc=True):
        last = set()
        if include_cc:
            last.update(self.cc_pending)
            self.cc_pending = []
        for e in ENGINES:
            if self.eng_ops[e]:
                last.add(self.eng_ops[e][-1].idx)
        last.update(self.dma_since_barrier)
        self.dma_since_barrier = []
        for e in ENGINES:
            self.pending_barrier.setdefault(e, set()).update(last)

    def flush(self, include_cc=True):
        nc = self.nc
        self.barrier(include_cc)
        new_ops = {e: self.eng_ops[e][self.emitted[e]:] for e in ENGINES}
        for e in ENGINES:
            for o in new_ops[e]:
                for d in o.deps:
                    self.ops[d].consumed = True
        for e in ENGINES:
            for o in self.eng_ops[e][-1:]:
                o.consumed = True
        for e in ENGINES:
            for o in new_ops[e]:
                if o.kind == "d":
                    i = self.dma_cnt[e]
                    self.dma_cnt[e] += 1
                    K = len(self.dma_sems[e])
                    s = self.dma_sems[e][i % K]
                    v = 16 * (i // K + 1)
                    o.sem = (s, v)
                    o.prev = (s, v - 16) if v > 16 else None
                    o.consumed = True
                elif o.kind == "cc":
                    self.cc_cnt += 1
                    o.sem = (self.cc_sem, self.cc_cnt)
                    o.consumed = True
                elif o.consumed:
                    st = self.eng_cnt[e]
                    if st[1] >= SEM_LIMIT:
                        st[0] += 1
                        st[1] = 0
                    st[1] += 1
                    o.sem = (self.eng_sems[e][st[0]], st[1])

        def emit_engine(ename, eng):
            waited = self.waited[ename]
            for o in new_ops[ename]:
                need = {}
                for d in o.deps:
                    s, v = self.ops[d].sem
                    if need.get(s, 0) < v:
                        need[s] = v
                if o.prev is not None:
                    s, v = o.prev
                    if need.get(s, 0) < v:
                        need[s] = v
                for s, v in need.items():
                    if waited.get(s, 0) < v:
                        eng.wait_ge(s, v)
                        waited[s] = v
                ins = o.fn(eng)
                if o.sem is not None and o.consumed:
                    if o.kind == "d":
                        ins.then_inc(o.sem[0], 16)
                    else:
                        ins.then_inc(o.sem[0], 1)

        with nc.Block() as block:
            @block.tensor
            def _(eng):
                emit_engine("pe", eng)

            @block.scalar
            def _(eng):
                emit_engine("act", eng)

            @block.vector
            def _(eng):
                emit_engine("dve", eng)

            @block.gpsimd
            def _(eng):
                emit_engine("pool", eng)

            @block.sync
            def _(eng):
                emit_engine("sp", eng)

        for e in ENGINES:
            self.emitted[e] = len(self.eng_ops[e])
        self.boundary = len(self.ops)

    def final_wait(self):
        nc = self.nc
        need = {}
        for d in self.pending_barrier.get("sp", set()):
            s, v = self.ops[d].sem
            if need.get(s, 0) < v:
                need[s] = v

        with nc.Block() as block:
            @block.sync
            def _(eng):
                for s, v in need.items():
                    eng.wait_ge(s, v)


import numpy as np
import ml_dtypes
from contextlib import ExitStack
import concourse.bass as bass
import concourse.mybir as mybir
from concourse.bass_utils import run_bass_kernel_spmd

I32 = mybir.dt.int32
AX = mybir.AxisListType
S = 8192
D = 1024
NB = 16
MW = [128, 128, 128, 64, 96, 128, 128, 128]
MOFF = [0, 128, 256, 384, 448, 544, 672, 800]
NCOL = 928
NPRM = 32
PI = float(np.pi)
C0 = float(np.exp(-0.5))
GN_EPS = 64e-5
P_GAIN, P_MIX, P_W0, P_A0, P_KK, P_KA, P_RK, P_LNW, P_LNB, P_AG = 0, 8, 13, 14, 15, 16, 17, 18, 19, 20
P_F2PI, P_EPS, P_TINY, P_GNEPS, P_OMKA = 22, 24, 25, 26, 27


class TB:
    def __init__(self, t):
        self.t = t
        self.b = Buf()


def v3(ap):
    return ap.rearrange("p (c t) -> p c t", c=4)


def build(phases=(1, 2, 6), nblocks=NB, debug=False, stage=9, do_rwkv=True):
    nc = bass.Bass("TRN2", target_bir_lowering=False)
    P1, P2, P6 = (1 in phases), (2 in phases), (6 in phases)
    xb = nc.dram_tensor("xb", [S, D], F32, kind="ExternalInput").ap()
    w_in = nc.dram_tensor("w_in", [D, NCOL], F32, kind="ExternalInput").ap()
    prm_d = nc.dram_tensor("prm", [128, NPRM], F32, kind="ExternalInput").ap()
    cbf_d = nc.dram_tensor("cbf", [128, 3, 128], BF16, kind="ExternalInput").ap()
    cf_d = nc.dram_tensor("cf", [128, 1408], F32, kind="ExternalInput").ap()
    lora_d = nc.dram_tensor("lora", [128, 3, 128], F32, kind="ExternalInput").ap()
    amask_d = nc.dram_tensor("amask", [128, 512], BF16, kind="ExternalInput").ap()
    qkv_kind = "Internal" if P1 else "ExternalInput"
    if debug and P1:
        qkv_kind = "ExternalOutput"
    dbg_q = nc.dram_tensor("dbg_q", [3, 128, S], BF16, kind=qkv_kind).ap()
    GW = S + 2
    g_in_kind = "Internal" if not debug else "ExternalOutput"
    QW = 2050
    gin_rq = [nc.dram_tensor(f"gin_r{q}", [128, QW], BF16, kind=g_in_kind if P1 else "Internal").ap() for q in range(4)]
    gin_aq = [nc.dram_tensor(f"gin_a{q}", [128, QW], BF16, kind=g_in_kind if P2 else "Internal").ap() for q in range(4)]
    gout_kind = "Internal" if (P1 and P2) else "ExternalInput"
    gout_rq = [nc.dram_tensor(f"gout_r{q}", [512, QW], BF16, kind=gout_kind).ap() for q in range(4)]
    gout_aq = [nc.dram_tensor(f"gout_a{q}", [512, QW], BF16, kind=gout_kind).ap() for q in range(4)]
    O_d = nc.dram_tensor("O_d", [3, S, 130], F32, kind="Internal").ap()
    if P6:
        xq_d = nc.dram_tensor("xq", [2050, D], F32, kind="ExternalInput").ap()
        w_out_d = nc.dram_tensor("w_out", [D, D], F32, kind="ExternalInput").ap()
        w_up_d = nc.dram_tensor("w_up", [D, 5632], F32, kind="ExternalInput").ap()
        w_dn_d = nc.dram_tensor("w_dn", [2816, D], F32, kind="ExternalInput").ap()
        p6_d = nc.dram_tensor("p6", [128, 128], F32, kind="ExternalInput").ap()
        fg_d = nc.dram_tensor("fgain", [128, D], F32, kind="ExternalInput").ap()
        out_d = nc.dram_tensor("out", [2048, D], F32, kind="ExternalOutput").ap()
        x1_d = nc.dram_tensor("x1_d", [2048, D], F32, kind="Internal").ap()
    if False:
        dbg_p = nc.dram_tensor("dbg_p", [8, 128, S], F32, kind="ExternalOutput").ap()

    st = ExitStack()
    with st:
        K = Sched(nc, st)

        cur = [st]

        def sb(name, shape, dt):
            return TB(cur[0].enter_context(nc.sbuf_tensor("s_" + name, shape, dt)))

        def ps(name, shape, dt):
            return TB(cur[0].enter_context(nc.psum_tensor("p_" + name, shape, dt)))

        prm = sb("prm", [128, NPRM], F32)
        cbf = sb("cbf", [128, 3, 128], BF16)
        cf = sb("cf", [128, 1408], F32)
        lora = sb("lora", [128, 3, 128], F32)
        glu_bf = sb("glu_bf", [128, 128], BF16)
        K.op("sp", lambda e: e.dma_start(out=prm.t[:], in_=prm_d), writes=[prm.b], kind="d")
        K.op("sp", lambda e: e.dma_start(out=cbf.t[:], in_=cbf_d), writes=[cbf.b], kind="d")
        K.op("sp", lambda e: e.dma_start(out=cf.t[:], in_=cf_d), writes=[cf.b], kind="d")
        K.op("sp", lambda e: e.dma_start(out=lora.t[:], in_=lora_d), writes=[lora.b], kind="d")
        K.op("pool", lambda e: e.tensor_copy(out=glu_bf.t[0:96, :], in_=lora.t[0:96, 2, :]), reads=[lora.b], writes=[glu_bf.b])
        K.op("pool", lambda e: e.tensor_scalar(out=prm.t[:, P_OMKA:P_OMKA + 1], in0=prm.t[:, P_KA:P_KA + 1], scalar1=-1.0, scalar2=1.0,
                                               op0=ALU.mult, op1=ALU.add), reads=[prm.b], writes=[prm.b])
        ident = cbf.t[:, 0, :]
        prot = cbf.t[:, 1, :]
        ident2 = cbf.t[:, 0:3:2, :]
        iota = cf.t[:, 0:512]
        blockones = cf.t[:, 512:640]
        maskXY = cf.t[:, 640:1152].rearrange("p (h x) -> p h x", h=2)
        mask3 = cf.t[:, 1152:1408]
        ones128 = None

        def pc(col, n=128, p0=0):
            return prm.t[p0:p0 + n, col:col + 1]

        def phase1():
            w_bf = sb("w_bf", [128, 8, NCOL], BF16)
            wst = [sb(f"wst{i}", [128, NCOL], F32) for i in range(2)]
            for kc in range(8):
                i = kc % 2
                K.op("sp", lambda e, kc=kc, i=i: e.dma_start(out=wst[i].t[:], in_=w_in[kc * 128:(kc + 1) * 128, :]),
                     writes=[wst[i].b], kind="d")
                if kc % 2 == 0:
                    K.op("dve", lambda e, kc=kc, i=i: e.tensor_scalar(out=w_bf.t[:, kc, :], in0=wst[i].t[:], scalar1=pc(P_GAIN + kc),
                                                                     scalar2=None, op0=ALU.mult),
                         reads=[wst[i].b, prm.b], writes=[w_bf.b])
                else:
                    K.op("act", lambda e, kc=kc, i=i: e.activation(out=w_bf.t[:, kc, :], in_=wst[i].t[:], func=AF.Copy, scale=pc(P_GAIN + kc)),
                         reads=[wst[i].b, prm.b], writes=[w_bf.b])

            NXT = 6
            xts = [sb(f"xt{i}", [128, D], F32) for i in range(NXT)]
            junk = sb("junk", [128, D], BF16)
            ssq = sb("ssq", [128, 64], F32)
            rstd = sb("rstd", [128, 64], F32)
            b_ss = [Buf() for _ in range(NB)]
            xns = [sb(f"xn{i}", [128, D], BF16) for i in range(2)]
            hTs = [sb(f"hT{i}", [128, 8, 512], BF16) for i in range(2)]
            tpbs = [ps(f"tpb{i}", [128, 8, 128], BF16) for i in range(2)]
            pjs = [ps(f"pj{i}", [128, 512], F32) for i in range(2)]
            bank3 = ps("bank3", [128, 512], F32)
            AA = ps("AA", [128, 512], F32)
            psxy = ps("psxy", [128, 2, 256], F32)
            bank7 = ps("bank7", [128, 512], F32)
            b_at = b_pp = bank3.b
            b_pz = bank7.b
            b_tok, b_ynT = tpbs[0].b, tpbs[1].b
            b_pu = b_psn = b_py = bank7.b
            ps_at = bank3.t[:, 0:256].rearrange("p (h x) -> p h x", h=2)
            ps_p = bank3.t[:, 256:512].rearrange("p (h x) -> p h x", h=2)
            ps_z = bank7.t[:, 384:512].rearrange("p (h x) -> p h x", h=2)
            ps_tok = tpbs[0].t[:, 0:5, :]
            ps_ynT = tpbs[1].t[:, 0:4, :].rearrange("p k x -> p (k x)")
            ps_u = bank7.t[:, 0:128].rearrange("p (h x) -> p h x", h=2)
            ps_s = bank7.t[:, 128:256].rearrange("p (h x) -> p h x", h=2)
            ps_y = bank7.t[:, 256:384]
            ostg = [sb(f"ostg{i}", [128, 512], BF16) for i in range(4)]
            ostg_i = [0]

            def next_ostg():
                o = ostg[ostg_i[0] % 4]
                ostg_i[0] += 1
                return o
            pbuf = [sb(f"pbuf{m}", [128, 513], F32) for m in range(5)]
            pmix = [sb(f"pmix{m}", [128, 512], F32) for m in range(5)]
            qb = sb("qb", [128, 512], BF16)
            ang = sb("ang", [128, 512], F32)
            angi = sb("angi", [128, 512], I32)
            ctab = sb("ctab", [128, 512], F32)
            stab = sb("stab", [128, 512], F32)
            t1, t2 = wst[0], wst[1]
            for m in range(5):
                K.op("pool", lambda e, m=m: e.memset(pbuf[m].t[:, 0:1], 0.0), writes=[pbuf[m].b])

            if do_rwkv:
                ones = sb("ones", [128, 128], F32)
                K.op("pool", lambda e: e.memset(ones.t[:], 1.0), writes=[ones.b])
                tw = sb("tw", [32, 512], F32)
                sg = sb("sg", [128, 512], F32)
                aa = sb("aa", [128, 512], F32)
                sgd = sb("sgd", [96, 512], BF16)
                gTS = [sb(f"gT{i}", [128, 512], BF16) for i in range(2)]
                ELcS = [sb(f"ELc{i}", [128, 4], F32) for i in range(2)]
                Lbuf = sb("Lbuf", [128, 4, 129], F32)
                K.op("pool", lambda e: e.memset(Lbuf.t[:], 0.0), writes=[Lbuf.b])
                EL = sb("EL", [128, 512], F32)
                ELn = sb("ELn", [128, 512], F32)
                ELx = sb("ELx", [128, 512], F32)
                kkr = sb("kkr", [128, 512], F32)
                sq = sb("sq", [128, 512], F32)
                rn = sb("rn", [128, 512], F32)
                kk = sb("kk", [128, 512], F32)
                k2 = sb("k2", [128, 512], F32)
                kka = sb("kka", [128, 512], F32)
                bonusTS = [sb(f"bonusT{i}", [128, 512], F32) for i in range(2)]
                ARzS = [[sb(f"ARz{i}_{h}", [128, 4, 256], BF16) for h in range(2)] for i in range(2)]
                for i in range(2):
                    for h in range(2):
                        K.op("pool", lambda e, h=h, i=i: e.memset(ARzS[i][h].t[:], 0.0), writes=[ARzS[i][h].b])
                KTS = [sb(f"KT{i}", [128, 512], BF16) for i in range(2)]
                BTS = [sb(f"BT{i}", [128, 512], BF16) for i in range(2)]
                VBS = [sb(f"VB{i}", [128, 512], BF16) for i in range(2)]
                TOK = [sb(f"TOK{c}", [128, 5, 128], BF16) for c in range(4)]
                M3 = [[sb(f"M3_{c}_{h}", [128, 384], BF16) for h in range(2)] for c in range(4)]
                XY = [[sb(f"XY{c}_{i}", [128, 2, 256], BF16) for i in range(2)] for c in range(4)]
                PP = [[sb(f"PP{c}_{i}", [128, 2, 128], BF16) for i in range(2)] for c in range(4)]
                ATz = [[sb(f"ATz{c}_{h}", [128, 128], BF16) for h in range(2)] for c in range(4)]
                for c in range(4):
                    for h in range(2):
                        K.op("pool", lambda e, c=c, h=h: e.memset(ATz[c][h].t[:], 0.0), writes=[ATz[c][h].b])
                Zs = [sb(f"Zs{c}", [128, 2, 64], BF16) for c in range(4)]
                Us = [sb(f"Us{c}", [128, 2, 64], BF16) for c in range(4)]
                Sbf = [sb(f"Sbf{i}", [128, 64], BF16) for i in range(2)]
                K.op("pool", lambda e: e.memset(Sbf[0].t[:], 0.0), writes=[Sbf[0].b])
                K.op("pool", lambda e: e.memset(Sbf[1].t[:], 0.0), writes=[Sbf[1].b])
                ys = sb("ys", [128, 4, 128], F32)
                bst = sb("bst", [128, 8, 6], F32)
                mv = sb("mv", [128, 8, 2], F32)
                grs = sb("grs", [128, 8], F32)
                yn = sb("yn", [128, 4, 128], BF16)
                yt = sb("yt", [128, 512], F32)
                s_idx = [0]

            TWO_PI_S = 2 * PI * (1 - 1e-6)
            Cb = sb("Cb", [128, 512], F32)
            Sb = sb("Sb", [128, 512], F32)
            cjs = sb("cjs", [128, 16], F32)
            sjs = sb("sjs", [128, 16], F32)
            nsj = sb("nsj", [128, 16], F32)

            def sincos(n, u_fn, s_out, c_out):
                K.op("dve", lambda e: u_fn(e, ang.t[:, 0:n]), reads=[cf.b, prm.b], writes=[ang.b])
                for add, dst in ((0.0, s_out), (0.25, c_out)):
                    if add:
                        K.op("dve", lambda e: e.tensor_scalar(out=ang.t[:, 0:n], in0=ang.t[:, 0:n], scalar1=0.25, scalar2=None, op0=ALU.add), reads=[ang.b], writes=[ang.b])
                    K.op("dve", lambda e: e.tensor_copy(out=angi.t[:, 0:n], in_=ang.t[:, 0:n]), reads=[ang.b], writes=[angi.b])
                    K.op("dve", lambda e: e.tensor_copy(out=t2.t[:, 0:n], in_=angi.t[:, 0:n]), reads=[angi.b], writes=[t2.b])
                    K.op("dve", lambda e: e.tensor_tensor(out=t1.t[:, 0:n], in0=ang.t[:, 0:n], in1=t2.t[:, 0:n], op=ALU.subtract), reads=[ang.b, t2.b], writes=[t1.b])
                    K.op("act", lambda e, dst=dst: e.activation(out=dst.t[:, 0:n], in_=t1.t[:, 0:n], func=AF.Sin, scale=TWO_PI_S), reads=[t1.b], writes=[dst.b])
            sincos(512, lambda e, o: e.tensor_scalar(out=o, in0=iota, scalar1=pc(P_F2PI), scalar2=None, op0=ALU.mult), Sb, Cb)
            sincos(16, lambda e, o: e.tensor_scalar(out=o, in0=iota[:, 0:16], scalar1=512.0, scalar2=pc(P_F2PI), op0=ALU.mult, op1=ALU.mult), sjs, cjs)
            K.op("dve", lambda e: e.tensor_scalar(out=nsj.t[:], in0=sjs.t[:], scalar1=-1.0, scalar2=None, op0=ALU.mult), reads=[sjs.b], writes=[nsj.b])
            pj_rr = [0]

            def next_pj():
                p = pjs[pj_rr[0] % 2]
                pj_rr[0] += 1
                return p

            def stageA(j):
                hT = hTs[j % 2]
                for tt in range(4):
                    ti = 4 * j + tt
                    xt = xts[ti % NXT]
                    r0 = 512 * j + 128 * tt
                    K.op("sp", lambda e, xt=xt, r0=r0: e.dma_start(out=xt.t[:], in_=xb[r0:r0 + 128, :]), writes=[xt.b], kind="d")
                    K.op("act", lambda e, xt=xt, ti=ti: e.activation(out=junk.t[:], in_=xt.t[:], func=AF.Square, accum_out=ssq.t[:, ti:ti + 1]),
                         reads=[xt.b], writes=[junk.b, b_ss[j]])
                K.op("act", lambda e, j=j: e.activation(out=rstd.t[:, 4 * j:4 * j + 4], in_=ssq.t[:, 4 * j:4 * j + 4], func=AF.Sqrt, scale=1.0 / D, bias=pc(P_EPS)),
                     reads=[b_ss[j], prm.b], writes=[b_ss[j]])
                K.op("dve", lambda e, j=j: e.reciprocal(out=rstd.t[:, 4 * j:4 * j + 4], in_=rstd.t[:, 4 * j:4 * j + 4]), reads=[b_ss[j]], writes=[b_ss[j]])
                for tt in range(4):
                    ti = 4 * j + tt
                    xt = xts[ti % NXT]
                    xn = xns[ti % 2]
                    K.op("act", lambda e, xt=xt, xn=xn, ti=ti: e.activation(out=xn.t[:], in_=xt.t[:], func=AF.Copy, scale=rstd.t[:, ti:ti + 1]),
                         reads=[xt.b, b_ss[j]], writes=[xn.b])
                    tpb = tpbs[ti % 2]
                    for fc in range(8):
                        K.op("pe", lambda e, xn=xn, fc=fc, tpb=tpb: e.transpose(out=tpb.t[:, fc, :], in_=xn.t[:, fc * 128:(fc + 1) * 128], identity=ident),
                             reads=[xn.b, cbf.b], writes=[tpb.b])
                    K.op("act", lambda e, hT=hT, tt=tt, tpb=tpb: e.copy(out=hT.t[:, :, tt * 128:(tt + 1) * 128], in_=tpb.t[:]),
                         reads=[tpb.b], writes=[hT.b])
                    yield
                K.op("dve", lambda e, j=j: e.tensor_scalar(out=ctab.t[:], in0=Cb.t[:], scalar1=cjs.t[:, j:j + 1], scalar2=None, op0=ALU.mult),
                     reads=[Cb.b, cjs.b], writes=[ctab.b])
                K.op("dve", lambda e, j=j: e.scalar_tensor_tensor(out=ctab.t[:], in0=Sb.t[:], scalar=nsj.t[:, j:j + 1], in1=ctab.t[:], op0=ALU.mult, op1=ALU.add),
                     reads=[Sb.b, nsj.b, ctab.b], writes=[ctab.b])
                K.op("dve", lambda e, j=j: e.tensor_scalar(out=stab.t[:], in0=Cb.t[:], scalar1=sjs.t[:, j:j + 1], scalar2=None, op0=ALU.mult),
                     reads=[Cb.b, sjs.b], writes=[stab.b])
                K.op("dve", lambda e, j=j: e.scalar_tensor_tensor(out=stab.t[:], in0=Sb.t[:], scalar=cjs.t[:, j:j + 1], in1=stab.t[:], op0=ALU.mult, op1=ALU.add),
                     reads=[Sb.b, cjs.b, stab.b], writes=[stab.b])
                for m in range(8):
                    pj = next_pj()
                    mw = MW[m]
                    for kc in range(8):
                        K.op("pe", lambda e, pj=pj, m=m, kc=kc, mw=mw, hT=hT: e.matmul(
                            out=pj.t[0:mw, :], lhsT=w_bf.t[:, kc, MOFF[m]:MOFF[m] + mw], rhs=hT.t[:, kc, :], start=(kc == 0), stop=(kc == 7)),
                            reads=[w_bf.b, hT.b], writes=[pj.b])
                    if m < 5:
                        pb = pbuf[m]
                        K.op("act", lambda e, pb=pb, pj=pj, mw=mw: e.copy(out=pb.t[0:mw, 1:513], in_=pj.t[0:mw, :]), reads=[pj.b], writes=[pb.b])
                        K.op("dve", lambda e, pb=pb, mw=mw: e.tensor_tensor(out=t1.t[0:mw, 0:512], in0=pb.t[0:mw, 0:512], in1=pb.t[0:mw, 1:513], op=ALU.subtract),
                             reads=[pb.b], writes=[t1.b])
                        K.op("dve", lambda e, pb=pb, mw=mw, m=m: e.scalar_tensor_tensor(out=pmix[m].t[0:mw, :], in0=t1.t[0:mw, 0:512], scalar=pc(P_MIX + m, mw),
                                                                                     in1=pb.t[0:mw, 1:513], op0=ALU.mult, op1=ALU.add),
                             reads=[pb.b, t1.b, prm.b], writes=[pmix[m].b])
                        K.op("pool", lambda e, pb=pb, mw=mw: e.tensor_copy(out=pb.t[0:mw, 0:1], in_=pb.t[0:mw, 512:513]), reads=[pb.b], writes=[pb.b])
                    elif m < 7:
                        og = next_ostg()
                        K.op("act", lambda e, pj=pj: e.copy(out=qb.t[:], in_=pj.t[:]), reads=[pj.b], writes=[qb.b])
                        pr = next_pj()
                        K.op("pe", lambda e, pr=pr: e.matmul(out=pr.t[:], lhsT=prot, rhs=qb.t[:], start=True, stop=True), reads=[qb.b, cbf.b], writes=[pr.b])
                        K.op("dve", lambda e: e.tensor_tensor(out=t1.t[:, 0:512], in0=qb.t[:], in1=ctab.t[:], op=ALU.mult), reads=[qb.b, ctab.b], writes=[t1.b])
                        K.op("dve", lambda e, pr=pr: e.tensor_tensor(out=t2.t[:, 0:512], in0=pr.t[:], in1=stab.t[:], op=ALU.mult), reads=[pr.b, stab.b], writes=[t2.b])
                        K.op("pool", lambda e, og=og: e.tensor_tensor(out=og.t[:], in0=t1.t[:, 0:512], in1=t2.t[:, 0:512], op=ALU.add),
                             reads=[t1.b, t2.b], writes=[og.b])
                        K.op("pool", lambda e, og=og, j=j, m=m: e.dma_start(out=dbg_q[m - 5, :, 512 * j:512 * j + 512], in_=og.t[:]), reads=[og.b], kind="d")
                    else:
                        og = next_ostg()
                        K.op("act", lambda e, pj=pj, og=og: e.copy(out=og.t[:], in_=pj.t[:]), reads=[pj.b], writes=[og.b])
                        K.op("pool", lambda e, og=og, j=j: e.dma_start(out=dbg_q[2, :, 512 * j:512 * j + 512], in_=og.t[:]), reads=[og.b], kind="d")
                    yield
                yield
            def stageB(j, ARz, KT, BT, VB, ELc, bonusT, gT):
                rp, kp, vp, lo, gd = pmix
                K.op("act", lambda e: e.activation(out=tw.t[:], in_=lo.t[0:32, :], func=AF.Tanh), reads=[lo.b], writes=[tw.b])
                pz = next_pj()
                K.op("pe", lambda e, pz=pz: e.matmul(out=pz.t[:], lhsT=lora.t[0:32, 0, :], rhs=tw.t[:], start=True, stop=True),
                     reads=[lora.b, tw.b], writes=[pz.b])
                K.op("act", lambda e, pz=pz: e.activation(out=sg.t[:], in_=pz.t[:], func=AF.Sigmoid, bias=pc(P_W0)), reads=[pz.b, prm.b], writes=[sg.b])
                pa = next_pj()
                K.op("pe", lambda e, pa=pa: e.matmul(out=pa.t[:], lhsT=lora.t[32:64, 1, :], rhs=lo.t[32:64, :], start=True, stop=True),
                     reads=[lora.b, lo.b], writes=[pa.b])
                K.op("act", lambda e, pa=pa: e.activation(out=aa.t[:], in_=pa.t[:], func=AF.Sigmoid, bias=pc(P_A0)), reads=[pa.b, prm.b], writes=[aa.b])
                yield
                K.op("act", lambda e: e.activation(out=sgd.t[:], in_=gd.t[0:96, :], func=AF.Sigmoid), reads=[gd.b], writes=[sgd.b])
                pg = next_pj()
                K.op("pe", lambda e, pg=pg: e.matmul(out=pg.t[:], lhsT=glu_bf.t[0:96, :], rhs=sgd.t[:], start=True, stop=True),
                     reads=[glu_bf.b, sgd.b], writes=[pg.b])
                K.op("act", lambda e, pg=pg: e.copy(out=gT.t[:], in_=pg.t[:]), reads=[pg.b], writes=[gT.b])
                yield
                for c in range(4):
                    K.op("dve", lambda e, c=c: e.tensor_tensor_scan(out=Lbuf.t[:, c, 1:129], data0=ones.t[:], data1=sg.t[:, 128 * c:128 * c + 128],
                                                                    initial=0.0, op0=ALU.mult, op1=ALU.add), reads=[ones.b, sg.b], writes=[Lbuf.b])
                K.op("act", lambda e: e.activation(out=v3(EL.t[:]), in_=Lbuf.t[:, :, 1:129], func=AF.Exp, scale=-C0), reads=[Lbuf.b], writes=[EL.b])
                K.op("act", lambda e: e.activation(out=v3(ELn.t[:]), in_=Lbuf.t[:, :, 1:129], func=AF.Exp, scale=C0), reads=[Lbuf.b], writes=[ELn.b])
                K.op("act", lambda e: e.activation(out=v3(ELx.t[:]), in_=Lbuf.t[:, :, 0:128], func=AF.Exp, scale=-C0), reads=[Lbuf.b], writes=[ELx.b])
                K.op("pool", lambda e: e.tensor_copy(out=ELc.t[:], in_=EL.t[:, 127:512:128]), reads=[EL.b], writes=[ELc.b])
                yield
                K.op("dve", lambda e: e.tensor_scalar(out=kkr.t[:], in0=kp.t[:], scalar1=pc(P_KK), scalar2=None, op0=ALU.mult), reads=[kp.b, prm.b], writes=[kkr.b])
                K.op("pool", lambda e: e.tensor_tensor(out=sq.t[:], in0=kkr.t[:], in1=kkr.t[:], op=ALU.mult), reads=[kkr.b], writes=[sq.b])
                pss = next_pj()
                K.op("pe", lambda e, pss=pss: e.matmul(out=pss.t[:], lhsT=blockones, rhs=sq.t[:], start=True, stop=True), reads=[cf.b, sq.b], writes=[pss.b])
                K.op("act", lambda e, pss=pss: e.activation(out=rn.t[:], in_=pss.t[:], func=AF.Sqrt, bias=pc(P_TINY)), reads=[pss.b, prm.b], writes=[rn.b])
                K.op("dve", lambda e: e.reciprocal(out=rn.t[:], in_=rn.t[:]), reads=[rn.b], writes=[rn.b])
                K.op("pool", lambda e: e.tensor_tensor(out=kk.t[:], in0=kkr.t[:], in1=rn.t[:], op=ALU.mult), reads=[kkr.b, rn.b], writes=[kk.b])
                yield
                K.op("dve", lambda e: e.tensor_scalar(out=k2.t[:], in0=aa.t[:], scalar1=pc(P_KA), scalar2=pc(P_OMKA), op0=ALU.mult, op1=ALU.add),
                     reads=[aa.b, prm.b], writes=[k2.b])
                K.op("pool", lambda e: e.tensor_tensor(out=k2.t[:], in0=k2.t[:], in1=kp.t[:], op=ALU.mult), reads=[k2.b, kp.b], writes=[k2.b])
                yield
                K.op("dve", lambda e: e.scalar_tensor_tensor(out=sq.t[:], in0=rp.t[:], scalar=pc(P_RK), in1=k2.t[:], op0=ALU.mult, op1=ALU.mult),
                     reads=[rp.b, k2.b, prm.b, sq.b], writes=[sq.b])
                pbs = next_pj()
                K.op("pe", lambda e, pbs=pbs: e.matmul(out=pbs.t[:], lhsT=blockones, rhs=sq.t[:], start=True, stop=True), reads=[cf.b, sq.b], writes=[pbs.b])
                K.op("dve", lambda e, pbs=pbs: e.tensor_tensor(out=bonusT.t[:], in0=pbs.t[:], in1=vp.t[:], op=ALU.mult), reads=[pbs.b, vp.b], writes=[bonusT.b])
                K.op("pool", lambda e: e.tensor_tensor(out=kka.t[:], in0=kk.t[:], in1=aa.t[:], op=ALU.mult), reads=[kk.b, aa.b], writes=[kka.b])
                yield
                for h in range(2):
                    hs = slice(64 * h, 64 * h + 64)
                    K.op("dve", lambda e, h=h, hs=hs: e.tensor_tensor(out=ARz[h].t[hs, :, 128:256], in0=v3(rp.t[hs, :]), in1=v3(EL.t[hs, :]), op=ALU.mult),
                         reads=[rp.b, EL.b], writes=[ARz[h].b])
                    K.op("dve", lambda e, h=h, hs=hs: e.scalar_tensor_tensor(out=ARz[h].t[hs, :, 0:128], in0=v3(kk.t[hs, :]), scalar=-1.0, in1=v3(ELx.t[hs, :]), op0=ALU.mult, op1=ALU.mult),
                         reads=[kk.b, ELx.b], writes=[ARz[h].b])
                K.op("dve", lambda e: e.tensor_tensor(out=KT.t[:], in0=k2.t[:], in1=ELn.t[:], op=ALU.mult), reads=[k2.b, ELn.b], writes=[KT.b])
                K.op("pool", lambda e: e.tensor_tensor(out=BT.t[:], in0=kka.t[:], in1=ELn.t[:], op=ALU.mult), reads=[kka.b, ELn.b], writes=[BT.b])
                K.op("pool", lambda e: e.tensor_copy(out=VB.t[:], in_=vp.t[:]), reads=[vp.b], writes=[VB.b])
                yield

            def blockCDE(j, pump, ARz, KT, BT, VB, ELc, bonusT, gT):
                for c in range(4):
                    cs = slice(128 * c, 128 * c + 128)
                    K.op("pe", lambda e, cs=cs: e.transpose(out=ps_tok[:, 0, :], in_=KT.t[:, cs], identity=ident), reads=[KT.b, cbf.b], writes=[b_tok])
                    K.op("pe", lambda e, cs=cs: e.transpose(out=ps_tok[:, 1, :], in_=BT.t[:, cs], identity=ident), reads=[BT.b, cbf.b], writes=[b_tok])
                    for h in range(2):
                        K.op("pe", lambda e, c=c, h=h: e.transpose(out=ps_tok[:, 2 + h, :], in_=ARz[h].t[:, c, 0:128], identity=ident), reads=[ARz[h].b, cbf.b], writes=[b_tok])
                    K.op("pe", lambda e, cs=cs: e.transpose(out=ps_tok[:, 4, :], in_=VB.t[:, cs], identity=ident), reads=[VB.b, cbf.b], writes=[b_tok])
                    K.op("act", lambda e, c=c: e.copy(out=TOK[c].t[:], in_=ps_tok), reads=[b_tok], writes=[TOK[c].b])
                if stage < 2:
                    return
                for c in range(4):
                    cs = slice(128 * c, 128 * c + 128)
                    for h in range(2):
                        hs = slice(64 * h, 64 * h + 64)
                        K.op("pe", lambda e, c=c, cs=cs, hs=hs, h=h: e.matmul(out=AA.t[:, 256 * h:256 * h + 128], lhsT=BT.t[:, cs], rhs=ARz[h].t[:, c, 0:128], start=True, stop=True),
                             reads=[BT.b, ARz[h].b], writes=[AA.b])
                        K.op("pe", lambda e, c=c, cs=cs, hs=hs, h=h: e.matmul(out=AA.t[:, 256 * h + 128:256 * h + 256], lhsT=ARz[h].t[:, c, 0:128], rhs=BT.t[:, cs], start=True, stop=True),
                             reads=[BT.b, ARz[h].b], writes=[AA.b])
                    K.op("dve", lambda e, c=c: e.tensor_tensor(out=XY[c][0].t[:], in0=AA.t[:].rearrange("p (h x) -> p h x", h=2), in1=maskXY, op=ALU.mult),
                         reads=[AA.b, cf.b], writes=[XY[c][0].b])
                    if stage >= 2.2:
                        K.op("pool", lambda e, c=c: e.tensor_tensor(out=PP[c][0].t[:], in0=XY[c][0].t[:, :, 0:128], in1=ident2, op=ALU.add),
                             reads=[XY[c][0].b, cbf.b], writes=[PP[c][0].b])
                if stage < 2.5:
                    return
                for c in range(4):
                    cs = slice(128 * c, 128 * c + 128)
                    for h in range(2):
                        hs = slice(64 * h, 64 * h + 64)
                        K.op("pe", lambda e, c=c, cs=cs, h=h: e.matmul(out=AA.t[:, 0:128], lhsT=BT.t[:, cs], rhs=ARz[h].t[:, c, 128:256], start=True, stop=True),
                             reads=[BT.b, ARz[h].b], writes=[AA.b])
                        K.op("pe", lambda e, c=c, cs=cs, h=h: e.matmul(out=AA.t[:, 128:384], lhsT=KT.t[:, cs], rhs=ARz[h].t[:, c, 0:256], start=True, stop=True),
                             reads=[KT.b, ARz[h].b], writes=[AA.b])
                        K.op("dve", lambda e, c=c, h=h: e.tensor_tensor(out=M3[c][h].t[:, 0:256], in0=AA.t[:, 0:256], in1=mask3, op=ALU.mult),
                             reads=[AA.b, cf.b], writes=[M3[c][h].b])
                        K.op("dve", lambda e, c=c, h=h: e.tensor_tensor(out=M3[c][h].t[:, 256:384], in0=AA.t[:, 256:384], in1=mask3[:, 0:128], op=ALU.mult),
                             reads=[AA.b, cf.b], writes=[M3[c][h].b])
                if stage < 3:
                    return
                xy_ring = [(psxy.t[:], psxy.b), (bank7.t[:].rearrange("p (h x) -> p h x", h=2), bank7.b)]
                pp_ring = [(ps_p, bank3.b), (AA.t[:, 0:256].rearrange("p (h x) -> p h x", h=2), AA.b)]
                for kq in range(6):
                    cur, nxt = kq % 2, (kq + 1) % 2

                    def xy_part(c, kq=kq, cur=cur, nxt=nxt):
                        Xc, Xn = XY[c][cur], XY[c][nxt]
                        pxy, bxy = xy_ring[c % 2]
                        for h in range(2):
                            if kq < 5:
                                K.op("pe", lambda e, h=h: e.matmul(out=pxy[:, h, 0:128], lhsT=Xc.t[:, h, 128:256], rhs=Xc.t[:, h, 0:128], start=True, stop=True),
                                     reads=[Xc.b], writes=[bxy])
                            K.op("pe", lambda e, h=h: e.matmul(out=pxy[:, h, 128:256], lhsT=Xc.t[:, h, 0:128], rhs=Xc.t[:, h, 128:256], start=True, stop=True),
                                 reads=[Xc.b], writes=[bxy])
                        if kq < 5:
                            K.op("act", lambda e: e.copy(out=Xn.t[:], in_=pxy), reads=[bxy], writes=[Xn.b])
                        else:
                            K.op("act", lambda e: e.copy(out=Xn.t[:, :, 128:256], in_=pxy[:, :, 128:256]), reads=[bxy], writes=[Xn.b])

                    def p_part(c, kq=kq, cur=cur, nxt=nxt):
                        Xn = XY[c][nxt]
                        Pc, Pn = PP[c][cur], PP[c][nxt]
                        ppp, bpp = pp_ring[c % 2]
                        for h in range(2):
                            K.op("pe", lambda e, h=h: e.matmul(out=ppp[:, h, :], lhsT=Xn.t[:, h, 128:256], rhs=Pc.t[:, h, :], start=True, stop=True),
                                 reads=[Xn.b, Pc.b], writes=[bpp])
                        K.op("dve", lambda e: e.tensor_tensor(out=Pn.t[:], in0=ppp, in1=Pc.t[:], op=ALU.add), reads=[bpp, Pc.b], writes=[Pn.b])
                    xy_part(0)
                    for c in range(1, 4):
                        xy_part(c)
                        p_part(c - 1)
                        pump()
                    p_part(3)
                    pump()
                if stage < 4:
                    return
                for c in range(4):
                    Tm = PP[c][0]
                    for h in range(2):
                        hs = slice(64 * h, 64 * h + 64)
                        K.op("pe", lambda e, c=c, h=h, hs=hs, Tm=Tm: e.matmul(out=ps_at[:, h, :], lhsT=TOK[c].t[:, 2 + h, :], rhs=Tm.t[:, h, :], start=True, stop=True),
                             reads=[TOK[c].b, Tm.b], writes=[b_at])
                        K.op("pe", lambda e, c=c, h=h, hs=hs: e.matmul(out=ps_z[:, h, :], lhsT=M3[c][h].t[:, 128:256], rhs=TOK[c].t[:, 4, hs], start=True, stop=True),
                             reads=[M3[c][h].b, TOK[c].b], writes=[b_pz])
                    for h in range(2):
                        hs = slice(64 * h, 64 * h + 64)
                        K.op("act", lambda e, c=c, h=h, hs=hs: e.copy(out=ATz[c][h].t[hs, :], in_=ps_at[hs, h, :]), reads=[b_at], writes=[ATz[c][h].b])
                    K.op("dve", lambda e, c=c: e.tensor_copy(out=Zs[c].t[:], in_=ps_z), reads=[b_pz], writes=[Zs[c].b])
                if stage < 5:
                    return
                for c in range(4):
                    Tm = PP[c][0]
                    Sc = Sbf[s_idx[0] % 2]
                    Sn = Sbf[(s_idx[0] + 1) % 2]
                    s_idx[0] += 1
                    for h in range(2):
                        hs = slice(64 * h, 64 * h + 64)
                        K.op("pe", lambda e, c=c, h=h, Tm=Tm: e.matmul(out=ps_u[:, h, :], lhsT=Tm.t[:, h, :], rhs=Zs[c].t[:, h, :], start=True, stop=False),
                             reads=[Tm.b, Zs[c].b], writes=[b_pu])
                        K.op("pe", lambda e, c=c, h=h, hs=hs, Sc=Sc: e.matmul(out=ps_u[:, h, :], lhsT=ATz[c][h].t[:], rhs=Sc.t[:], start=False, stop=True),
                             reads=[ATz[c][h].b, Sc.b], writes=[b_pu])
                    K.op("act", lambda e, c=c: e.copy(out=Us[c].t[:], in_=ps_u), reads=[b_pu], writes=[Us[c].b])
                    pump()
                    for h in range(2):
                        hs = slice(64 * h, 64 * h + 64)
                        K.op("pe", lambda e, c=c, h=h, hs=hs: e.matmul(out=ps_s[:, h, :], lhsT=TOK[c].t[:, 0, :], rhs=TOK[c].t[:, 4, hs], start=True, stop=False),
                             reads=[TOK[c].b], writes=[b_psn])
                        K.op("pe", lambda e, c=c, h=h, hs=hs: e.matmul(out=ps_s[:, h, :], lhsT=TOK[c].t[:, 1, :], rhs=Us[c].t[:, h, :], start=False, stop=False),
                             reads=[TOK[c].b, Us[c].b], writes=[b_psn])
                        K.op("pe", lambda e, c=c, h=h, hs=hs, Sc=Sc: e.matmul(out=ps_s[:, h, :], lhsT=ident, rhs=Sc.t[:], start=False, stop=True),
                             reads=[cbf.b, Sc.b], writes=[b_psn])
                    for h in range(2):
                        hs = slice(64 * h, 64 * h + 64)
                        K.op("dve" if h == 0 else "act", (lambda e, c=c, Sn=Sn, h=h, hs=hs: e.tensor_scalar(out=Sn.t[hs, :], in0=ps_s[hs, h, :], scalar1=ELc.t[hs, c:c + 1], scalar2=None, op0=ALU.mult)) if h == 0 else
                             (lambda e, c=c, Sn=Sn, h=h, hs=hs: e.activation(out=Sn.t[hs, :], in_=ps_s[hs, h, :], func=AF.Copy, scale=ELc.t[hs, c:c + 1])),
                             reads=[b_psn, ELc.b], writes=[Sn.b])
                    for h in range(2):
                        hs = slice(64 * h, 64 * h + 64)
                        K.op("pe", lambda e, c=c, h=h, hs=hs, Sc=Sc: e.matmul(out=bank7.t[:, 256 + 64 * h:320 + 64 * h], lhsT=ARz[h].t[:, c, 128:256], rhs=Sc.t[:], start=True, stop=False),
                             reads=[ARz[h].b, Sc.b], writes=[b_py])
                        K.op("pe", lambda e, c=c, h=h: e.matmul(out=bank7.t[:, 256 + 64 * h:320 + 64 * h], lhsT=M3[c][h].t[:, 0:128], rhs=Us[c].t[:, h, :], start=False, stop=False),
                             reads=[M3[c][h].b, Us[c].b], writes=[b_py])
                        K.op("pe", lambda e, c=c, h=h, hs=hs: e.matmul(out=bank7.t[:, 256 + 64 * h:320 + 64 * h], lhsT=M3[c][h].t[:, 256:384], rhs=TOK[c].t[:, 4, hs], start=False, stop=True),
                             reads=[M3[c][h].b, TOK[c].b], writes=[b_py])
                    K.op("act", lambda e, c=c: e.copy(out=ys.t[:, c, :], in_=ps_y), reads=[b_py], writes=[ys.b])
                    pump()
                    for h in range(2):
                        K.op("dve", lambda e, c=c, h=h: e.bn_stats(out=bst.t[:, 2 * c + h, :], in_=ys.t[:, c, 64 * h:64 * h + 64]), reads=[ys.b], writes=[bst.b])
                        K.op("dve", lambda e, c=c, h=h: e.bn_aggr(out=mv.t[:, 2 * c + h, :], in_=bst.t[:, 2 * c + h, :]), reads=[bst.b], writes=[mv.b])
                if stage < 6:
                    return
                K.op("act", lambda e: e.activation(out=grs.t[:], in_=mv.t[:, :, 1], func=AF.Sqrt, bias=pc(P_GNEPS)), reads=[mv.b, prm.b], writes=[grs.b])
                K.op("dve", lambda e: e.reciprocal(out=grs.t[:], in_=grs.t[:]), reads=[grs.b], writes=[grs.b])
                for c in range(4):
                    for h in range(2):
                        i = 2 * c + h
                        K.op("dve", lambda e, c=c, h=h, i=i: e.tensor_scalar(out=yn.t[:, c, 64 * h:64 * h + 64], in0=ys.t[:, c, 64 * h:64 * h + 64],
                                                                            scalar1=mv.t[:, i, 0:1], scalar2=grs.t[:, i:i + 1], op0=ALU.subtract, op1=ALU.mult),
                             reads=[ys.b, mv.b, grs.b], writes=[yn.b])
                for c in range(4):
                    K.op("pe", lambda e, c=c: e.transpose(out=ps_ynT[:, 128 * c:128 * c + 128], in_=yn.t[:, c, :], identity=ident), reads=[yn.b, cbf.b], writes=[b_ynT])
                K.op("dve", lambda e: e.tensor_scalar(out=yt.t[:], in0=ps_ynT, scalar1=pc(P_LNW), scalar2=pc(P_LNB), op0=ALU.mult, op1=ALU.add),
                     reads=[b_ynT, prm.b], writes=[yt.b])
                K.op("pool", lambda e: e.tensor_tensor(out=yt.t[:], in0=yt.t[:], in1=bonusT.t[:], op=ALU.add), reads=[yt.b, bonusT.b], writes=[yt.b])
                pump(100)
                og = next_ostg()
                K.op("pool", lambda e, og=og: e.tensor_tensor(out=og.t[:], in0=yt.t[:], in1=gT.t[:], op=ALU.mult),
                     reads=[yt.b, gT.b], writes=[og.b])
                K.op("sp", lambda e, og=og, j=j: e.dma_start(out=gin_rq[j // 4][:, 2 + 512 * (j % 4):2 + 512 * (j % 4) + 512], in_=og.t[:]), reads=[og.b], writes=[b_ginr[j // 4]], kind="d")
                if j % 4 == 3 and j < 15:
                    K.op("sp", lambda e, og=og, j=j: e.dma_start(out=gin_rq[j // 4 + 1][:, 0:2], in_=og.t[:, 510:512]), reads=[og.b], writes=[b_ginr[j // 4 + 1]], kind="d")
                if j == 0:
                    K.op("pool", lambda e: e.memset(qb.t[:, 0:2], 0.0), writes=[qb.b])
                    K.op("sp", lambda e: e.dma_start(out=gin_rq[0][:, 0:2], in_=qb.t[:, 0:2]), reads=[qb.b], writes=[b_ginr[0]], kind="d")
                if j % 4 == 3 and P2 and P6:
                    K.op("pool", lambda e, q=j // 4: e.collective_compute("AllGather", ALU.bypass, replica_groups=[[0, 1, 2, 3], [4, 5, 6, 7]], ins=[gin_rq[q]], outs=[gout_rq[q]]),
                         reads=[b_ginr[j // 4]], kind="cc")
                if debug and False:
                    for i, tl in enumerate([sg, aa, kk, k2, EL, ys]):
                        src = tl.t[:] if tl is not ys else ys.t[:].rearrange("p c t -> p (c t)")
                        K.op("sp", lambda e, i=i, src=src, j=j: e.dma_start(out=dbg_p[i, :, 512 * j:512 * j + 512], in_=src), reads=[tl.b], kind="d")

            import itertools
            b_ginr = [Buf() for _ in range(4)]
            sets = [dict(ARz=ARzS[i], KT=KTS[i], BT=BTS[i], VB=VBS[i], ELc=ELcS[i], bonusT=bonusTS[i], gT=gTS[i]) for i in range(2)]
            for _ in stageA(0):
                pass
            for _ in stageB(0, **sets[0]):
                pass
            for j in range(nblocks):
                gens = [stageA(j + 1), stageB(j + 1, **sets[(j + 1) % 2])] if j + 1 < nblocks else []
                itr = itertools.chain(*gens)

                def pump(k=1, itr=itr):
                    for _ in range(k):
                        next(itr, None)
                blockCDE(j, pump, **sets[j % 2])
                pump(1000)

        def phase2():
            kTt = sb("kTt", [128, S], BF16)
            vTt = sb("vTt", [128, S], BF16)
            qz = sb("qz", [128, 2, S], BF16)
            amask = sb("amask", [128, 512], BF16)
            zt = sb("zt", [128, 2], BF16)
            K.op("pool", lambda e: e.memset(zt.t[:], 0.0), writes=[zt.b])
            b_gina = [Buf() for _ in range(4)]
            K.op("sp", lambda e: e.dma_start(out=gin_aq[0][:, 0:2], in_=zt.t[:]), reads=[zt.b], writes=[b_gina[0]], kind="d")
            K.op("sp", lambda e: e.dma_start(out=amask.t[:], in_=amask_d), writes=[amask.b], kind="d")
            K.op("sp", lambda e: e.dma_start(out=kTt.t[:], in_=dbg_q[1]), writes=[kTt.b], kind="d")
            K.op("pool", lambda e: e.memset(qz.t[64:128, 0, :], 0.0), writes=[qz.b])
            K.op("pool", lambda e: e.memset(qz.t[0:64, 1, :], 0.0), writes=[qz.b])
            K.op("sp", lambda e: e.dma_start(out=qz.t[0:64, 0, :], in_=dbg_q[0, 0:64, :]), writes=[qz.b], kind="d")
            K.op("sp", lambda e: e.dma_start(out=qz.t[64:128, 1, :], in_=dbg_q[0, 64:128, :]), writes=[qz.b], kind="d")
            K.op("sp", lambda e: e.dma_start(out=vTt.t[:], in_=dbg_q[2]), writes=[vTt.b], kind="d")
            Vaug = [sb(f"Vaug{i}", [128, 2, 65], BF16) for i in range(5)]
            for i in range(5):
                K.op("pool", lambda e, i=i: e.memset(Vaug[i].t[:], 1.0), writes=[Vaug[i].b])
            Pm = [sb(f"Pm{i}", [128, 512], BF16) for i in range(4)]
            Ost = [sb(f"Ost{i}", [128, 8, 130], F32) for i in range(2)]
            psS = [ps(f"psS{i}", [128, 512], F32) for i in range(3)]
            psO = [ps(f"psO{i}", [128, 512], F32) for i in range(2)]
            psV = [ps(f"psV{i}", [128, 1024], BF16) for i in range(2)]
            b_Od = [Buf() for _ in range(3)]
            tiles = []
            obi = 0
            ti = 0
            for p, d in enumerate((1, 4, 16)):
                nblk = S // (128 * d)
                Ov = O_d[p].rearrange("(n i r) c -> r i n c", i=128, r=d)
                nb8 = min(8, nblk)
                for r in range(d):
                    vprev = None
                    for n in range(nblk):
                        t0 = r + 128 * d * n
                        T = dict(p=p, d=d, r=r, n=n, Ov=Ov, nb8=nb8, ti=ti,
                                 tok=slice(t0, t0 + 127 * d + 1, d), ptok=slice(t0 - 128 * d, t0 - d + 1, d),
                                 va=Vaug[ti % 5], vprev=vprev, pv=psV[ti % 2], pS=psS[ti % 3], pO=psO[ti % 2], pm=Pm[ti % 4])
                        vprev = T["va"]
                        tiles.append(T)
                        ti += 1

            def front(T):
                pv, va, pS, pm, tok, ptok, n = T["pv"], T["va"], T["pS"], T["pm"], T["tok"], T["ptok"], T["n"]
                K.op("pe", lambda e: e.transpose(out=pv.t[:, 0:128], in_=vTt.t[:, tok], identity=ident), reads=[vTt.b, cbf.b], writes=[pv.b])
                K.op("dve", lambda e: e.tensor_copy(out=va.t[:, :, 0:64], in_=pv.t[:, 0:128].rearrange("p (h x) -> p h x", h=2)), reads=[pv.b], writes=[va.b])
                K.op("pe", lambda e: e.matmul(out=pS.t[:, 0:256], lhsT=kTt.t[:, tok], rhs=qz.t[:, :, tok], start=True, stop=True), reads=[kTt.b, qz.b], writes=[pS.b])
                w = 256
                if n > 0:
                    K.op("pe", lambda e: e.matmul(out=pS.t[:, 256:512], lhsT=kTt.t[:, ptok], rhs=qz.t[:, :, tok], start=True, stop=True), reads=[kTt.b, qz.b], writes=[pS.b])
                    w = 512
                K.op("act", lambda e: e.activation(out=pm.t[:, 0:w], in_=pS.t[:, 0:w], func=AF.Exp, scale=0.125), reads=[pS.b], writes=[pm.b])
                K.op("pool", lambda e: e.tensor_tensor(out=pm.t[:, 0:w], in0=pm.t[:, 0:w], in1=amask.t[:, 0:w], op=ALU.mult), reads=[pm.b, amask.b], writes=[pm.b])

            def back(T):
                nonlocal obi
                pO, pm, va, vprev, n, p, r, nb8, Ov = T["pO"], T["pm"], T["va"], T["vprev"], T["n"], T["p"], T["r"], T["nb8"], T["Ov"]
                for h in range(2):
                    K.op("pe", lambda e, h=h: e.matmul(out=pO.t[:, 65 * h:65 * h + 65], lhsT=pm.t[:, 128 * h:128 * h + 128], rhs=va.t[:, h, :], start=True, stop=(n == 0)),
                         reads=[pm.b, va.b], writes=[pO.b])
                    if n > 0:
                        K.op("pe", lambda e, h=h: e.matmul(out=pO.t[:, 65 * h:65 * h + 65], lhsT=pm.t[:, 256 + 128 * h:256 + 128 * h + 128], rhs=vprev.t[:, h, :], start=False, stop=True),
                             reads=[pm.b, vprev.b], writes=[pO.b])
                ob = Ost[obi % 2]
                K.op("dve", lambda e: e.tensor_copy(out=ob.t[:, n % 8, :], in_=pO.t[:, 0:130]), reads=[pO.b], writes=[ob.b])
                if n % 8 == nb8 - 1:
                    n0 = n - (nb8 - 1)
                    K.op("sp", lambda e: e.dma_start(out=Ov[r, :, n0:n0 + nb8, :], in_=ob.t[:, 0:nb8, :]), reads=[ob.b], writes=[b_Od[p]], kind="d")
                    obi += 1

            for i, T in enumerate(tiles):
                front(T)
                if i > 1:
                    back(tiles[i - 2])
            back(tiles[-2])
            back(tiles[-1])
            S3 = sb("S3", [128, 3, 8, 130], F32)
            b_S3 = [Buf() for _ in range(3)]
            sqt = sb("sqt", [128, 16, 64], F32)
            ssq2 = sb("ssq2", [128, 16], F32)
            d2 = sb("d2", [128, 16], F32)
            rr = sb("rr", [128, 16], F32)
            yb = sb("yb", [128, 16, 64], BF16)
            ya = [sb(f"ya{i}", [128, 1024], BF16) for i in range(2)]
            psT = ps("psT", [128, 8, 128], BF16)
            for bt in range(8):
                for p in range(3):
                    src = O_d[p][1024 * bt:1024 * bt + 1024, :].rearrange("(k i) c -> i k c", i=128)
                    K.op("sp", lambda e, p=p, src=src: e.dma_start(out=S3.t[:, p, :, :], in_=src), reads=[b_Od[p]], writes=[b_S3[p]], kind="d")
                acc = S3.t[:, 0, :, :].rearrange("p k c -> p (k c)")
                for p in (1, 2):
                    K.op("dve", lambda e, p=p, acc=acc: e.tensor_tensor(out=acc, in0=acc, in1=S3.t[:, p, :, :].rearrange("p k c -> p (k c)"), op=ALU.add),
                         reads=[b_S3[0], b_S3[p]], writes=[b_S3[0]])
                a16 = S3.t[:, 0, :, :].rearrange("p k (h c) -> p (k h) c", h=2)
                num = a16[:, :, 0:64]
                den = a16[:, :, 64]
                K.op("act", lambda e, num=num: e.activation(out=sqt.t[:], in_=num, func=AF.Square), reads=[b_S3[0]], writes=[sqt.b])
                K.op("dve", lambda e: e.tensor_reduce(out=ssq2.t[:], in_=sqt.t[:], axis=AX.X, op=ALU.add), reads=[sqt.b], writes=[ssq2.b])
                K.op("dve", lambda e, den=den: e.tensor_tensor(out=d2.t[:], in0=den, in1=den, op=ALU.mult), reads=[b_S3[0]], writes=[d2.b])
                K.op("dve", lambda e: e.tensor_scalar(out=d2.t[:], in0=d2.t[:], scalar1=1e-6, scalar2=None, op0=ALU.mult), reads=[d2.b], writes=[d2.b])
                K.op("dve", lambda e: e.scalar_tensor_tensor(out=rr.t[:], in0=ssq2.t[:], scalar=1.0 / 64, in1=d2.t[:], op0=ALU.mult, op1=ALU.add),
                     reads=[ssq2.b, d2.b], writes=[rr.b])
                K.op("act", lambda e: e.activation(out=rr.t[:], in_=rr.t[:], func=AF.Sqrt), reads=[rr.b], writes=[rr.b])
                K.op("dve", lambda e: e.reciprocal(out=rr.t[:], in_=rr.t[:]), reads=[rr.b], writes=[rr.b])
                rrb = bass.AP(rr.t, 0, [[16, 128], [1, 16], [0, 64]])
                K.op("dve", lambda e, num=num, rrb=rrb: e.tensor_tensor(out=yb.t[:], in0=num, in1=rrb, op=ALU.mult), reads=[b_S3[0], rr.b], writes=[yb.b])
                for k in range(8):
                    K.op("pe", lambda e, k=k: e.transpose(out=psT.t[:, k, :], in_=yb.t[:, 2 * k:2 * k + 2, :].rearrange("p h c -> p (h c)"), identity=ident),
                         reads=[yb.b, cbf.b], writes=[psT.b])
                yo = ya[bt % 2]
                K.op("act", lambda e, yo=yo: e.activation(out=yo.t[:], in_=psT.t[:].rearrange("p k c -> p (k c)"), func=AF.Copy, scale=pc(P_AG)),
                     reads=[psT.b, prm.b], writes=[yo.b])
                K.op("sp", lambda e, yo=yo, bt=bt: e.dma_start(out=gin_aq[bt // 2][:, 2 + 1024 * (bt % 2):2 + 1024 * (bt % 2) + 1024], in_=yo.t[:]), reads=[yo.b], writes=[b_gina[bt // 2]], kind="d")
                if bt % 2 == 1 and bt < 7:
                    K.op("sp", lambda e, yo=yo, bt=bt: e.dma_start(out=gin_aq[bt // 2 + 1][:, 0:2], in_=yo.t[:, 1022:1024]), reads=[yo.b], writes=[b_gina[bt // 2 + 1]], kind="d")
                if bt % 2 == 1 and P1 and P6:
                    K.op("pool", lambda e, q=bt // 2: e.collective_compute("AllGather", ALU.bypass, replica_groups=[[0, 1, 2, 3], [4, 5, 6, 7]], ins=[gin_aq[q]], outs=[gout_aq[q]]),
                         reads=[b_gina[bt // 2]], kind="cc")


        def phase6a(h2T, h2Th):
            RG = [[0, 1, 2, 3], [4, 5, 6, 7]]
            b_gr = [Buf() for _ in range(4)]
            b_ga = [Buf() for _ in range(4)]
            if P1 and P2:
                pass
            wo = sb("wo", [128, 8, D], BF16)
            wstg = [sb(f"wstg{i}", [128, 1024], F32) for i in range(2)]
            bw_o = [Buf() for _ in range(8)]
            for kc in range(8):
                stg = wstg[kc % 2]
                K.op("sp", lambda e, stg=stg, kc=kc: e.dma_start(out=stg.t[:], in_=w_out_d[128 * kc:128 * kc + 128, :]), writes=[stg.b], kind="d")
                K.op("pool" if kc % 2 else "dve", lambda e, stg=stg, kc=kc: e.tensor_copy(out=wo.t[:, kc, :], in_=stg.t[:]), reads=[stg.b], writes=[bw_o[kc]])
            Gq = [sb(f"Gq{i}", [128, 4, 512], BF16) for i in range(2)]
            yT = sb("yT", [128, 8, 512], BF16)
            xt6 = [sb(f"xt6_{i}", [128, D], F32) for i in range(2)]
            x1s = [sb(f"x1s{i}", [128, D], F32) for i in range(4)]
            junk6 = sb("junk6", [128, D], BF16)
            ssq6 = sb("ssq6", [128, 4], F32)
            r6 = sb("r6", [128, 4], F32)
            xn6 = [sb(f"xn6_{i}", [128, D], BF16) for i in range(2)]
            psA = ps("psA", [128, 1024], F32)
            tp6 = ps("tp6", [128, 8, 128], BF16)
            gr4 = [g_.rearrange("(c p) t -> p c t", p=128) for g_ in gout_rq]
            ga4 = [g_.rearrange("(c p) t -> p c t", p=128) for g_ in gout_aq]
            gi = [0]

            def block(kb, ntok, col_in_q, xrow0, hdst, hcol0):
                for q in range(4):
                    col = col_in_q
                    for part, (g4, bg) in enumerate(((gr4[q], b_gr[q]), (ga4[q], b_ga[q]))):
                        G = Gq[gi[0] % 2]
                        gi[0] += 1
                        K.op("sp", lambda e, G=G, col=col, g4=g4: e.dma_start(out=G.t[:, :, 0:ntok], in_=g4[:, :, col:col + ntok]), reads=[bg], writes=[G.b], kind="d")
                        ysl = yT.t[:, 4 * part:4 * part + 4, 0:ntok]
                        if q == 0:
                            K.op("dve", lambda e, G=G, ysl=ysl: e.tensor_scalar(out=ysl, in0=G.t[:, :, 0:ntok], scalar1=q6(FL), scalar2=None, op0=ALU.mult),
                                 reads=[G.b, p6.b], writes=[yT.b])
                        else:
                            K.op("dve", lambda e, G=G, q=q, ysl=ysl: e.scalar_tensor_tensor(out=ysl, in0=G.t[:, :, 0:ntok], scalar=q6(FL + q), in1=ysl,
                                                                                            op0=ALU.mult, op1=ALU.add), reads=[G.b, p6.b, yT.b], writes=[yT.b])
                ntt = (ntok + 127) // 128
                for tt in range(ntt):
                    tw_ = min(128, ntok - 128 * tt)
                    xt = xt6[tt % 2]
                    x1 = x1s[tt]
                    K.op("sp", lambda e, xt=xt, tt=tt, tw_=tw_: e.dma_start(out=xt.t[0:tw_, :], in_=xq_d[xrow0 + 128 * tt:xrow0 + 128 * tt + tw_, :]), writes=[xt.b], kind="d")
                    for half in range(2):
                        for kc in range(8):
                            K.op("pe", lambda e, half=half, kc=kc, tt=tt, tw_=tw_: e.matmul(out=psA.t[0:tw_, 512 * half:512 * half + 512], lhsT=yT.t[:, kc, 128 * tt:128 * tt + tw_],
                                                                                         rhs=wo.t[:, kc, 512 * half:512 * half + 512], start=(kc == 0), stop=(kc == 7)),
                                 reads=[yT.b, bw_o[kc]], writes=[psA.b])
                    K.op("dve", lambda e, xt=xt, x1=x1, tw_=tw_: e.tensor_tensor(out=x1.t[0:tw_, :], in0=psA.t[0:tw_, :], in1=xt.t[0:tw_, :], op=ALU.add),
                         reads=[psA.b, xt.b], writes=[x1.b])
                    if kb >= 0:
                        r0 = 512 * kb + 128 * tt
                        K.op("sp", lambda e, x1=x1, r0=r0: e.dma_start(out=x1_d[r0:r0 + 128, :], in_=x1.t[:]), reads=[x1.b], kind="d")
                    K.op("act", lambda e, x1=x1, tt=tt, tw_=tw_: e.activation(out=junk6.t[0:tw_, :], in_=x1.t[0:tw_, :], func=AF.Square, accum_out=ssq6.t[0:tw_, tt:tt + 1]),
                         reads=[x1.b], writes=[junk6.b, ssq6.b])
                pw = min(128, ntok)
                K.op("act", lambda e: e.activation(out=r6.t[0:pw, 0:ntt], in_=ssq6.t[0:pw, 0:ntt], func=AF.Sqrt, scale=1.0 / D, bias=q6(EP, pw)), reads=[ssq6.b, p6.b], writes=[r6.b])
                K.op("dve", lambda e: e.reciprocal(out=r6.t[0:pw, 0:ntt], in_=r6.t[0:pw, 0:ntt]), reads=[r6.b], writes=[r6.b])
                for tt in range(ntt):
                    tw_ = min(128, ntok - 128 * tt)
                    x1 = x1s[tt]
                    xn = xn6[tt % 2]
                    K.op("act", lambda e, x1=x1, xn=xn, tt=tt, tw_=tw_: e.activation(out=xn.t[0:tw_, :], in_=x1.t[0:tw_, :], func=AF.Copy, scale=r6.t[0:tw_, tt:tt + 1]),
                         reads=[x1.b, r6.b], writes=[xn.b])
                    for fc in range(8):
                        K.op("pe", lambda e, xn=xn, fc=fc, tw_=tw_: e.transpose(out=tp6.t[:, fc, 0:tw_], in_=xn.t[0:tw_, fc * 128:(fc + 1) * 128], identity=cbf.t[0:tw_, 0, 0:tw_]),
                             reads=[xn.b, cbf.b], writes=[tp6.b])
                    c0 = hcol0 + 128 * tt
                    K.op("act", lambda e, tw_=tw_, c0=c0: e.copy(out=hdst.t[:, :, c0:c0 + tw_], in_=tp6.t[:, :, 0:tw_]), reads=[tp6.b], writes=[hdst.b])

            block(-1, 2, 0, 0, h2Th, 0)
            for kb in range(4):
                block(kb, 512, 2 + 512 * kb, 2 + 512 * kb, h2T, 512 * kb)

        def phase6b(h2T, h2Th):
            print("phase6b start remaining", nc.sbuf_bytes_remaining)
            wd = sb("wd", [128, 22, D], BF16)
            wdst = [sb(f"wdst{i}", [128, 1024], F32) for i in range(1)]
            bw_d = [Buf() for _ in range(22)]
            wust = [sb(f"wust{i}", [128, 8, 256], F32) for i in range(1)]
            wub = [sb(f"wub{i}", [128, 8, 256], BF16) for i in range(2)]
            actT = sb("actT", [128, 22, 2048], BF16)
            gbs = [sb(f"gb{i}", [128, 514], F32) for i in range(2)]
            gbi = 0
            c1 = [sb(f"c1_{i}", [128, 512], F32) for i in range(3)]
            psg = [ps(f"psg{i}", [128, 512], F32) for i in range(2)]
            psv = [ps(f"psv{i}", [128, 512], F32) for i in range(2)]
            psA = ps("psA2", [128, 1024], F32)
            wup3 = w_up_d.rearrange("(kc p) c -> p kc c", p=128)
            pi = 0
            for m in range(22):
                ws = wust[0]
                wb = wub[m % 2]
                K.op("sp", lambda e, ws=ws, m=m: e.dma_start(out=ws.t[:, :, 0:128], in_=wup3[:, :, 128 * m:128 * m + 128]), writes=[ws.b], kind="d")
                K.op("sp", lambda e, ws=ws, m=m: e.dma_start(out=ws.t[:, :, 128:256], in_=wup3[:, :, 2816 + 128 * m:2816 + 128 * m + 128]), writes=[ws.b], kind="d")
                for kc in range(8):
                    if kc % 2 == 0:
                        K.op("dve", lambda e, ws=ws, wb=wb, kc=kc: e.tensor_scalar(out=wb.t[:, kc, :], in0=ws.t[:, kc, :], scalar1=q6(G2 + kc), scalar2=None, op0=ALU.mult),
                             reads=[ws.b, p6.b], writes=[wb.b])
                    else:
                        K.op("act", lambda e, ws=ws, wb=wb, kc=kc: e.activation(out=wb.t[:, kc, :], in_=ws.t[:, kc, :], func=AF.Copy, scale=q6(G2 + kc)),
                             reads=[ws.b, p6.b], writes=[wb.b])
                wst_ = wdst[0]
                K.op("sp", lambda e, wst_=wst_, m=m: e.dma_start(out=wst_.t[:], in_=w_dn_d[128 * m:128 * m + 128, :]), writes=[wst_.b], kind="d")
                K.op("act", lambda e, wst_=wst_, m=m: e.copy(out=wd.t[:, m, :], in_=wst_.t[:]), reads=[wst_.b], writes=[bw_d[m]])
                pg = psg[pi % 2]
                for kc in range(8):
                    K.op("pe", lambda e, pg=pg, wb=wb, kc=kc: e.matmul(out=pg.t[:, 0:2], lhsT=wb.t[:, kc, 0:128], rhs=h2Th.t[:, kc, :], start=(kc == 0), stop=(kc == 7)),
                         reads=[wb.b, h2Th.b], writes=[pg.b])
                gb = gbs[gbi % 2]
                K.op("act", lambda e, pg=pg, gb=gb: e.copy(out=gb.t[:, 0:2], in_=pg.t[:, 0:2]), reads=[pg.b], writes=[gb.b])
                pi += 1
                for kb in range(4):
                    pg = psg[pi % 2]
                    pvv = psv[pi % 2]
                    c_ = c1[pi % 3]
                    pi += 1
                    gb = gbs[gbi % 2]
                    gbn = gbs[(gbi + 1) % 2]
                    gbi += 1
                    hsl = slice(512 * kb, 512 * kb + 512)
                    for kc in range(8):
                        K.op("pe", lambda e, pg=pg, wb=wb, kc=kc, hsl=hsl: e.matmul(out=pg.t[:], lhsT=wb.t[:, kc, 0:128], rhs=h2T.t[:, kc, hsl], start=(kc == 0), stop=(kc == 7)),
                             reads=[wb.b, h2T.b], writes=[pg.b])
                    for kc in range(8):
                        K.op("pe", lambda e, pvv=pvv, wb=wb, kc=kc, hsl=hsl: e.matmul(out=pvv.t[:], lhsT=wb.t[:, kc, 128:256], rhs=h2T.t[:, kc, hsl], start=(kc == 0), stop=(kc == 7)),
                             reads=[wb.b, h2T.b], writes=[pvv.b])
                    K.op("act", lambda e, pg=pg, gb=gb: e.copy(out=gb.t[:, 2:514], in_=pg.t[:]), reads=[pg.b], writes=[gb.b])
                    K.op("act", lambda e, c_=c_, m=m, pg=pg: e.activation(out=c_.t[:], in_=pg.t[:], func=AF.Identity, scale=q6(CW + 3 * m + 2), bias=q6(CB + m)),
                         reads=[pg.b, p6.b], writes=[c_.b])
                    K.op("dve", lambda e, c_=c_, m=m, gb=gb: e.scalar_tensor_tensor(out=c_.t[:], in0=gb.t[:, 1:513], scalar=q6(CW + 3 * m + 1), in1=c_.t[:], op0=ALU.mult, op1=ALU.add),
                         reads=[gb.b, p6.b, c_.b], writes=[c_.b])
                    K.op("dve", lambda e, c_=c_, m=m, gb=gb: e.scalar_tensor_tensor(out=c_.t[:], in0=gb.t[:, 0:512], scalar=q6(CW + 3 * m + 0), in1=c_.t[:], op0=ALU.mult, op1=ALU.add),
                         reads=[gb.b, p6.b, c_.b], writes=[c_.b])
                    if kb < 3:
                        K.op("pool", lambda e, gb=gb, gbn=gbn: e.tensor_copy(out=gbn.t[:, 0:2], in_=gb.t[:, 512:514]), reads=[gb.b], writes=[gbn.b])
                    K.op("act", lambda e, c_=c_: e.activation(out=c_.t[:], in_=c_.t[:], func=AF.Silu), reads=[c_.b], writes=[c_.b])
                    K.op("dve", lambda e, c_=c_, pvv=pvv, m=m, hsl=hsl: e.tensor_tensor(out=actT.t[:, m, hsl], in0=pvv.t[:], in1=c_.t[:], op=ALU.mult), reads=[c_.b, pvv.b], writes=[actT.b])
            class VW:
                def __init__(self, ap, b):
                    self.t = ap
                    self.b = b
            fg = wdst[0]
            K.op("sp", lambda e: e.dma_start(out=fg.t[:], in_=fg_d), writes=[fg.b], kind="d")
            xr = [VW(wust[0].t[:].rearrange("p a b -> p (a b)")[:, 1024 * i:1024 * i + 1024], wust[0].b if i == 0 else Buf()) for i in range(2)]
            x2 = xr
            junk7 = VW(wub[0].t[:].rearrange("p a b -> p (a b)")[:, 0:1024], wub[0].b)
            ssq7 = sb("ssq7", [128, 16], F32)
            r7 = sb("r7", [128, 16], F32)
            for t16 in range(16):
                xr_ = xr[t16 % 2]
                x2_ = x2[t16 % 2]
                K.op("sp", lambda e, xr_=xr_, t16=t16: e.dma_start(out=xr_.t[:], in_=x1_d[128 * t16:128 * t16 + 128, :]), writes=[xr_.b], kind="d")
                for half in range(2):
                    for m in range(22):
                        K.op("pe", lambda e, half=half, m=m, t16=t16: e.matmul(out=psA.t[:, 512 * half:512 * half + 512], lhsT=actT.t[:, m, 128 * t16:128 * t16 + 128],
                                                                             rhs=wd.t[:, m, 512 * half:512 * half + 512], start=(m == 0), stop=(m == 21)),
                             reads=[actT.b, bw_d[m]], writes=[psA.b])
                K.op("dve", lambda e, x2_=x2_, xr_=xr_: e.tensor_tensor(out=x2_.t[:], in0=psA.t[:], in1=xr_.t[:], op=ALU.add), reads=[psA.b, xr_.b], writes=[x2_.b])
                K.op("act", lambda e, x2_=x2_, t16=t16: e.activation(out=junk7.t[:], in_=x2_.t[:], func=AF.Square, accum_out=ssq7.t[:, t16:t16 + 1]),
                     reads=[x2_.b], writes=[junk7.b, ssq7.b])
                K.op("act", lambda e, t16=t16: e.activation(out=r7.t[:, t16:t16 + 1], in_=ssq7.t[:, t16:t16 + 1], func=AF.Sqrt, scale=1.0 / D, bias=q6(EP)), reads=[ssq7.b, p6.b], writes=[r7.b])
                K.op("dve", lambda e, t16=t16: e.reciprocal(out=r7.t[:, t16:t16 + 1], in_=r7.t[:, t16:t16 + 1]), reads=[r7.b], writes=[r7.b])
                K.op("dve", lambda e, x2_=x2_, t16=t16: e.scalar_tensor_tensor(out=x2_.t[:], in0=x2_.t[:], scalar=r7.t[:, t16:t16 + 1], in1=fg.t[:], op0=ALU.mult, op1=ALU.mult),
                     reads=[x2_.b, r7.b, fg.b], writes=[x2_.b])
                K.op("sp", lambda e, x2_=x2_, t16=t16: e.dma_start(out=out_d[128 * t16:128 * t16 + 128, :], in_=x2_.t[:]), reads=[x2_.b], kind="d")


        if P1:
            ph = ExitStack()
            cur[0] = ph
            with ph:
                phase1()
                print("phase1 sbuf bytes remaining", nc.sbuf_bytes_remaining)
                K.flush(include_cc=False)
            cur[0] = st
        if P2:
            ph = ExitStack()
            cur[0] = ph
            with ph:
                phase2()
                print("phase2 sbuf bytes remaining", nc.sbuf_bytes_remaining)
                K.flush()
            cur[0] = st
        if P6:
            G2, FL, CB, CW, EP = 0, 8, 12, 34, 100
            ph0 = ExitStack()
            cur[0] = ph0
            with ph0:
                p6 = sb("p6", [128, 128], F32)
                K.op("sp", lambda e: e.dma_start(out=p6.t[:], in_=p6_d), writes=[p6.b], kind="d")

                def q6(col, n=128):
                    return p6.t[0:n, col:col + 1]
                h2T = sb("h2T", [128, 8, 2048], BF16)
                h2Th = sb("h2Th", [128, 8, 2], BF16)
                ph = ExitStack()
                cur[0] = ph
                with ph:
                    phase6a(h2T, h2Th)
                    print("phase6a sbuf bytes remaining", nc.sbuf_bytes_remaining)
                    K.flush()
                ph = ExitStack()
                cur[0] = ph
                with ph:
                    phase6b(h2T, h2Th)
                    print("phase6b sbuf bytes remaining", nc.sbuf_bytes_remaining)
                    K.flush()
            cur[0] = st
        K.final_wait()
    return nc


def _core_inputs_p1(inp, c):
    b, g = c // 4, c % 4
    l = 0
    w = inp['w_in'][l]
    sm = inp['rwkv_shift_mix'][l]
    A0 = 1696
    r128 = np.arange(128*g, 128*g+128)
    cols = np.concatenate([r128, 512+r128, 1024+r128, np.arange(1536,1600), np.arange(1600,1696), A0+r128, A0+512+r128, A0+1024+r128])
    w_c = np.ascontiguousarray(w[:, cols])
    prm = np.zeros((128, 32), np.float32)
    prm[:, 0:8] = inp['mix_norm_gain'][l].reshape(8,128).T
    MO = [0,128,256,384,448]; MW=[128,128,128,64,96]
    for m in range(5):
        prm[:MW[m], 8+m] = sm[cols[MO[m]:MO[m]+MW[m]]]
    ch = slice(128*g, 128*g+128)
    prm[:, 13] = inp['w0'][l][ch]; prm[:, 14] = inp['a0'][l][ch]; prm[:,15] = inp['k_k'][l][ch]; prm[:,16]=inp['k_a'][l][ch]
    prm[:, 17] = inp['r_k'][l].reshape(-1)[ch]; prm[:,18]=inp['ln_x_w'][l][ch]; prm[:,19]=inp['ln_x_b'][l][ch]
    prm[:, 20] = inp['attn_norm_gain'][l][ch]
    invf = (500000.0 ** (-np.arange(8, dtype=np.float32) * 2.0 / 16)).astype(np.float32)
    for h in range(2):
        for cc in range(16):
            prm[64*h+cc, 22] = invf[cc % 8] / np.float32(2*np.pi)
    prm[:, 24] = 1e-6; prm[:, 25] = 1e-24; prm[:, 26] = 64e-5
    cbf = np.zeros((128,3,128), np.float32)
    cbf[:,0,:] = np.eye(128); cbf[:,2,:] = np.eye(128)
    for h in range(2):
        for cc in range(8):
            cbf[64*h+cc+8, 1, 64*h+cc] = -1.0
            cbf[64*h+cc, 1, 64*h+cc+8] = 1.0
    cf = np.zeros((128, 1408), np.float32)
    cf[:, 0:512] = np.arange(512, dtype=np.float32)[None, :]
    bo = np.zeros((128,128), np.float32); bo[:64,:64] = 1; bo[64:,64:] = 1
    cf[:, 512:640] = bo
    ii = np.arange(128)
    SU = (ii[:,None] < ii[None,:]).astype(np.float32); SL = (ii[:,None] > ii[None,:]).astype(np.float32); UI = (ii[:,None] <= ii[None,:]).astype(np.float32)
    cf[:, 640:1152] = np.concatenate([SU, SL, SU, SL], axis=1)
    cf[:, 1152:1408] = np.concatenate([UI, SU], axis=1)
    lora = np.zeros((128,3,128), np.float32)
    lora[0:32, 0, :] = inp['w_lora_up'][l][:, ch]
    lora[32:64, 1, :] = inp['a_lora_up'][l][:, ch]
    lora[0:96, 2, :] = inp['g_lora_up'][l][:, ch]
    return dict(xb=np.ascontiguousarray(inp['x'][b]), w_in=w_c, prm=prm, cbf=cbf.astype(ml_dtypes.bfloat16), cf=cf, lora=lora), cols


def _core_inputs(inp, c):
    m, cols = _core_inputs_p1(inp, c)
    b, g = c // 4, c % 4
    ii = np.arange(128)
    UI = (ii[:,None] <= ii[None,:]).astype(np.float32); LI = (ii[:,None] >= ii[None,:]).astype(np.float32)
    m['amask'] = np.concatenate([UI, UI, LI, LI], axis=1).astype(ml_dtypes.bfloat16)
    l = 0
    x = inp['x'][b]
    xq = np.zeros((2050, 1024), np.float32)
    lo = 2048*g - 2
    if g == 0:
        xq[2:] = x[0:2048]
    else:
        xq[:] = x[lo:lo+2050]
    m['xq'] = xq
    m['w_out'] = np.ascontiguousarray(inp['w_out'][l])
    m['w_up'] = np.ascontiguousarray(inp['w_ffn_up'][l])
    m['w_dn'] = np.ascontiguousarray(inp['w_ffn_down'][l])
    p6 = np.zeros((128,128), np.float32)
    p6[:, 0:8] = inp['ffn_norm_gain'][l].reshape(8,128).T
    p6[:, 8+g] = 1.0
    p6[:, 12:34] = inp['ffn_conv_b'][l].reshape(22,128).T
    cw = inp['ffn_conv_w'][l]
    for k in range(3):
        p6[:, 34+k:34+66:3] = cw[k].reshape(22,128).T
    p6[:, 100] = 1e-6
    m['p6'] = p6
    m['fgain'] = np.ascontiguousarray(np.broadcast_to(inp['final_norm_gain'][None,:], (128,1024))).astype(np.float32)
    return m, cols


def kernel(**inputs):
    inp = {k: np.asarray(v) for k, v in inputs.items()}
    nc = build(phases=(1, 2, 6))
    maps = [_core_inputs(inp, c)[0] for c in range(8)]
    res = run_bass_kernel_spmd(nc, maps, core_ids=list(range(8)))
    out = np.zeros((2, S, D), np.float32)
    for c in range(8):
        b, g = c // 4, c % 4
        out[b, 2048 * g:2048 * g + 2048] = np.asarray(res.results[c]["out"], dtype=np.float32)
    return out
```
